# Optimizing a Trainium2 kernel written in Bass

```python
import math
import jax, jax.numpy as jnp
from jax import lax
import numpy as np

D_MODEL = 1024
BATCH = 16
SEQ = 4096
DEPTH = 2

EPS = 1e-6
SSD_HEADS = 16
SSD_HEAD_DIM = 64
SSD_WIDTH = SSD_HEADS * SSD_HEAD_DIM
SSD_GROUPS = 2
SSD_STATE = 128
SSD_CONV = 4
SSD_CHUNK = 128
SSD_CONV_CH = SSD_WIDTH + 2 * SSD_GROUPS * SSD_STATE
DT_MIN = 1e-3
DT_MAX = 1e-1
POOL_GROUPS = 4
POOL_GROUP_DIM = 128
POOL_WIDTH = POOL_GROUPS * POOL_GROUP_DIM
POOL_WINDOWS = (2, 4, 8, 16)
MLA_HEADS = 8
MLA_Q_RANK = 384
MLA_KV_RANK = 256
MLA_NOPE = 64
MLA_ROPE = 32
MLA_V = 64
MLA_QK = MLA_NOPE + MLA_ROPE
MLA_WIDTH = MLA_HEADS * MLA_V
ROPE_THETA = 10000.0
Q_BLOCK = 128
MIX_WIDTH = SSD_WIDTH + POOL_WIDTH + MLA_WIDTH
IN_SIZES = (SSD_WIDTH, SSD_CONV_CH, SSD_HEADS, POOL_WIDTH, MLA_Q_RANK, MLA_KV_RANK, MLA_ROPE)
IN_COLS = SSD_WIDTH + SSD_CONV_CH + SSD_HEADS + POOL_WIDTH + MLA_Q_RANK + MLA_KV_RANK + MLA_ROPE
D_FF = 2816
FFN_CONV = 3

kernel_name = "hybrid_ssd_pool_mla_convffn"


def rmsnorm(x, w):
    xf = x.astype(jnp.float32)
    var = jnp.mean(xf * xf, axis=-1, keepdims=True)
    return (xf * lax.rsqrt(var + EPS)).astype(x.dtype) * w


def causal_dwconv(x, w, b):
    K = w.shape[0]
    S = x.shape[1]
    xp = jnp.pad(x, ((0, 0), (K - 1, 0), (0, 0)))
    acc = xp[:, 0:S] * w[0] + b
    for k in range(1, K):
        acc = acc + xp[:, k:k + S] * w[k]
    return acc


def rope(x, cos, sin):
    x1, x2 = jnp.split(x, 2, axis=-1)
    return jnp.concatenate([x1 * cos - x2 * sin, x1 * sin + x2 * cos], axis=-1)


def rope_tables(positions):
    inv_freq = ROPE_THETA ** (-jnp.arange(0, MLA_ROPE, 2, dtype=jnp.float32) / MLA_ROPE)
    ang = positions.astype(jnp.float32)[..., None] * inv_freq
    return jnp.cos(ang), jnp.sin(ang)


def ssd_mixer(z, xbc, dt_raw, conv_w, conv_b, dt_bias, a_log, d_skip, norm_w):
    Bsz, S, _ = xbc.shape
    G, E, P, N, L = SSD_GROUPS, SSD_HEADS // SSD_GROUPS, SSD_HEAD_DIM, SSD_STATE, SSD_CHUNK
    nc = S // L
    xbc = jax.nn.silu(causal_dwconv(xbc, conv_w, conv_b))
    xs, bs, cs = jnp.split(xbc, [SSD_WIDTH, SSD_WIDTH + G * N], axis=-1)
    xs = xs.reshape(Bsz, nc, L, G, E, P)
    bs = bs.reshape(Bsz, nc, L, G, N)
    cs = cs.reshape(Bsz, nc, L, G, N)
    dt = jax.nn.softplus((dt_raw + dt_bias).astype(jnp.float32)).reshape(Bsz, nc, L, G, E)
    a = -jnp.exp(a_log.astype(jnp.float32)).reshape(G, E)
    da = dt * a
    xdt = xs * dt[..., None]
    da_cum = jnp.cumsum(da, axis=2)
    causal = jnp.tril(jnp.ones((L, L), dtype=bool))
    seg = da_cum[:, :, :, None] - da_cum[:, :, None, :]
    decay = jnp.exp(jnp.where(causal[None, None, :, :, None, None], seg, -jnp.inf))
    cb = jnp.einsum("bclgn,bcsgn->bclsg", cs, bs)
    y_diag = jnp.einsum("bclsg,bclsge,bcsgep->bclgep", cb, decay, xdt)
    decay_to_end = jnp.exp(da_cum[:, :, -1:] - da_cum)
    chunk_states = jnp.einsum("bclgn,bclge,bclgep->bcgepn", bs, decay_to_end, xdt)
    chunk_decay = jnp.exp(da_cum[:, :, -1])

    def step(h, inp):
        dec, st = inp
        return h * dec[..., None, None] + st, h

    h0 = jnp.zeros((Bsz, G, E, P, N), dtype=chunk_states.dtype)
    _, h_in = lax.scan(step, h0, (jnp.moveaxis(chunk_decay, 1, 0), jnp.moveaxis(chunk_states, 1, 0)))
    h_in = jnp.moveaxis(h_in, 0, 1)
    y_off = jnp.einsum("bclgn,bcgepn,bclge->bclgep", cs, h_in, jnp.exp(da_cum))
    y = y_diag + y_off + xs * d_skip.reshape(G, E)[:, :, None]
    y = y.reshape(Bsz, S, SSD_WIDTH)
    return rmsnorm(y * jax.nn.silu(z), norm_w)


def pool_mixer(u, pool_w, pool_scale):
    Bsz, S, _ = u.shape
    uf = u.astype(jnp.float32)
    csum = jnp.pad(jnp.cumsum(uf, axis=1), ((0, 0), (1, 0), (0, 0)))
    count = jnp.arange(1, S + 1, dtype=jnp.float32)[:, None]
    means = []
    for gi, w in enumerate(POOL_WINDOWS):
        c = csum[:, :, gi * POOL_GROUP_DIM:(gi + 1) * POOL_GROUP_DIM]
        lag = jnp.pad(c, ((0, 0), (w - 1, 0), (0, 0)))[:, :S]
        means.append((c[:, 1:] - lag) / jnp.minimum(count, float(w)))
    pooled = (jnp.concatenate(means, axis=-1) - uf).astype(u.dtype)
    pooled = pooled.reshape(Bsz, S, POOL_GROUPS, POOL_GROUP_DIM)
    y = jnp.einsum("bsgc,gcd->bsgd", pooled, pool_w).reshape(Bsz, S, POOL_WIDTH)
    return y * pool_scale


def mla_mixer(c_q, c_kv, k_pe, cos, sin, q_norm, w_uq, kv_norm, w_ukv):
    Bsz, S, _ = c_q.shape
    H = MLA_HEADS
    q = (rmsnorm(c_q, q_norm) @ w_uq).reshape(Bsz, S, H, MLA_QK)
    kv = (rmsnorm(c_kv, kv_norm) @ w_ukv).reshape(Bsz, S, H, MLA_NOPE + MLA_V)
    q_nope, q_pe = jnp.split(q, [MLA_NOPE], axis=-1)
    k_nope, v = jnp.split(kv, [MLA_NOPE], axis=-1)
    q_pe = rope(q_pe, cos[:, :, None, :], sin[:, :, None, :])
    k_pe = rope(k_pe, cos, sin)
    q = jnp.concatenate([q_nope, q_pe], axis=-1)
    k = jnp.concatenate([k_nope, jnp.broadcast_to(k_pe[:, :, None, :], (Bsz, S, H, MLA_ROPE))], axis=-1)
    scale = 1.0 / math.sqrt(MLA_QK)
    nb = S // Q_BLOCK
    qb = jnp.moveaxis(q.reshape(Bsz, nb, Q_BLOCK, H, MLA_QK), 1, 0)
    key_pos = jnp.arange(S)

    def attend(args):
        q_blk, i = args
        s = jnp.einsum("bqhd,bkhd->bhqk", q_blk, k).astype(jnp.float32) * scale
        q_pos = i * Q_BLOCK + jnp.arange(Q_BLOCK)
        s = jnp.where(key_pos[None, :] <= q_pos[:, None], s, -jnp.inf)
        p = jax.nn.softmax(s, axis=-1).astype(v.dtype)
        return jnp.einsum("bhqk,bkhd->bqhd", p, v)

    o = lax.map(attend, (qb, jnp.arange(nb)))
    return jnp.moveaxis(o, 0, 1).reshape(Bsz, S, MLA_WIDTH)


def conv_ffn(h, w_up, conv_w, conv_b, w_down):
    up = causal_dwconv(h @ w_up, conv_w, conv_b)
    gate, val = jnp.split(up, 2, axis=-1)
    return (jax.nn.silu(gate) * val) @ w_down


def setup_inputs(seed: int = 0) -> dict:
    key = jax.random.key(seed)
    ks = jax.random.split(key, 24)
    f32 = jnp.float32

    def nrm(k, shape, scale):
        return jax.random.normal(k, shape, f32) * scale

    def gain(k, shape):
        return 1.0 + 0.02 * jax.random.normal(k, shape, f32)

    x = jax.random.normal(ks[0], (BATCH, SEQ, D_MODEL), f32)
    offsets = jax.random.randint(ks[1], (BATCH, 1), 0, 1024, dtype=jnp.int32)
    positions = (offsets + jnp.arange(SEQ, dtype=jnp.int32)[None, :]).astype(jnp.int32)
    u_dt = jax.random.uniform(ks[2], (DEPTH, SSD_HEADS), f32)
    dt0 = jnp.exp(u_dt * (math.log(DT_MAX) - math.log(DT_MIN)) + math.log(DT_MIN))
    dt_bias = dt0 + jnp.log(-jnp.expm1(-dt0))
    a_log = jnp.log(jax.random.uniform(ks[3], (DEPTH, SSD_HEADS), f32, 1.0, 16.0))
    return {
        "x": x,
        "positions": positions,
        "attn_norm": gain(ks[4], (DEPTH, D_MODEL)),
        "w_in": nrm(ks[5], (DEPTH, D_MODEL, IN_COLS), D_MODEL ** -0.5),
        "ssd_conv_w": nrm(ks[6], (DEPTH, SSD_CONV, SSD_CONV_CH), SSD_CONV ** -0.5),
        "ssd_conv_b": nrm(ks[7], (DEPTH, SSD_CONV_CH), 0.02),
        "ssd_dt_bias": dt_bias,
        "ssd_a_log": a_log,
        "ssd_d": 1.0 + 0.1 * jax.random.normal(ks[8], (DEPTH, SSD_HEADS), f32),
        "ssd_norm": gain(ks[9], (DEPTH, SSD_WIDTH)),
        "pool_w": nrm(ks[10], (DEPTH, POOL_GROUPS, POOL_GROUP_DIM, POOL_GROUP_DIM), POOL_GROUP_DIM ** -0.5),
        "pool_scale": gain(ks[11], (DEPTH, POOL_WIDTH)),
        "mla_q_norm": gain(ks[12], (DEPTH, MLA_Q_RANK)),
        "mla_w_uq": nrm(ks[13], (DEPTH, MLA_Q_RANK, MLA_HEADS * MLA_QK), MLA_Q_RANK ** -0.5),
        "mla_kv_norm": gain(ks[14], (DEPTH, MLA_KV_RANK)),
        "mla_w_ukv": nrm(ks[15], (DEPTH, MLA_KV_RANK, MLA_HEADS * (MLA_NOPE + MLA_V)), MLA_KV_RANK ** -0.5),
        "w_out": nrm(ks[16], (DEPTH, MIX_WIDTH, D_MODEL), MIX_WIDTH ** -0.5),
        "ffn_norm": gain(ks[17], (DEPTH, D_MODEL)),
        "ffn_w_up": nrm(ks[18], (DEPTH, D_MODEL, 2 * D_FF), D_MODEL ** -0.5),
        "ffn_conv_w": nrm(ks[19], (DEPTH, FFN_CONV, 2 * D_FF), FFN_CONV ** -0.5),
        "ffn_conv_b": nrm(ks[20], (DEPTH, 2 * D_FF), 0.02),
        "ffn_w_down": nrm(ks[21], (DEPTH, D_FF, D_MODEL), D_FF ** -0.5),
        "final_norm": gain(ks[22], (D_MODEL,)),
    }


def reference(x, positions, attn_norm, w_in, ssd_conv_w, ssd_conv_b, ssd_dt_bias, ssd_a_log,
              ssd_d, ssd_norm, pool_w, pool_scale, mla_q_norm, mla_w_uq, mla_kv_norm, mla_w_ukv,
              w_out, ffn_norm, ffn_w_up, ffn_conv_w, ffn_conv_b, ffn_w_down, final_norm):
    cos, sin = rope_tables(positions)
    splits = [int(s) for s in np.cumsum(IN_SIZES)[:-1]]
    for l in range(DEPTH):
        h = rmsnorm(x, attn_norm[l])
        proj = h @ w_in[l]
        z, xbc, dt_raw, u, c_q, c_kv, k_pe = jnp.split(proj, splits, axis=-1)
        y_ssd = ssd_mixer(z, xbc, dt_raw, ssd_conv_w[l], ssd_conv_b[l], ssd_dt_bias[l],
                          ssd_a_log[l], ssd_d[l], ssd_norm[l])
        y_pool = pool_mixer(u, pool_w[l], pool_scale[l])
        y_mla = mla_mixer(c_q, c_kv, k_pe, cos, sin, mla_q_norm[l], mla_w_uq[l],
                          mla_kv_norm[l], mla_w_ukv[l])
        x = x + jnp.concatenate([y_ssd, y_pool, y_mla], axis=-1) @ w_out[l]
        h = rmsnorm(x, ffn_norm[l])
        x = x + conv_ffn(h, ffn_w_up[l], ffn_conv_w[l], ffn_conv_b[l], ffn_w_down[l])
    return rmsnorm(x, final_norm)
```

```python
import contextlib
import numpy as np
import concourse.bass as bass
import concourse.mybir as mybir
from concourse.bass_utils import run_bass_kernel_spmd

F32 = mybir.dt.float32; BF16 = mybir.dt.bfloat16; I32 = mybir.dt.int32
AF = mybir.ActivationFunctionType; ALU = mybir.AluOpType

NCORE = 8; D = 1024; S = 4096; NSEQ = 2; NT = NSEQ * S; TT = 512; NTILE = NT // TT
DEPTH = 2; DFF = 2816; EPS = 1e-6
NCH_IN = 32
CH_Z, CH_X, CH_B, CH_C, CH_DT, CH_U, CH_CQ, CH_CKV, CH_KPE, CH_KPP = 0, 8, 16, 18, 20, 21, 25, 28, 30, 31
TWO_PI = float(2 * np.pi)


class Buf:
    __slots__ = ("name", "w", "r")

    def __init__(self, name):
        self.name = name; self.w = None; self.r = []


class Trk:
    ENG = ("pe", "act", "dve", "pool", "sp")

    def __init__(self, nc, stack):
        self.nc = nc; self.stack = stack
        self.eng = {"pe": nc.tensor, "act": nc.scalar, "dve": nc.vector, "pool": nc.gpsimd, "sp": nc.sync}
        self.sem = {k: stack.enter_context(nc.semaphore("prog_" + k)) for k in self.ENG}
        self.cnt = {k: 0 for k in self.ENG}
        self.waited = {}
        self.dmasems = {}; self.dmacnt = {}
        self.nins = 0

    def _wait(self, e, tok):
        if tok is None:
            return
        sem, val, key, src = tok
        if src == e and (e == "pe" or val > self.cnt[e]):
            return
        k = (e, key)
        if self.waited.get(k, 0) >= val:
            return
        self.waited[k] = val
        self.eng[e].wait_ge(sem, val)

    def _deps(self, e, reads, writes):
        for b in reads:
            self._wait(e, b.w)
        for b in writes:
            self._wait(e, b.w)
            for t in b.r:
                self._wait(e, t)

    def _mark(self, tok, reads, writes):
        for b in reads:
            b.r.append(tok)
            if len(b.r) > 8:
                d = {}
                for t in b.r:
                    if t[2] not in d or d[t[2]][1] < t[1]:
                        d[t[2]] = t
                b.r = list(d.values())
        for b in writes:
            b.w = tok; b.r = []

    def op(self, e, fn, reads=(), writes=(), signal=True):
        self._deps(e, reads, writes)
        ins = fn(self.eng[e])
        self.nins += 1
        if signal:
            self.cnt[e] += 1
            ins.then_inc(self.sem[e], 1)
            tok = (self.sem[e], self.cnt[e], "prog_" + e, e)
        else:
            tok = (self.sem[e], self.cnt[e] + 1, "prog_" + e, e)
        self._mark(tok, reads, writes)
        return tok

    def dma(self, q, chan, pairs, reads=(), writes=()):
        self._deps(q, reads, writes)
        if chan not in self.dmasems:
            self.dmasems[chan] = self.stack.enter_context(self.nc.semaphore("dma_" + chan))
            self.dmacnt[chan] = 0
        for (o, i) in pairs:
            self.dmacnt[chan] += 16
            self.eng[q].dma_start(out=o, in_=i).then_inc(self.dmasems[chan], 16)
            self.nins += 1
        tok = (self.dmasems[chan], self.dmacnt[chan], "dma_" + chan, "dma")
        self._mark(tok, reads, writes)
        return tok

    def barrier(self):
        toks = []
        for k in self.ENG:
            if self.cnt[k] > 0:
                toks.append((self.sem[k], self.cnt[k], "prog_" + k, k))
        for c, s in self.dmasems.items():
            toks.append((s, self.dmacnt[c], "dma_" + c, "dma"))
        for e in self.ENG:
            for t in toks:
                if t[3] == e:
                    continue
                self._wait(e, t)


class VMap:
    def __init__(self):
        self.off = {}; self.n = 0

    def add(self, name, n):
        self.off[name] = self.n; self.n += n

    def __call__(self, name, i=0):
        return self.off[name] + i


def _vmap():
    V = VMap()
    for l in range(DEPTH):
        V.add(f"attn_norm{l}", 8); V.add(f"ssd_cw{l}", 48); V.add(f"ssd_cb{l}", 12)
        V.add(f"ssd_norm{l}", 8); V.add(f"pool_scale{l}", 4); V.add(f"q_norm{l}", 3)
        V.add(f"kv_norm{l}", 2); V.add(f"ffn_norm{l}", 8); V.add(f"ffn_cw{l}", 132)
        V.add(f"ffn_cb{l}", 44); V.add(f"dskip{l}", 8); V.add(f"dt_bias{l}", 1); V.add(f"a_log{l}", 16)
    V.add("final_norm", 8); V.add("pool_rc", 64); V.add("invf", 1); V.add("sgn", 1)
    return V


VM = _vmap()
NV = VM.n


def _colmajor(v):
    v = np.asarray(v, np.float32)
    return np.ascontiguousarray(v.reshape(-1, 128).T)


def _build_vecs(inp):
    vecs = np.zeros((128, NV), np.float32)

    def put(name, arr):
        arr = np.asarray(arr, np.float32)
        vecs[:arr.shape[0], VM(name):VM(name) + arr.shape[1]] = arr
    for l in range(DEPTH):
        put(f"attn_norm{l}", _colmajor(inp["attn_norm"][l]))
        cw = inp["ssd_conv_w"][l]
        put(f"ssd_cw{l}", np.concatenate([_colmajor(cw[k]) for k in range(4)], axis=1))
        put(f"ssd_cb{l}", _colmajor(inp["ssd_conv_b"][l]))
        put(f"ssd_norm{l}", _colmajor(inp["ssd_norm"][l]))
        put(f"pool_scale{l}", _colmajor(inp["pool_scale"][l]))
        put(f"q_norm{l}", _colmajor(inp["mla_q_norm"][l]))
        put(f"kv_norm{l}", _colmajor(inp["mla_kv_norm"][l]))
        put(f"ffn_norm{l}", _colmajor(inp["ffn_norm"][l]))
        fw = inp["ffn_conv_w"][l]
        put(f"ffn_cw{l}", np.concatenate([_colmajor(fw[k]) for k in range(3)], axis=1))
        put(f"ffn_cb{l}", _colmajor(inp["ffn_conv_b"][l]))
        put(f"dskip{l}", _colmajor(np.repeat(np.asarray(inp["ssd_d"][l], np.float32), 64)))
        put(f"dt_bias{l}", np.asarray(inp["ssd_dt_bias"][l], np.float32).reshape(16, 1))
        put(f"a_log{l}", np.broadcast_to(np.asarray(inp["ssd_a_log"][l], np.float32)[None, :], (128, 16)))
    put("final_norm", _colmajor(inp["final_norm"]))
    rc = np.zeros((128, 64), np.float32)
    for g, w in enumerate((2, 4, 8, 16)):
        rc[:, g * 16:(g + 1) * 16] = 1.0 / np.minimum(np.arange(1, 17), w)[None, :]
    put("pool_rc", rc)
    invf = np.zeros((128, 1), np.float32); sgn = np.zeros((128, 1), np.float32)
    fr = (10000.0 ** (-np.arange(0, 32, 2, dtype=np.float32) / 32)).astype(np.float32)
    invf[64:80, 0] = fr; invf[80:96, 0] = fr
    sgn[64:80, 0] = -1.0; sgn[80:96, 0] = 1.0
    put("invf", invf); put("sgn", sgn)
    return vecs


def _layout_w_in(w):
    w = np.asarray(w, np.float32)
    o = np.zeros((1024, NCH_IN * 128), np.float32)
    z, xbc, dt, u, cq, ckv, kpe = np.split(w, np.cumsum([1024, 1536, 16, 512, 384, 256])[:], axis=1)
    o[:, 0:1024] = z
    o[:, 1024:2560] = xbc
    o[:, CH_DT * 128:CH_DT * 128 + 16] = dt
    o[:, CH_U * 128:CH_U * 128 + 512] = u
    o[:, CH_CQ * 128:CH_CQ * 128 + 384] = cq
    o[:, CH_CKV * 128:CH_CKV * 128 + 256] = ckv
    o[:, CH_KPE * 128 + 64:CH_KPE * 128 + 96] = kpe
    o[:, CH_KPP * 128 + 64:CH_KPP * 128 + 96] = np.concatenate([kpe[:, 16:32], kpe[:, 0:16]], axis=1)
    return o


def build_program(dbg=None, phases=None, nlayers=DEPTH):
    dbg = dbg or ()
    nc = bass.Bass("TRN2", target_bir_lowering=False)
    es = contextlib.ExitStack()
    T = Trk(nc, es)

    def din(name, shape, dt=F32):
        return nc.dram_tensor(name, list(shape), dt, kind="ExternalInput").ap()

    def dscr(name, shape, dt):
        kind = "ExternalOutput" if name in dbg else "Internal"
        return nc.dram_tensor(name, list(shape), dt, kind=kind).ap()

    xT = din("xT", [D, NT]); pos = din("pos", [1, NT], I32); vecs_d = din("vecs", [128, NV])
    w_in_d = din("w_in", [DEPTH, 1024, NCH_IN * 128])
    pool_w_d = din("pool_w", [DEPTH, 4, 128, 128])
    w_uq_d = din("w_uq", [DEPTH, 384, 8, 96]); w_uqp_d = din("w_uqp", [DEPTH, 384, 8, 96])
    w_kn_d = din("w_kn", [DEPTH, 256, 512]); w_v_d = din("w_v", [DEPTH, 256, 512])
    w_out_d = din("w_out", [DEPTH, 2048, 1024])
    w_up_d = din("w_up", [DEPTH, 1024, 2 * DFF]); w_dn_d = din("w_dn", [DEPTH, DFF, 1024])
    out_d = nc.dram_tensor("out", [D, NT], F32, kind="ExternalOutput").ap()

    projT = dscr("projT", [NCH_IN * 128, NT], BF16)
    dtT = dscr("dtT", [16, NT], F32)
    ymixT = dscr("ymixT", [2048, NT], BF16)
    xres = dscr("xres", [D, NT], F32)
    h2T = dscr("h2T", [D, NT], BF16)
    ropeC = dscr("ropeC", [32, NT], F32); ropeS = dscr("ropeS", [32, NT], F32)

    PS = [nc.alloc_psum_tensor(f"ps{i}", [128, 512], F32) for i in range(8)]
    PB = [Buf(f"ps{i}") for i in range(8)]

    uid = [0]

    def sb(stack, name, shape, dt=F32):
        uid[0] += 1
        return stack.enter_context(nc.sbuf_tensor(f"s{uid[0]}_{name}", list(shape), dt))

    vecs = sb(es, "vecs", [128, NV]); Bvecs = Buf("vecs")
    ones_bf = sb(es, "ones_bf", [128, 128], BF16); ident_bf = sb(es, "ident_bf", [128, 128], BF16)
    ident_f = sb(es, "ident_f", [128, 128]); triu_bf = sb(es, "triu_bf", [128, 128], BF16)
    epsb = sb(es, "epsb", [128, 1]); Bconst = Buf("const")
    T.dma("sp", "vecs", [(vecs[:], vecs_d)], writes=[Bvecs])
    T.op("pool", lambda e: e.memset(ones_bf[:], 1.0), writes=[Bconst])
    T.op("pool", lambda e: e.memset(epsb[:], EPS), writes=[Bconst])
    for t_, cmp_ in ((ident_bf, ALU.is_equal), (ident_f, ALU.is_equal), (triu_bf, ALU.is_ge)):
        T.op("pool", lambda e: e.memset(t_[:], 1.0), writes=[Bconst])
        T.op("pool", lambda e: e.affine_select(out=t_[:], in_=t_[:], pattern=[[1, 128]], compare_op=cmp_,
                                                fill=0.0, base=0, channel_multiplier=-1),
             reads=[Bconst], writes=[Bconst])

    def vc(name, i=0, n=1, p0=0, p1=128):
        return vecs[p0:p1, VM(name, i):VM(name, i) + n]

    def run(ph):
        return phases is None or ph in phases

    if run("rope"):
        with contextlib.ExitStack() as ph:
            PW = 2048
            pi_ = sb(ph, "r_pi", [128, PW], I32); ang = sb(ph, "r_ang", [128, PW]); kf = sb(ph, "r_kf", [128, PW])
            ki = sb(ph, "r_ki", [128, PW], I32); rr = sb(ph, "r_rr", [128, PW]); mm = sb(ph, "r_mm", [128, PW])
            Bp, Ba, Bk, Br, Bm = Buf("pi"), Buf("ang"), Buf("kf"), Buf("rr"), Buf("mm")
            R = slice(64, 96)
            for pc in range(NT // PW):
                cs = slice(pc * PW, (pc + 1) * PW)
                T.dma("sp", "r_ld", [(pi_[R, :], pos[:, cs].partition_broadcast(32))], writes=[Bp])
                T.op("dve", lambda e: e.tensor_copy(out=ang[R, :], in_=pi_[R, :]), reads=[Bp], writes=[Ba])
                T.op("dve", lambda e: e.tensor_scalar(out=ang[R, :], in0=ang[R, :], scalar1=vc("invf", p0=64, p1=96),
                                                      scalar2=None, op0=ALU.mult), reads=[Ba, Bvecs], writes=[Ba])
                for which, dst in ((0, ropeS), (1, ropeC)):
                    src = ang
                    if which == 1:
                        T.op("dve", lambda e: e.tensor_scalar(out=rr[R, :], in0=ang[R, :], scalar1=float(np.pi / 2),
                                                              scalar2=None, op0=ALU.add), reads=[Ba], writes=[Br])
                        src = rr
                    T.op("dve", lambda e: e.tensor_scalar(out=kf[R, :], in0=src[R, :], scalar1=float(1 / TWO_PI),
                                                          scalar2=None, op0=ALU.mult), reads=[Ba, Br], writes=[Bk])
                    T.op("dve", lambda e: e.tensor_copy(out=ki[R, :], in_=kf[R, :]), reads=[Bk], writes=[Bk])
                    T.op("dve", lambda e: e.tensor_copy(out=kf[R, :], in_=ki[R, :]), reads=[Bk], writes=[Bk])
                    T.op("dve", lambda e: e.scalar_tensor_tensor(out=rr[R, :], in0=kf[R, :], scalar=-TWO_PI, in1=src[R, :],
                                                                 op0=ALU.mult, op1=ALU.add), reads=[Bk, Ba, Br], writes=[Br])
                    for (cmp_, thr, add) in ((ALU.is_gt, float(np.pi), -TWO_PI), (ALU.is_lt, float(-np.pi), TWO_PI)):
                        T.op("dve", lambda e: e.tensor_scalar(out=mm[R, :], in0=rr[R, :], scalar1=thr, scalar2=add,
                                                              op0=cmp_, op1=ALU.mult), reads=[Br], writes=[Bm])
                        T.op("dve", lambda e: e.tensor_tensor(out=rr[R, :], in0=rr[R, :], in1=mm[R, :], op=ALU.add),
                             reads=[Br, Bm], writes=[Br])
                    T.op("act", lambda e: e.activation(out=rr[R, :], in_=rr[R, :], func=AF.Sin), reads=[Br], writes=[Br])
                    if which == 0:
                        T.op("dve", lambda e: e.tensor_scalar(out=rr[R, :], in0=rr[R, :], scalar1=vc("sgn", p0=64, p1=96),
                                                              scalar2=None, op0=ALU.mult), reads=[Br, Bvecs], writes=[Br])
                    T.dma("sp", "r_st", [(dst[:, cs], rr[R, :])], reads=[Br])
            T.barrier()

    for l in range(nlayers):
        xsrc = xT if l == 0 else xres
        if run("p1"):
            with contextlib.ExitStack() as ph:
                win = sb(ph, "win", [128, 8, NCH_IN * 128], BF16); Bwin = Buf("win")
                T.dma("pool", "w0", [(win[:, kc, :], w_in_d[l, kc * 128:(kc + 1) * 128, :]) for kc in range(8)], writes=[Bwin])
                xt = [sb(ph, f"p1x{i}", [128, 8, TT]) for i in range(2)]; Bxt = [Buf("xt0"), Buf("xt1")]
                sq = sb(ph, "p1sq", [128, 8, TT], BF16); Bsq = Buf("sq")
                hbs = [sb(ph, f"p1h{i}", [128, 8, TT], BF16) for i in range(2)]; Bhs = [Buf("h0"), Buf("h1")]
                rstd = sb(ph, "p1rstd", [128, TT]); Brs = Buf("rstd")
                stg = [sb(ph, f"p1stg{i}", [128, NCH_IN, TT], BF16) for i in range(2)]
                Bstg = [[Buf(f"stg{i}_{a}") for a in range(4)] for i in range(2)]
                dts = [sb(ph, f"p1dt{i}", [16, TT]) for i in range(2)]; Bdts = [Buf("dts0"), Buf("dts1")]

                def load_x(j):
                    T.dma("sp", f"p1x{j % 2}", [(xt[j % 2][:], xsrc[:, j * TT:(j + 1) * TT].rearrange("(c p) t -> p c t", p=128))],
                          writes=[Bxt[j % 2]])

                def norm_part(j, part):
                    bb = j % 2; X = xt[bb]
                    if part == 0:
                        T.op("act", lambda e: e.activation(out=sq[:], in_=X[:], func=AF.Square), reads=[Bxt[bb]], writes=[Bsq])
                    elif part == 1:
                        for c in range(8):
                            T.op("pe", lambda e: e.matmul(PS[0][:], ones_bf[:], sq[:, c, :], start=(c == 0), stop=(c == 7)),
                                 reads=[Bsq, Bconst], writes=[PB[0]], signal=(c == 7))
                    elif part == 2:
                        T.op("act", lambda e: e.activation(out=rstd[:], in_=PS[0][:], func=AF.Ln, bias=epsb[:, 0:1], scale=1.0 / D),
                             reads=[PB[0], Bconst], writes=[Brs])
                        T.op("act", lambda e: e.activation(out=rstd[:], in_=rstd[:], func=AF.Exp, scale=-0.5), reads=[Brs], writes=[Brs])
                    else:
                        for c in range(8):
                            T.op("dve", lambda e: e.scalar_tensor_tensor(out=hbs[bb][:, c, :], in0=X[:, c, :], scalar=vc(f"attn_norm{l}", c),
                                                                         in1=rstd[:], op0=ALU.mult, op1=ALU.mult),
                                 reads=[Bxt[bb], Brs, Bvecs], writes=[Bhs[bb]], signal=(c == 7))
                load_x(0); load_x(1)
                for part in range(4):
                    norm_part(0, part)
                ev = 0
                for j in range(NTILE):
                    b = j % 2
                    hb = hbs[b]; Bh = Bhs[b]
                    if j >= 1 and j + 1 < NTILE:
                        load_x(j + 1)
                    for m in range(NCH_IN):
                        M = 16 if m == CH_DT else (96 if m >= CH_KPE else 128)
                        pb = 1 + (m % 6)
                        if j + 1 < NTILE and m in (6, 10, 14, 18):
                            norm_part(j + 1, (m - 6) // 4)
                        for kc in range(8):
                            T.op("pe", lambda e: e.matmul(PS[pb][0:M, :], win[:, kc, m * 128:m * 128 + M], hb[:, kc, :],
                                                          start=(kc == 0), stop=(kc == 7)),
                                 reads=[Bh, Bwin], writes=[PB[pb]], signal=(kc == 7))
                        if m == CH_DT:
                            T.op("dve", lambda e: e.tensor_copy(out=dts[b][:], in_=PS[pb][0:16, :]), reads=[PB[pb]], writes=[Bdts[b]])
                            continue
                        a = m // 8
                        if ev % 2 == 0:
                            T.op("act", lambda e: e.activation(out=stg[b][0:M, m, :], in_=PS[pb][0:M, :], func=AF.Copy),
                                 reads=[PB[pb]], writes=[Bstg[b][a]])
                        else:
                            T.op("dve", lambda e: e.tensor_copy(out=stg[b][0:M, m, :], in_=PS[pb][0:M, :]),
                                 reads=[PB[pb]], writes=[Bstg[b][a]])
                        ev += 1
                        if m % 8 == 7:
                            cs = slice(j * TT, (j + 1) * TT)
                            if a < 2:
                                T.dma("pool", f"p1s{b}{a}", [(projT[a * 1024:(a + 1) * 1024, cs].rearrange("(c p) t -> p c t", p=128),
                                                            stg[b][:, a * 8:(a + 1) * 8, :])], reads=[Bstg[b][a]])
                            elif a == 2:
                                T.dma("pool", f"p1s{b}{a}", [(projT[2048:2560, cs].rearrange("(c p) t -> p c t", p=128), stg[b][:, 16:20, :]),
                                                            (projT[2688:3072, cs].rearrange("(c p) t -> p c t", p=128), stg[b][:, 21:24, :])],
                                      reads=[Bstg[b][a]])
                            else:
                                T.dma("pool", f"p1s{b}{a}", [(projT[3072:3840, cs].rearrange("(c p) t -> p c t", p=128), stg[b][:, 24:30, :]),
                                                            (projT[CH_KPE * 128 + 64:CH_KPE * 128 + 96, cs], stg[b][64:96, CH_KPE, :]),
                                                            (projT[CH_KPP * 128 + 64:CH_KPP * 128 + 96, cs], stg[b][64:96, CH_KPP, :])],
                                      reads=[Bstg[b][a]])
                    T.dma("pool", f"p1d{b}", [(dtT[:, j * TT:(j + 1) * TT], dts[b][:])], reads=[Bdts[b]])
                T.barrier()


        if run("p2"):
            with contextlib.ExitStack() as ph:
                pw = sb(ph, "pw", [128, 4, 128], BF16); Bpw = Buf("pw")
                T.dma("pool", "w0", [(pw[:, g, :], pool_w_d[l, g]) for g in range(4)], writes=[Bpw])
                HL = 16
                ut = [sb(ph, f"p2u{i}", [128, 4, HL + TT], BF16) for i in range(2)]; But = [Buf("ut0"), Buf("ut1")]
                uf = sb(ph, "p2uf", [128, HL + TT]); sA = sb(ph, "p2a", [128, HL + TT]); sB = sb(ph, "p2b", [128, HL + TT])
                Buf_, BsA, BsB = Buf("uf"), Buf("sA"), Buf("sB")
                t16 = sb(ph, "p2t16", [128, 16]); Bt16 = Buf("t16")
                pl = [sb(ph, f"p2pl{i}", [128, TT], BF16) for i in range(2)]; Bpl = [Buf("pl0"), Buf("pl1")]
                stg = [sb(ph, f"p2s{i}", [128, 4, TT], BF16) for i in range(2)]; Bstg = [Buf("s0"), Buf("s1")]
                urows = projT[CH_U * 128:(CH_U + 4) * 128, :].rearrange("(c p) t -> p c t", p=128)

                def load_u(j):
                    b = j % 2; t0 = j * TT
                    if j % (S // TT) == 0:
                        T.op("pool", lambda e: e.memset(ut[b][:, :, 0:HL], 0.0), writes=[But[b]])
                        T.dma("sp", f"p2u{b}", [(ut[b][:, :, HL:], urows[:, :, t0:t0 + TT])], writes=[But[b]])
                    else:
                        T.dma("sp", f"p2u{b}", [(ut[b][:, :, :], urows[:, :, t0 - HL:t0 + TT])], writes=[But[b]])
                load_u(0)
                k = 0
                for j in range(NTILE):
                    b = j % 2
                    if j + 1 < NTILE:
                        load_u(j + 1)
                    first = (j % (S // TT) == 0)
                    for g in range(4):
                        w = 2 << g
                        T.op("act", lambda e: e.activation(out=uf[:], in_=ut[b][:, g, :], func=AF.Copy), reads=[But[b]], writes=[Buf_])
                        cur, Bcur = uf, Buf_
                        for s_ in range(g + 1):
                            sh = 1 << s_
                            nxt, Bn = (sA, BsA) if cur is not sA else (sB, BsB)
                            v0 = sh - 1
                            T.op("dve", lambda e: e.tensor_tensor(out=nxt[:, v0 + sh:], in0=cur[:, v0 + sh:], in1=cur[:, v0:HL + TT - sh], op=ALU.add),
                                 reads=[Bcur], writes=[Bn])
                            cur, Bcur = nxt, Bn
                        pb_ = k % 2; k += 1
                        T.op("dve", lambda e: e.scalar_tensor_tensor(out=pl[pb_][:], in0=cur[:, HL:], scalar=1.0 / w, in1=uf[:, HL:],
                                                                     op0=ALU.mult, op1=ALU.subtract), reads=[Bcur, Buf_], writes=[Bpl[pb_]])
                        if first:
                            T.op("dve", lambda e: e.tensor_tensor(out=t16[:], in0=cur[:, HL:HL + 16], in1=vc("pool_rc", g * 16, 16), op=ALU.mult),
                                 reads=[Bcur, Bvecs], writes=[Bt16])
                            T.op("dve", lambda e: e.tensor_tensor(out=pl[pb_][:, 0:16], in0=t16[:], in1=uf[:, HL:HL + 16], op=ALU.subtract),
                                 reads=[Bt16, Buf_], writes=[Bpl[pb_]])
                        pq = 1 + (k % 4)
                        T.op("pe", lambda e: e.matmul(PS[pq][:], pw[:, g, :], pl[pb_][:], start=True, stop=True),
                             reads=[Bpw, Bpl[pb_]], writes=[PB[pq]])
                        T.op("act", lambda e: e.activation(out=stg[b][:, g, :], in_=PS[pq][:], func=AF.Copy, scale=vc(f"pool_scale{l}", g)),
                             reads=[PB[pq], Bvecs], writes=[Bstg[b]])
                    T.dma("sp", f"p2s{b}", [(ymixT[1024:1536, j * TT:(j + 1) * TT].rearrange("(c p) t -> p c t", p=128), stg[b][:])],
                          reads=[Bstg[b]])
                T.barrier()

        if run("p3"):
            with contextlib.ExitStack() as ph:
                HL = 3; NC4 = TT // 128
                xin = [sb(ph, f"p3x{i}", [128, 12, HL + TT], BF16) for i in range(2)]; Bxin = [Buf("xin0"), Buf("xin1")]
                zin = [sb(ph, f"p3z{i}", [128, 8, TT], BF16) for i in range(2)]; Bzin = [Buf("z0"), Buf("z1")]
                dtr = [sb(ph, f"p3d{i}", [16, TT]) for i in range(2)]; Bdtr = [Buf("dtr0"), Buf("dtr1")]
                xc = sb(ph, "p3xc", [128, 12, TT], BF16); Bxc = [Buf(f"xc{c}") for c in range(12)]
                dg = sb(ph, "p3dg", [128, 48, 128], BF16); Bdg = Buf("dg")
                for kc_ in range(48):
                    T.op("pool", lambda e: e.tensor_scalar(out=dg[:, kc_, :], in0=ident_bf[:], scalar1=vc(f"ssd_cw{l}", kc_), scalar2=None, op0=ALU.mult),
                         reads=[Bconst, Bvecs], writes=[Bdg], signal=(kc_ == 47))
                dtT_s = sb(ph, "p3dtT", [16, TT]); BdtT = Buf("dtT")
                arow = sb(ph, "p3arow", [128, 16]); Barow = Buf("arow")
                dt_tok = sb(ph, "p3dttok", [128, NC4, 16]); da_bf = sb(ph, "p3dabf", [128, NC4, 16], BF16)
                cum_s = sb(ph, "p3cum", [128, NC4, 16]); dte = sb(ph, "p3dte", [128, NC4, 16]); dtw = sb(ph, "p3dtw", [128, NC4, 16])
                edec = sb(ph, "p3edec", [128, NC4, 16]); BsmA = Buf("smallA")
                rhsA = sb(ph, "p3rhsA", [128, NC4, 16, 128], BF16); BrhsA = [Buf(f"rhsA{c}") for c in range(NC4)]
                xdt = sb(ph, "p3xdt", [128, NC4, 1024], BF16); Bxdt = [Buf(f"xdt{c}") for c in range(NC4)]
                xw = sb(ph, "p3xw", [128, NC4, 1024], BF16); Bxw = [Buf(f"xw{c}") for c in range(NC4)]
                bst = sb(ph, "p3bst", [128, NC4, 256], BF16); Bbst = [Buf(f"bst{c}") for c in range(NC4)]
                NS3 = 3
                cbm = [sb(ph, f"p3cbm{i}", [128, 128], BF16) for i in range(NS3)]; Bcbm = [Buf(f"cbm{i}") for i in range(NS3)]
                seg = [sb(ph, f"p3seg{i}", [128, 8, 128]) for i in range(NS3)]; BsegA = [Buf(f"segA{i}") for i in range(NS3)]; BsegB = [Buf(f"segB{i}") for i in range(NS3)]
                Dm = [sb(ph, f"p3Dm{i}", [128, 8, 128], BF16) for i in range(NS3)]; BDm = [Buf(f"Dm{i}") for i in range(NS3)]
                Eb = [sb(ph, f"p3Eb{i}", [128, 8, 128], BF16) for i in range(NS3)]; BEb = [Buf(f"Eb{i}") for i in range(NS3)]
                Mm = [sb(ph, f"p3M{i}", [128, 8, 128], BF16) for i in range(NS3)]; BMm = [Buf(f"M{i}") for i in range(NS3)]
                CE = [sb(ph, f"p3CE{i}", [128, 8, 128], BF16) for i in range(NS3)]; BCE = [Buf(f"CE{i}") for i in range(NS3)]
                Hs = sb(ph, "p3H", [128, 1024]); BH = [Buf("H0"), Buf("H1")]
                Hbf = sb(ph, "p3Hbf", [128, NC4, 1024], BF16); BHbf = [[Buf(f"Hbf{c}_{g}") for g in range(2)] for c in range(NC4)]
                ysb = sb(ph, "p3ysb", [128, 8, TT]); Bysb = [Buf(f"ysb{f}") for f in range(8)]
                zsbs = [sb(ph, f"p3zsb{i}", [128, 8, TT], BF16) for i in range(2)]; Bzsbs = [[Buf(f"zsb{i}_{f}") for f in range(8)] for i in range(2)]
                sq = sb(ph, "p3sq", [128, 8, TT], BF16); Bsq = Buf("sq")
                rstd = sb(ph, "p3rstd", [128, TT]); Brs = Buf("rstd")
                yout = [sq] * 2; Byout = [Bsq] * 2
                T.op("act", lambda e: e.activation(out=arow[:], in_=vc(f"a_log{l}", 0, 16), func=AF.Exp), reads=[Bvecs], writes=[Barow])
                T.op("dve", lambda e: e.tensor_scalar(out=arow[:], in0=arow[:], scalar1=-1.0, scalar2=None, op0=ALU.mult), reads=[Barow], writes=[Barow])
                xrows = projT[CH_X * 128:(CH_X + 12) * 128, :].rearrange("(c p) t -> p c t", p=128)
                zrows = projT[0:1024, :].rearrange("(c p) t -> p c t", p=128)
                PSXb = PS[1][:].bitcast(BF16)
                PSBb = PS[2][:].bitcast(BF16)
                DTC, CUMC, TOTC, CBC = 128, 192, 256, 320

                def load3(j):
                    b = j % 2; t0 = j * TT
                    if j % (S // TT) == 0:
                        T.op("pool", lambda e: e.memset(xin[b][:, :, 0:HL], 0.0), writes=[Bxin[b]])
                        T.dma("sp", f"p3x{b}", [(xin[b][:, :, HL:], xrows[:, :, t0:t0 + TT])], writes=[Bxin[b]])
                    else:
                        T.dma("sp", f"p3x{b}", [(xin[b][:, :, :], xrows[:, :, t0 - HL:t0 + TT])], writes=[Bxin[b]])
                    T.dma("sp", f"p3z{b}", [(zin[b][:], zrows[:, :, t0:t0 + TT])], writes=[Bzin[b]])
                    T.dma("sp", f"p3d{b}", [(dtr[b][:], dtT[:, t0:t0 + TT])], writes=[Bdtr[b]])
                def stage0(j):
                    b = j % 2; zsb = zsbs[b]; Bzsb = Bzsbs[b]
                    for c in range(12):
                        pq = 3 + c % 4
                        for k_ in range(4):
                            T.op("pe", lambda e: e.matmul(PS[pq][:], dg[:, k_ * 12 + c, :], xin[b][:, c, k_:k_ + TT], start=(k_ == 0), stop=(k_ == 3)),
                                 reads=[Bdg, Bxin[b]], writes=[PB[pq]], signal=(k_ == 3))
                        T.op("act", lambda e: e.activation(out=xc[:, c, :], in_=PS[pq][:], func=AF.Silu, bias=vc(f"ssd_cb{l}", c)), reads=[PB[pq], Bvecs], writes=[Bxc[c]])
                    for f in range(8):
                        T.op("act", lambda e: e.activation(out=zsb[:, f, :], in_=zin[b][:, f, :], func=AF.Silu), reads=[Bzin[b]], writes=[Bzsb[f]])

                def dtpart(j):
                    b = j % 2
                    T.op("act", lambda e: e.activation(out=dtT_s[:], in_=dtr[b][:], func=AF.Exp, bias=vc(f"dt_bias{l}", p1=16), scale=1.0),
                         reads=[Bdtr[b], Bvecs], writes=[BdtT])
                    T.op("act", lambda e: e.activation(out=dtT_s[:], in_=dtT_s[:], func=AF.Ln, bias=1.0), reads=[BdtT], writes=[BdtT])
                    for c4 in range(NC4):
                        T.op("pe", lambda e: e.transpose(out=PS[2][:, DTC + 16 * c4:DTC + 16 * c4 + 16], in_=dtT_s[0:16, c4 * 128:(c4 + 1) * 128], identity=ident_f[0:16, 0:16]),
                             reads=[BdtT, Bconst], writes=[PB[2]], signal=(c4 == NC4 - 1))

                def epilogueA(j):
                    b = j % 2; zsb = zsbs[b]; Bzsb = Bzsbs[b]
                    for f in range(8):
                        T.op("dve", lambda e: e.tensor_tensor(out=ysb[:, f, :], in0=ysb[:, f, :], in1=zsb[:, f, :], op=ALU.mult), reads=[Bysb[f], Bzsb[f]], writes=[Bysb[f]])
                    T.op("act", lambda e: e.activation(out=sq[:], in_=ysb[:], func=AF.Square), reads=Bysb, writes=[Bsq])
                    for c in range(8):
                        T.op("pe", lambda e: e.matmul(PS[3][:], ones_bf[:], sq[:, c, :], start=(c == 0), stop=(c == 7)),
                             reads=[Bsq, Bconst], writes=[PB[3]], signal=(c == 7))
                    T.op("act", lambda e: e.activation(out=rstd[:], in_=PS[3][:], func=AF.Ln, bias=epsb[:, 0:1], scale=1.0 / 1024), reads=[PB[3], Bconst], writes=[Brs])
                    T.op("act", lambda e: e.activation(out=rstd[:], in_=rstd[:], func=AF.Exp, scale=-0.5), reads=[Brs], writes=[Brs])

                def epilogueB(j):
                    b = j % 2
                    for c in range(8):
                        T.op("dve", lambda e: e.scalar_tensor_tensor(out=yout[b][:, c, :], in0=ysb[:, c, :], scalar=vc(f"ssd_norm{l}", c), in1=rstd[:],
                                                                     op0=ALU.mult, op1=ALU.mult), reads=[Bysb[c], Brs, Bvecs], writes=[Byout[b]], signal=(c == 7))
                    T.dma("sp", f"p3s{b}", [(ymixT[0:1024, j * TT:(j + 1) * TT].rearrange("(c p) t -> p c t", p=128), yout[b][:])], reads=[Byout[b]])

                def stage1a():
                    F2 = lambda t: t[:].rearrange("p c h -> p (c h)")
                    T.op("dve", lambda e: e.tensor_copy(out=F2(dt_tok), in_=PS[2][:, DTC:DTC + 64]), reads=[PB[2]], writes=[BsmA])
                    T.op("dve", lambda e: e.tensor_tensor(out=da_bf[:], in0=dt_tok[:], in1=arow[:].unsqueeze(1).broadcast_to([128, NC4, 16]), op=ALU.mult), reads=[BsmA, Barow], writes=[BsmA])
                    T.op("pe", lambda e: e.matmul(PS[2][:, CUMC:CUMC + 64], triu_bf[:], F2(da_bf), start=True, stop=True), reads=[BsmA, Bconst], writes=[PB[2]], signal=False)
                    T.op("pe", lambda e: e.matmul(PS[2][:, TOTC:TOTC + 64], ones_bf[:], F2(da_bf), start=True, stop=True), reads=[BsmA, Bconst], writes=[PB[2]])
                    T.op("dve", lambda e: e.tensor_copy(out=F2(cum_s), in_=PS[2][:, CUMC:CUMC + 64]), reads=[PB[2]], writes=[BsmA])
                    T.op("dve", lambda e: e.tensor_tensor(out=F2(dte), in0=PS[2][:, TOTC:TOTC + 64], in1=F2(cum_s), op=ALU.subtract), reads=[PB[2], BsmA], writes=[BsmA])
                    T.op("act", lambda e: e.activation(out=F2(dte), in_=F2(dte), func=AF.Exp), reads=[BsmA], writes=[BsmA])
                    T.op("act", lambda e: e.activation(out=F2(edec), in_=PS[2][:, TOTC:TOTC + 64], func=AF.Exp), reads=[PB[2]], writes=[BsmA])
                    T.op("dve", lambda e: e.tensor_tensor(out=F2(dtw), in0=F2(dte), in1=F2(dt_tok), op=ALU.mult), reads=[BsmA], writes=[BsmA])
                    for c4 in range(NC4):
                        T.op("pool", lambda e: e.tensor_tensor(out=rhsA[:, c4, :, :], in0=da_bf[:, c4, :].unsqueeze(2).broadcast_to([128, 16, 128]),
                                                               in1=triu_bf[:].unsqueeze(1).broadcast_to([128, 16, 128]), op=ALU.mult),
                             reads=[BsmA, Bconst], writes=[BrhsA[c4]])

                load3(0)
                dtpart(0)
                stage1a()
                stage0(0)
                for j in range(NTILE):
                    b = j % 2
                    if j + 1 < NTILE:
                        load3(j + 1)
                    if j % (S // TT) == 0:
                        for g in range(2):
                            T.op("pool", lambda e: e.memset(Hs[:, g * 512:(g + 1) * 512], 0.0), writes=[BH[g]])
                            T.op("pool", lambda e: e.memset(Hbf[:, 0, g * 512:(g + 1) * 512], 0.0), writes=[BHbf[0][g]])
                    for c4 in range(NC4):
                        cc = slice(c4 * 128, (c4 + 1) * 128)
                        for f in range(8):
                            T.op("pe", lambda e: e.transpose(out=PSXb[:, f * 128:(f + 1) * 128], in_=xc[:, f, cc], identity=ident_bf[:]),
                                 reads=[Bxc[f], Bconst], writes=[PB[1]], signal=(f == 7))
                        for f in range(2):
                            T.op("pe", lambda e: e.transpose(out=PSBb[:, f * 128:(f + 1) * 128], in_=xc[:, 8 + f, cc], identity=ident_bf[:]),
                                 reads=[Bxc[8 + f], Bconst], writes=[PB[2]], signal=(f == 1))
                        T.op("act", lambda e: e.activation(out=bst[:, c4, :], in_=PSBb[:, 0:256], func=AF.Copy), reads=[PB[2]], writes=[Bbst[c4]])
                        T.op("dve", lambda e: e.tensor_tensor(out=xdt[:, c4, :].rearrange("p (h d) -> p h d", h=16), in0=PSXb.rearrange("p (h d) -> p h d", h=16),
                                                              in1=dt_tok[:, c4, :].unsqueeze(2).broadcast_to([128, 16, 64]), op=ALU.mult),
                             reads=[PB[1], BsmA], writes=[Bxdt[c4]])
                        T.op("dve", lambda e: e.tensor_tensor(out=xw[:, c4, :].rearrange("p (h d) -> p h d", h=16), in0=PSXb.rearrange("p (h d) -> p h d", h=16),
                                                              in1=dtw[:, c4, :].unsqueeze(2).broadcast_to([128, 16, 64]), op=ALU.mult),
                             reads=[PB[1], BsmA], writes=[Bxw[c4]])

                    def stage2(it):
                        c4, g = divmod(it, 2); cc = slice(c4 * 128, (c4 + 1) * 128)
                        pa = (3, 4) if it % 2 == 0 else (5, 6)
                        ip = it % NS3
                        for q in range(2):
                            T.op("pe", lambda e: e.matmul(PS[pa[q]][:], ones_bf[:], rhsA[:, c4, 8 * g + 4 * q:8 * g + 4 * q + 4, :].rearrange("p h l -> p (h l)"),
                                                          start=True, stop=True), reads=[BrhsA[c4], Bconst], writes=[PB[pa[q]]])
                        T.op("pe", lambda e: e.matmul(PS[2][:, CBC:CBC + 128], xc[:, 8 + g, cc], xc[:, 10 + g, cc], start=True, stop=True),
                             reads=[Bxc[8 + g], Bxc[10 + g]], writes=[PB[2]])
                        T.op("dve", lambda e: e.tensor_tensor(out=cbm[ip][:], in0=PS[2][:, CBC:CBC + 128], in1=triu_bf[:], op=ALU.mult),
                             reads=[PB[2], Bconst], writes=[Bcbm[ip]])
                        for hh in range(8):
                            h = 8 * g + hh
                            src_ = PS[pa[hh // 4]][:, (hh % 4) * 128:(hh % 4 + 1) * 128]
                            if hh < 4:
                                T.op("act", lambda e: e.activation(out=seg[ip][:, hh, :], in_=src_, func=AF.Relu, bias=cum_s[:, c4, h:h + 1], scale=-1.0),
                                     reads=[PB[pa[0]], BsmA], writes=[BsegA[ip]], signal=(hh == 3))
                            else:
                                T.op("dve", lambda e: e.tensor_scalar(out=seg[ip][:, hh, :], in0=src_, scalar1=cum_s[:, c4, h:h + 1], scalar2=0.0, op0=ALU.subtract, op1=ALU.min),
                                     reads=[PB[pa[1]], BsmA], writes=[BsegB[ip]], signal=(hh == 7))
                        T.op("act", lambda e: e.activation(out=Dm[ip][:, 0:4, :], in_=seg[ip][:, 0:4, :], func=AF.Exp, scale=-1.0), reads=[BsegA[ip]], writes=[BDm[ip]], signal=False)
                        T.op("act", lambda e: e.activation(out=Dm[ip][:, 4:8, :], in_=seg[ip][:, 4:8, :], func=AF.Exp), reads=[BsegB[ip]], writes=[BDm[ip]])
                        for q in range(2):
                            T.op("act", lambda e: e.activation(out=Eb[ip][:, 4 * q:4 * q + 4, :].rearrange("p h l -> p (h l)"), in_=PS[pa[q]][:], func=AF.Exp),
                                 reads=[PB[pa[q]]], writes=[BEb[ip]], signal=(q == 1))
                        T.op("pool", lambda e: e.tensor_tensor(out=Mm[ip][:], in0=Dm[ip][:], in1=cbm[ip][:].unsqueeze(1).broadcast_to([128, 8, 128]), op=ALU.mult),
                             reads=[BDm[ip], Bcbm[ip]], writes=[BMm[ip]])
                        T.op("pool", lambda e: e.tensor_tensor(out=CE[ip][:], in0=Eb[ip][:], in1=xc[:, 10 + g, cc].unsqueeze(1).broadcast_to([128, 8, 128]), op=ALU.mult),
                             reads=[BEb[ip], Bxc[10 + g]], writes=[BCE[ip]])

                    def stage34(it):
                        c4, g = divmod(it, 2); ip = it % NS3; cc = slice(c4 * 128, (c4 + 1) * 128)
                        py = 7 if it % 2 == 0 else 0
                        for hp in range(4):
                            for hx in range(2):
                                h = 8 * g + 2 * hp + hx
                                o_ = PS[py][hx * 64:(hx + 1) * 64, hp * 128:(hp + 1) * 128]
                                kw = {"tile_position": (0, 64)} if hx == 1 else {}
                                T.op("pe", lambda e: e.matmul(o_, xdt[:, c4, h * 64:(h + 1) * 64], Mm[ip][:, 2 * hp + hx, :], start=True, stop=False, **kw),
                                     reads=[Bxdt[c4], BMm[ip]], writes=[PB[py]], signal=False)
                                T.op("pe", lambda e: e.matmul(o_, Hbf[:, c4, h * 64:(h + 1) * 64], CE[ip][:, 2 * hp + hx, :], start=False, stop=True, **kw),
                                     reads=[BHbf[c4][g], BCE[ip]], writes=[PB[py]], signal=(hp == 3 and hx == 1))
                        T.op("pe", lambda e: e.matmul(PS[1][:], bst[:, c4, g * 128:(g + 1) * 128], xw[:, c4, g * 512:(g + 1) * 512], start=True, stop=True),
                             reads=[Bbst[c4], Bxw[c4]], writes=[PB[1]])
                        for hp in range(4):
                            f = 4 * g + hp
                            T.op("dve", lambda e: e.scalar_tensor_tensor(out=ysb[:, f, cc], in0=xc[:, f, cc], scalar=vc(f"dskip{l}", f),
                                                                         in1=PS[py][:, hp * 128:(hp + 1) * 128], op0=ALU.mult, op1=ALU.add),
                                 reads=[Bxc[f], PB[py], Bvecs], writes=[Bysb[f]])
                        Hg = Hs[:, g * 512:(g + 1) * 512]
                        T.op("dve", lambda e: e.tensor_tensor(out=Hg.rearrange("p (h d) -> p h d", h=8), in0=Hg.rearrange("p (h d) -> p h d", h=8),
                                                              in1=edec[:, c4, 8 * g:8 * g + 8].unsqueeze(2).broadcast_to([128, 8, 64]), op=ALU.mult),
                             reads=[BH[g], BsmA], writes=[BH[g]])
                        T.op("dve", lambda e: e.tensor_tensor(out=Hg, in0=Hg, in1=PS[1][:], op=ALU.add), reads=[BH[g], PB[1]], writes=[BH[g]])
                        nx = (c4 + 1) % NC4
                        T.op("act", lambda e: e.activation(out=Hbf[:, nx, g * 512:(g + 1) * 512], in_=Hg, func=AF.Copy), reads=[BH[g]], writes=[BHbf[nx][g]])
                    NIT = 2 * NC4
                    for it in range(NIT + 2):
                        if it < NIT:
                            stage2(it)
                        if it >= 2:
                            stage34(it - 2)
                    if j + 1 < NTILE:
                        dtpart(j + 1)
                    epilogueA(j)
                    if j + 1 < NTILE:
                        stage1a()
                        stage0(j + 1)
                    epilogueB(j)
                T.barrier()

        if run("p4"):
            with contextlib.ExitStack() as ph:
                SC = 1.0 / float(np.sqrt(96.0))
                wuq = sb(ph, "wuq", [128, 3, 8, 96], BF16); wuqp = sb(ph, "wuqp", [128, 3, 8, 96], BF16)
                wkn = sb(ph, "wkn", [128, 2, 512], BF16); wv = sb(ph, "wv", [128, 2, 512], BF16); Bw4 = Buf("w4")
                T.dma("pool", "w0", [(wuq[:, c, :, :], w_uq_d[l, c * 128:(c + 1) * 128]) for c in range(3)]
                      + [(wuqp[:, c, :, :], w_uqp_d[l, c * 128:(c + 1) * 128]) for c in range(3)]
                      + [(wkn[:, c, :], w_kn_d[l, c * 128:(c + 1) * 128, :]) for c in range(2)]
                      + [(wv[:, c, :], w_v_d[l, c * 128:(c + 1) * 128, :]) for c in range(2)], writes=[Bw4])
                kT = sb(ph, "kT", [128, 8, S], BF16); BkT = Buf("kT")
                Va = sb(ph, "Va", [128, S // 128, 8, 128], BF16); BVa = Buf("Va")
                T.op("pool", lambda e: e.memset(Va[:].rearrange("p a h d -> p (a h d)"), 1.0), writes=[BVa])
                cq = [sb(ph, f"cq{i}", [128, 3, TT], BF16) for i in range(2)]; Bcq = [Buf("cq0"), Buf("cq1")]
                ckv = [sb(ph, f"ckv{i}", [128, 2, TT], BF16) for i in range(2)]; Bckv = [Buf("ckv0"), Buf("ckv1")]
                kpe = [sb(ph, f"kpe{i}", [128, 2, TT], BF16) for i in range(2)]; Bkpe = [Buf("kpe0"), Buf("kpe1")]
                rcs = [sb(ph, f"rcs{i}", [128, 2, TT]) for i in range(2)]; Brcs = [Buf("rcs0"), Buf("rcs1")]
                sq = sb(ph, "p4sq", [128, 3, TT], BF16); Bsq = Buf("sq")
                rstd = sb(ph, "p4rstd", [128, TT]); Brs = Buf("rstd")
                cqn = sb(ph, "cqn", [128, 3, TT], BF16); Bcqn = Buf("cqn")
                ckvn = sb(ph, "ckvn", [128, 2, TT], BF16); Bckvn = Buf("ckvn")
                t1 = sb(ph, "p4t1", [128, TT]); t2 = sb(ph, "p4t2", [128, TT]); Bt1 = Buf("t1"); Bt2 = Buf("t2")
                qT = sb(ph, "qT", [128, 8, TT], BF16); BqT = [Buf(f"qT{h}") for h in range(8)]
                NPT = 8
                Pt = [sb(ph, f"Pt{i}", [128, TT], BF16) for i in range(NPT)]; BPt = [Buf(f"Pt{i}") for i in range(NPT)]
                Rr = t2; BRr = Bt2
                rb = sb(ph, "rb", [64, TT]); Brb = Buf("rb")
                yo = [sb(ph, "yo", [64, 8, TT], BF16)] * 2; Byo = [Buf("yo")] * 2
                R = slice(64, 96)
                cqrows = projT[CH_CQ * 128:(CH_CQ + 3) * 128, :].rearrange("(c p) t -> p c t", p=128)
                ckvrows = projT[CH_CKV * 128:(CH_CKV + 2) * 128, :].rearrange("(c p) t -> p c t", p=128)

                def load4(j):
                    b = j % 2; cs = slice(j * TT, (j + 1) * TT)
                    T.dma("sp", f"p4a{b}", [(cq[b][:], cqrows[:, :, cs])], writes=[Bcq[b]])
                    T.dma("sp", f"p4b{b}", [(ckv[b][:], ckvrows[:, :, cs])], writes=[Bckv[b]])
                    T.dma("sp", f"p4c{b}", [(kpe[b][R, 0, :], projT[CH_KPE * 128 + 64:CH_KPE * 128 + 96, cs]),
                                            (kpe[b][R, 1, :], projT[CH_KPP * 128 + 64:CH_KPP * 128 + 96, cs])], writes=[Bkpe[b]])
                    T.dma("sp", f"p4d{b}", [(rcs[b][R, 0, :], ropeC[:, cs]), (rcs[b][R, 1, :], ropeS[:, cs])], writes=[Brcs[b]])

                def rms(src, Bsrc, nchk, dst, Bdst, wname):
                    T.op("act", lambda e: e.activation(out=sq[:, 0:nchk, :], in_=src[:], func=AF.Square), reads=[Bsrc], writes=[Bsq])
                    for c in range(nchk):
                        T.op("pe", lambda e: e.matmul(PS[0][:], ones_bf[:], sq[:, c, :], start=(c == 0), stop=(c == nchk - 1)),
                             reads=[Bsq, Bconst], writes=[PB[0]], signal=(c == nchk - 1))
                    T.op("act", lambda e: e.activation(out=rstd[:], in_=PS[0][:], func=AF.Ln, bias=epsb[:, 0:1], scale=1.0 / (128 * nchk)),
                         reads=[PB[0], Bconst], writes=[Brs])
                    T.op("act", lambda e: e.activation(out=rstd[:], in_=rstd[:], func=AF.Exp, scale=-0.5), reads=[Brs], writes=[Brs])
                    for c in range(nchk):
                        T.op("dve", lambda e: e.scalar_tensor_tensor(out=dst[:, c, :], in0=src[:, c, :], scalar=vc(wname, c), in1=rstd[:],
                                                                     op0=ALU.mult, op1=ALU.mult), reads=[Bsrc, Brs, Bvecs], writes=[Bdst], signal=(c == nchk - 1))
                load4(0)
                pk = 0; rot = 0
                for j in range(NTILE):
                    b = j % 2; jj = j % (S // TT); cl = slice(jj * TT, (jj + 1) * TT)
                    if j + 1 < NTILE:
                        load4(j + 1)
                    rms(cq[b], Bcq[b], 3, cqn, Bcqn, f"q_norm{l}")
                    rms(ckv[b], Bckv[b], 2, ckvn, Bckvn, f"kv_norm{l}")
                    T.op("dve", lambda e: e.tensor_tensor(out=t1[R, :], in0=kpe[b][R, 0, :], in1=rcs[b][R, 0, :], op=ALU.mult), reads=[Bkpe[b], Brcs[b]], writes=[Bt1])
                    T.op("dve", lambda e: e.tensor_tensor(out=t2[R, :], in0=kpe[b][R, 1, :], in1=rcs[b][R, 1, :], op=ALU.mult), reads=[Bkpe[b], Brcs[b]], writes=[Bt2])
                    T.op("dve", lambda e: e.tensor_tensor(out=t1[R, :], in0=t1[R, :], in1=t2[R, :], op=ALU.add), reads=[Bt1, Bt2], writes=[Bt1])
                    T.op("act", lambda e: e.activation(out=kT[R, :, cl], in_=t1[R, :].unsqueeze(1).broadcast_to([32, 8, TT]), func=AF.Copy), reads=[Bt1], writes=[BkT])
                    for h in range(8):
                        pq = 1 + rot % 4; rot += 1
                        for c in range(2):
                            T.op("pe", lambda e: e.matmul(PS[pq][0:64, :], wkn[:, c, h * 64:(h + 1) * 64], ckvn[:, c, :], start=(c == 0), stop=(c == 1)),
                                 reads=[Bw4, Bckvn], writes=[PB[pq]], signal=(c == 1))
                        T.op("act", lambda e: e.activation(out=kT[0:64, h, cl], in_=PS[pq][0:64, :], func=AF.Copy), reads=[PB[pq]], writes=[BkT])
                    for blk in range(4):
                        pq = 1 + rot % 4; rot += 1
                        for c in range(2):
                            T.op("pe", lambda e: e.matmul(PS[pq][:], ckvn[:, c, blk * 128:(blk + 1) * 128], wv[:, c, :], start=(c == 0), stop=(c == 1)),
                                 reads=[Bw4, Bckvn], writes=[PB[pq]], signal=(c == 1))
                        en = "act"
                        if en == "act":
                            T.op("act", lambda e: e.activation(out=Va[:, jj * 4 + blk, :, 0:64], in_=PS[pq][:].rearrange("p (h d) -> p h d", h=8), func=AF.Copy),
                                 reads=[PB[pq]], writes=[BVa])
                        else:
                            T.op("dve", lambda e: e.tensor_copy(out=Va[:, jj * 4 + blk, :, 0:64], in_=PS[pq][:].rearrange("p (h d) -> p h d", h=8)),
                                 reads=[PB[pq]], writes=[BVa])
                    for h in range(8):
                        pq = 1 + rot % 4; rot += 1
                        pq2 = 1 + rot % 4; rot += 1
                        for c in range(3):
                            T.op("pe", lambda e: e.matmul(PS[pq][0:96, :], wuq[:, c, h, :], cqn[:, c, :], start=(c == 0), stop=(c == 2)),
                                 reads=[Bw4, Bcqn], writes=[PB[pq]], signal=(c == 2))
                        for c in range(3):
                            T.op("pe", lambda e: e.matmul(PS[pq2][0:96, :], wuqp[:, c, h, :], cqn[:, c, :], start=(c == 0), stop=(c == 2)),
                                 reads=[Bw4, Bcqn], writes=[PB[pq2]], signal=(c == 2))
                        T.op("act", lambda e: e.activation(out=qT[0:64, h, :], in_=PS[pq][0:64, :], func=AF.Copy), reads=[PB[pq]], writes=[BqT[h]])
                        T.op("dve", lambda e: e.tensor_tensor(out=t1[R, :], in0=PS[pq][R, :], in1=rcs[b][R, 0, :], op=ALU.mult), reads=[PB[pq], Brcs[b]], writes=[Bt1])
                        T.op("dve", lambda e: e.tensor_tensor(out=t2[R, :], in0=PS[pq2][R, :], in1=rcs[b][R, 1, :], op=ALU.mult), reads=[PB[pq2], Brcs[b]], writes=[Bt2])
                        T.op("dve", lambda e: e.tensor_tensor(out=qT[R, h, :], in0=t1[R, :], in1=t2[R, :], op=ALU.add), reads=[Bt1, Bt2], writes=[BqT[h]])
                    nkb = 4 * (jj + 1)
                    steps = [(h, kb) for h in range(8) for kb in range(nkb)]
                    LA = 6
                    info = {}
                    deferred = []

                    def emit_score(i):
                        nonlocal rot, pk
                        h, kb = steps[i]
                        off = 0 if kb < 4 * jj else (kb - 4 * jj) * 128
                        pq = 1 + rot % 4; rot += 1
                        pp = pk % NPT; pk += 1
                        info[i] = (off, pp)
                        T.op("pe", lambda e: e.matmul(PS[pq][:, off:TT], kT[0:96, h, kb * 128:(kb + 1) * 128], qT[0:96, h, off:TT], start=True, stop=True),
                             reads=[BkT, BqT[h]], writes=[PB[pq]])
                        T.op("act", lambda e: e.activation(out=Pt[pp][:, off:TT], in_=PS[pq][:, off:TT], func=AF.Exp, scale=SC), reads=[PB[pq]], writes=[BPt[pp]])
                        if kb >= 4 * jj:
                            T.op("pool", lambda e: e.tensor_tensor(out=Pt[pp][:, off:off + 128], in0=Pt[pp][:, off:off + 128], in1=triu_bf[:], op=ALU.mult),
                                 reads=[BPt[pp], Bconst], writes=[BPt[pp]])

                    def fin2(h):
                        po = 5 + h % 3
                        T.op("pe", lambda e: e.matmul(PS[0][0:64, :], ident_f[64:128, 64:128], Rr[64:128, :], start=True, stop=True),
                             reads=[BRr, Bconst], writes=[PB[0]])
                        T.op("dve", lambda e: e.tensor_copy(out=rb[:], in_=PS[0][0:64, :]), reads=[PB[0]], writes=[Brb])
                        T.op("dve", lambda e: e.tensor_tensor(out=yo[b][:, h, :], in0=PS[po][0:64, :], in1=rb[:], op=ALU.mult), reads=[PB[po], Brb], writes=[Byo[b]])

                    def emit_pv(i):
                        h, kb = steps[i]; off, pp = info.pop(i); po = 5 + h % 3
                        T.op("pe", lambda e: e.matmul(PS[po][:, off:TT], Va[:, kb, h, :], Pt[pp][:, off:TT], start=(kb == 0), stop=(kb == nkb - 1)),
                             reads=[BVa, BPt[pp]], writes=[PB[po]], signal=(kb == nkb - 1))
                        if kb == nkb - 1:
                            T.op("dve", lambda e: e.reciprocal(out=Rr[64:128, :], in_=PS[po][64:128, :]), reads=[PB[po]], writes=[BRr])
                            deferred.append((i + min(9, nkb - 1), h))
                    for i in range(len(steps) + LA + 12):
                        if i < len(steps):
                            emit_score(i)
                        if 0 <= i - LA < len(steps):
                            emit_pv(i - LA)
                        while deferred and deferred[0][0] <= i - LA:
                            fin2(deferred.pop(0)[1])
                    assert not deferred
                    T.dma("sp", f"p4s{b}", [(ymixT[1536:2048, j * TT:(j + 1) * TT].rearrange("(h p) t -> p h t", p=64), yo[b][:])], reads=[Byo[b]])
                T.barrier()

        if run("p5"):
            with contextlib.ExitStack() as ph:
                wo = sb(ph, "wo", [128, 16, 1024], BF16); Bwo = Buf("wo")
                T.dma("pool", "w0", [(wo[:, kc, :], w_out_d[l, kc * 128:(kc + 1) * 128, :]) for kc in range(16)], writes=[Bwo])
                ym = [sb(ph, f"p5y{i}", [128, 16, TT], BF16) for i in range(2)]; Bym = [Buf("ym0"), Buf("ym1")]
                xt = [sb(ph, f"p5x{i}", [128, 8, TT]) for i in range(2)]; Bxt = [Buf("xt0"), Buf("xt1")]
                x1 = [sb(ph, f"p5o{i}", [128, 8, TT]) for i in range(2)]; Bx1 = [Buf("x10"), Buf("x11")]
                sq = sb(ph, "p5sq", [128, 8, TT], BF16); Bsq = Buf("sq")
                rstd = sb(ph, "p5rstd", [128, TT]); Brs = Buf("rstd")
                h2 = [sb(ph, f"p5h{i}", [128, 8, TT], BF16) for i in range(2)]; Bh2 = [Buf("h20"), Buf("h21")]

                def load5(j):
                    b = j % 2; cs = slice(j * TT, (j + 1) * TT)
                    T.dma("sp", f"p5y{b}", [(ym[b][:], ymixT[:, cs].rearrange("(c p) t -> p c t", p=128))], writes=[Bym[b]])
                    T.dma("sp", f"p5x{b}", [(xt[b][:], xsrc[:, cs].rearrange("(c p) t -> p c t", p=128))], writes=[Bxt[b]])
                load5(0)
                for j in range(NTILE):
                    b = j % 2; cs = slice(j * TT, (j + 1) * TT)
                    if j + 1 < NTILE:
                        load5(j + 1)
                    for oc in range(8):
                        pq = 1 + oc % 6
                        for kc in range(16):
                            T.op("pe", lambda e: e.matmul(PS[pq][:], wo[:, kc, oc * 128:(oc + 1) * 128], ym[b][:, kc, :], start=(kc == 0), stop=(kc == 15)),
                                 reads=[Bwo, Bym[b]], writes=[PB[pq]], signal=(kc == 15))
                        T.op("dve", lambda e: e.tensor_tensor(out=x1[b][:, oc, :], in0=PS[pq][:], in1=xt[b][:, oc, :], op=ALU.add),
                             reads=[PB[pq], Bxt[b]], writes=[Bx1[b]])
                    T.dma("pool", f"p5s{b}", [(xres[:, cs].rearrange("(c p) t -> p c t", p=128), x1[b][:])], reads=[Bx1[b]])
                    T.op("act", lambda e: e.activation(out=sq[:], in_=x1[b][:], func=AF.Square), reads=[Bx1[b]], writes=[Bsq])
                    for c in range(8):
                        T.op("pe", lambda e: e.matmul(PS[0][:], ones_bf[:], sq[:, c, :], start=(c == 0), stop=(c == 7)),
                             reads=[Bsq, Bconst], writes=[PB[0]], signal=(c == 7))
                    T.op("act", lambda e: e.activation(out=rstd[:], in_=PS[0][:], func=AF.Ln, bias=epsb[:, 0:1], scale=1.0 / D), reads=[PB[0], Bconst], writes=[Brs])
                    T.op("act", lambda e: e.activation(out=rstd[:], in_=rstd[:], func=AF.Exp, scale=-0.5), reads=[Brs], writes=[Brs])
                    for c in range(8):
                        T.op("dve", lambda e: e.scalar_tensor_tensor(out=h2[b][:, c, :], in0=x1[b][:, c, :], scalar=vc(f"ffn_norm{l}", c), in1=rstd[:],
                                                                     op0=ALU.mult, op1=ALU.mult), reads=[Bx1[b], Brs, Bvecs], writes=[Bh2[b]])
                    T.dma("pool", f"p5t{b}", [(h2T[:, cs].rearrange("(c p) t -> p c t", p=128), h2[b][:])], reads=[Bh2[b]])
                T.barrier()

        for hf in range(2):
            if not run("p6"):
                continue
            with contextlib.ExitStack() as ph:
                NH = 11; W = NH * 128
                wu = sb(ph, "wu", [128, 8, 2 * W], BF16); wd = sb(ph, "wd", [128, NH, 1024], BF16); Bwu = Buf("wu"); Bwd = Buf("wd")
                T.dma("pool", "w0", [(wu[:, kc, 0:W], w_up_d[l, kc * 128:(kc + 1) * 128, hf * W:(hf + 1) * W]) for kc in range(8)]
                      + [(wu[:, kc, W:2 * W], w_up_d[l, kc * 128:(kc + 1) * 128, DFF + hf * W:DFF + (hf + 1) * W]) for kc in range(8)], writes=[Bwu])
                T.dma("pool", "w1", [(wd[:, i, :], w_dn_d[l, (hf * NH + i) * 128:(hf * NH + i + 1) * 128, :]) for i in range(NH)], writes=[Bwd])
                hb = [sb(ph, f"p6h{i}", [128, 8, TT], BF16) for i in range(2)]; Bhb = [Buf("hb0"), Buf("hb1")]
                xin = [sb(ph, f"p6x{i}", [128, 8, TT]) for i in range(2)]; Bxin = [Buf("xin0"), Buf("xin1")]
                xo = [sb(ph, f"p6o{i}", [128, 8, TT]) for i in range(2)]; Bxo = [Buf("xo0"), Buf("xo1")]
                ag = [sb(ph, f"p6ag{i}", [128, TT]) for i in range(2)]; Bag = [Buf("ag0"), Buf("ag1")]
                av = [sb(ph, f"p6av{i}", [128, TT]) for i in range(2)]; Bav = [Buf("av0"), Buf("av1")]
                gg = sb(ph, "p6g", [128, NH, TT], BF16); Bgg = [Buf(f"g{i}") for i in range(NH)]
                final = (l == nlayers - 1 and hf == 1)
                if final:
                    sq = sb(ph, "p6sq", [128, 8, TT], BF16); Bsq = Buf("sq")
                    rstd = sb(ph, "p6rstd", [128, TT]); Brs = Buf("rstd")
                tiles = []
                for s_ in range(NSEQ):
                    t0 = 0
                    while t0 < S:
                        ln = min(TT - 2, S - t0); tiles.append((s_, t0, ln)); t0 += ln
                dst = out_d if final else xres

                def load6(i):
                    s_, t0, ln = tiles[i]; b = i % 2; g0 = s_ * S + t0
                    if t0 == 0:
                        T.op("pool", lambda e: e.memset(hb[b][:, :, 0:2], 0.0), writes=[Bhb[b]])
                        T.dma("sp", f"p6h{b}", [(hb[b][:, :, 2:2 + ln], h2T[:, g0:g0 + ln].rearrange("(c p) t -> p c t", p=128))], writes=[Bhb[b]])
                    else:
                        T.dma("sp", f"p6h{b}", [(hb[b][:, :, 0:2 + ln], h2T[:, g0 - 2:g0 + ln].rearrange("(c p) t -> p c t", p=128))], writes=[Bhb[b]])
                    T.dma("sp", f"p6x{b}", [(xin[b][:, :, 0:ln], xres[:, g0:g0 + ln].rearrange("(c p) t -> p c t", p=128))], writes=[Bxin[b]])
                load6(0)
                rot = 0
                for i, (s_, t0, ln) in enumerate(tiles):
                    b = i % 2; N = ln + 2; g0 = s_ * S + t0
                    if i + 1 < len(tiles):
                        load6(i + 1)
                    for ci in range(NH):
                        pg = 1 + rot % 6; rot += 1
                        pv = 1 + rot % 6; rot += 1
                        for (pq, co) in ((pg, ci * 128), (pv, W + ci * 128)):
                            for kc in range(8):
                                T.op("pe", lambda e: e.matmul(PS[pq][:, 0:N], wu[:, kc, co:co + 128], hb[b][:, kc, 0:N], start=(kc == 0), stop=(kc == 7)),
                                     reads=[Bwu, Bhb[b]], writes=[PB[pq]], signal=(kc == 7))
                        cg = hf * NH + ci; cv = 22 + hf * NH + ci
                        for (pq, cidx, a_, Ba) in ((pg, cg, ag[ci % 2], Bag[ci % 2]), (pv, cv, av[ci % 2], Bav[ci % 2])):
                            T.op("act", lambda e: e.activation(out=a_[:, 0:ln], in_=PS[pq][:, 2:N], func=AF.Identity, bias=vc(f"ffn_cb{l}", cidx),
                                                               scale=vc(f"ffn_cw{l}", 2 * 44 + cidx)), reads=[PB[pq], Bvecs], writes=[Ba])
                            T.op("dve", lambda e: e.scalar_tensor_tensor(out=a_[:, 0:ln], in0=PS[pq][:, 1:N - 1], scalar=vc(f"ffn_cw{l}", 1 * 44 + cidx), in1=a_[:, 0:ln],
                                                                         op0=ALU.mult, op1=ALU.add), reads=[PB[pq], Bvecs, Ba], writes=[Ba])
                            T.op("dve", lambda e: e.scalar_tensor_tensor(out=a_[:, 0:ln], in0=PS[pq][:, 0:N - 2], scalar=vc(f"ffn_cw{l}", 0 * 44 + cidx), in1=a_[:, 0:ln],
                                                                         op0=ALU.mult, op1=ALU.add), reads=[PB[pq], Bvecs, Ba], writes=[Ba])
                        T.op("act", lambda e: e.activation(out=ag[ci % 2][:, 0:ln], in_=ag[ci % 2][:, 0:ln], func=AF.Silu), reads=[Bag[ci % 2]], writes=[Bag[ci % 2]])
                        T.op("pool", lambda e: e.tensor_tensor(out=gg[:, ci, 0:ln], in0=ag[ci % 2][:, 0:ln], in1=av[ci % 2][:, 0:ln], op=ALU.mult),
                             reads=[Bag[ci % 2], Bav[ci % 2]], writes=[Bgg[ci]])
                    for oc in range(8):
                        pq = 7 if oc % 2 == 0 else 0
                        for ci in range(NH):
                            T.op("pe", lambda e: e.matmul(PS[pq][:, 0:ln], wd[:, ci, oc * 128:(oc + 1) * 128], gg[:, ci, 0:ln], start=(ci == 0), stop=(ci == NH - 1)),
                                 reads=[Bwd, Bgg[ci]], writes=[PB[pq]], signal=(ci == NH - 1))
                        T.op("dve", lambda e: e.tensor_tensor(out=xo[b][:, oc, 0:ln], in0=PS[pq][:, 0:ln], in1=xin[b][:, oc, 0:ln], op=ALU.add),
                             reads=[PB[pq], Bxin[b]], writes=[Bxo[b]])
                    if final:
                        T.op("act", lambda e: e.activation(out=sq[:, :, 0:ln], in_=xo[b][:, :, 0:ln], func=AF.Square), reads=[Bxo[b]], writes=[Bsq])
                        for c in range(8):
                            T.op("pe", lambda e: e.matmul(PS[1][:, 0:ln], ones_bf[:], sq[:, c, 0:ln], start=(c == 0), stop=(c == 7)),
                                 reads=[Bsq, Bconst], writes=[PB[1]], signal=(c == 7))
                        T.op("act", lambda e: e.activation(out=rstd[:, 0:ln], in_=PS[1][:, 0:ln], func=AF.Ln, bias=epsb[:, 0:1], scale=1.0 / D), reads=[PB[1], Bconst], writes=[Brs])
                        T.op("act", lambda e: e.activation(out=rstd[:, 0:ln], in_=rstd[:, 0:ln], func=AF.Exp, scale=-0.5), reads=[Brs], writes=[Brs])
                        for c in range(8):
                            T.op("dve", lambda e: e.scalar_tensor_tensor(out=xo[b][:, c, 0:ln], in0=xo[b][:, c, 0:ln], scalar=vc("final_norm", c), in1=rstd[:, 0:ln],
                                                                         op0=ALU.mult, op1=ALU.mult), reads=[Brs, Bvecs, Bxo[b]], writes=[Bxo[b]])
                    T.dma("pool", f"p6s{b}", [(dst[:, g0:g0 + ln].rearrange("(c p) t -> p c t", p=128), xo[b][:, :, 0:ln])], reads=[Bxo[b]])
                T.barrier()

    T.barrier()
    return nc, es


def _prep_inputs(inp):
    x = np.asarray(inp["x"], np.float32)
    posi = np.asarray(inp["positions"], np.int32)
    shared = {
        "vecs": _build_vecs(inp),
        "w_in": np.stack([_layout_w_in(inp["w_in"][l]) for l in range(DEPTH)]),
        "pool_w": np.ascontiguousarray(np.asarray(inp["pool_w"], np.float32)),
    }
    uq = np.asarray(inp["mla_w_uq"], np.float32).reshape(DEPTH, 384, 8, 96)
    uqp = uq.copy()
    uqp[..., 64:80] = uq[..., 80:96]; uqp[..., 80:96] = uq[..., 64:80]
    ukv = np.asarray(inp["mla_w_ukv"], np.float32).reshape(DEPTH, 256, 8, 128)
    shared["w_uq"] = np.ascontiguousarray(uq); shared["w_uqp"] = np.ascontiguousarray(uqp)
    shared["w_kn"] = np.ascontiguousarray(ukv[..., :64].reshape(DEPTH, 256, 512))
    shared["w_v"] = np.ascontiguousarray(ukv[..., 64:].reshape(DEPTH, 256, 512))
    shared["w_out"] = np.ascontiguousarray(np.asarray(inp["w_out"], np.float32))
    shared["w_up"] = np.ascontiguousarray(np.asarray(inp["ffn_w_up"], np.float32))
    shared["w_dn"] = np.ascontiguousarray(np.asarray(inp["ffn_w_down"], np.float32))
    in_maps = []
    for c in range(NCORE):
        m = dict(shared)
        m["xT"] = np.ascontiguousarray(np.concatenate([x[2 * c].T, x[2 * c + 1].T], axis=1))
        m["pos"] = np.ascontiguousarray(posi[2 * c:2 * c + 2].reshape(1, NT))
        in_maps.append(m)
    return in_maps


def kernel(**inp):
    in_maps = _prep_inputs(inp)
    nc, es = build_program()
    res = run_bass_kernel_spmd(nc, in_maps, core_ids=list(range(NCORE)))
    out = np.empty((2 * NCORE, S, D), np.float32)
    for c in range(NCORE):
        o = np.asarray(res.results[c]["out"])
        out[2 * c] = o[:, :S].T
        out[2 * c + 1] = o[:, S:].T
    return out
```

```python
import contextlib
import numpy as np
import concourse.bass as bass
import concourse.mybir as mybir
from concourse.bass_utils import run_bass_kernel_spmd

F32 = mybir.dt.float32; BF16 = mybir.dt.bfloat16; I32 = mybir.dt.int32
AF = mybir.ActivationFunctionType; ALU = mybir.AluOpType

NCORE = 8; D = 1024; S = 4096; NSEQ = 2; NT = NSEQ * S; TT = 512; NTILE = NT // TT
DEPTH = 2; DFF = 2816; EPS = 1e-6
NCH_IN = 32
CH_Z, CH_X, CH_B, CH_C, CH_DT, CH_U, CH_CQ, CH_CKV, CH_KPE, CH_KPP = 0, 8, 16, 18, 20, 21, 25, 28, 30, 31
TWO_PI = float(2 * np.pi)


class Buf:
    __slots__ = ("name", "w", "r")

    def __init__(self, name):
        self.name = name; self.w = None; self.r = []


class Trk:
    ENG = ("pe", "act", "dve", "pool", "sp")

    def __init__(self, nc, stack):
        self.nc = nc; self.stack = stack
        self.eng = {"pe": nc.tensor, "act": nc.scalar, "dve": nc.vector, "pool": nc.gpsimd, "sp": nc.sync}
        self.sem = {k: stack.enter_context(nc.semaphore("prog_" + k)) for k in self.ENG}
        self.cnt = {k: 0 for k in self.ENG}
        self.waited = {}
        self.dmasems = {}; self.dmacnt = {}
        self.nins = 0

    def _wait(self, e, tok):
        if tok is None:
            return
        sem, val, key, src = tok
        if src == e and (e == "pe" or val > self.cnt[e]):
            return
        k = (e, key)
        if self.waited.get(k, 0) >= val:
            return
        self.waited[k] = val
        self.eng[e].wait_ge(sem, val)

    def _deps(self, e, reads, writes):
        for b in reads:
            self._wait(e, b.w)
        for b in writes:
            self._wait(e, b.w)
            for t in b.r:
                self._wait(e, t)

    def _mark(self, tok, reads, writes):
        for b in reads:
            b.r.append(tok)
            if len(b.r) > 8:
                d = {}
                for t in b.r:
                    if t[2] not in d or d[t[2]][1] < t[1]:
                        d[t[2]] = t
                b.r = list(d.values())
        for b in writes:
            b.w = tok; b.r = []

    def op(self, e, fn, reads=(), writes=(), signal=True):
        self._deps(e, reads, writes)
        ins = fn(self.eng[e])
        self.nins += 1
        if signal:
            self.cnt[e] += 1
            ins.then_inc(self.sem[e], 1)
            tok = (self.sem[e], self.cnt[e], "prog_" + e, e)
        else:
            tok = (self.sem[e], self.cnt[e] + 1, "prog_" + e, e)
        self._mark(tok, reads, writes)
        return tok

    def dma(self, q, chan, pairs, reads=(), writes=()):
        self._deps(q, reads, writes)
        if chan not in self.dmasems:
            self.dmasems[chan] = self.stack.enter_context(self.nc.semaphore("dma_" + chan))
            self.dmacnt[chan] = 0
        for (o, i) in pairs:
            self.dmacnt[chan] += 16
            self.eng[q].dma_start(out=o, in_=i).then_inc(self.dmasems[chan], 16)
            self.nins += 1
        tok = (self.dmasems[chan], self.dmacnt[chan], "dma_" + chan, "dma")
        self._mark(tok, reads, writes)
        return tok

    def barrier(self):
        toks = []
        for k in self.ENG:
            if self.cnt[k] > 0:
                toks.append((self.sem[k], self.cnt[k], "prog_" + k, k))
        for c, s in self.dmasems.items():
            toks.append((s, self.dmacnt[c], "dma_" + c, "dma"))
        for e in self.ENG:
            for t in toks:
                if t[3] == e:
                    continue
                self._wait(e, t)


class VMap:
    def __init__(self):
        self.off = {}; self.n = 0

    def add(self, name, n):
        self.off[name] = self.n; self.n += n

    def __call__(self, name, i=0):
        return self.off[name] + i


def _vmap():
    V = VMap()
    for l in range(DEPTH):
        V.add(f"attn_norm{l}", 8); V.add(f"ssd_cw{l}", 48); V.add(f"ssd_cb{l}", 12)
        V.add(f"ssd_norm{l}", 8); V.add(f"pool_scale{l}", 4); V.add(f"q_norm{l}", 3)
        V.add(f"kv_norm{l}", 2); V.add(f"ffn_norm{l}", 8); V.add(f"ffn_cw{l}", 132)
        V.add(f"ffn_cb{l}", 44); V.add(f"dskip{l}", 8); V.add(f"dt_bias{l}", 1); V.add(f"a_log{l}", 16)
    V.add("final_norm", 8); V.add("pool_rc", 64); V.add("invf", 1); V.add("sgn", 1)
    return V


VM = _vmap()
NV = VM.n


def _colmajor(v):
    v = np.asarray(v, np.float32)
    return np.ascontiguousarray(v.reshape(-1, 128).T)


def _build_vecs(inp):
    vecs = np.zeros((128, NV), np.float32)

    def put(name, arr):
        arr = np.asarray(arr, np.float32)
        vecs[:arr.shape[0], VM(name):VM(name) + arr.shape[1]] = arr
    for l in range(DEPTH):
        put(f"attn_norm{l}", _colmajor(inp["attn_norm"][l]))
        cw = inp["ssd_conv_w"][l]
        put(f"ssd_cw{l}", np.concatenate([_colmajor(cw[k]) for k in range(4)], axis=1))
        put(f"ssd_cb{l}", _colmajor(inp["ssd_conv_b"][l]))
        put(f"ssd_norm{l}", _colmajor(inp["ssd_norm"][l]))
        put(f"pool_scale{l}", _colmajor(inp["pool_scale"][l]))
        put(f"q_norm{l}", _colmajor(inp["mla_q_norm"][l]))
        put(f"kv_norm{l}", _colmajor(inp["mla_kv_norm"][l]))
        put(f"ffn_norm{l}", _colmajor(inp["ffn_norm"][l]))
        fw = inp["ffn_conv_w"][l]
        put(f"ffn_cw{l}", np.concatenate([_colmajor(fw[k]) for k in range(3)], axis=1))
        put(f"ffn_cb{l}", _colmajor(inp["ffn_conv_b"][l]))
        put(f"dskip{l}", _colmajor(np.repeat(np.asarray(inp["ssd_d"][l], np.float32), 64)))
        put(f"dt_bias{l}", np.asarray(inp["ssd_dt_bias"][l], np.float32).reshape(16, 1))
        put(f"a_log{l}", np.broadcast_to(np.asarray(inp["ssd_a_log"][l], np.float32)[None, :], (128, 16)))
    put("final_norm", _colmajor(inp["final_norm"]))
    rc = np.zeros((128, 64), np.float32)
    for g, w in enumerate((2, 4, 8, 16)):
        rc[:, g * 16:(g + 1) * 16] = 1.0 / np.minimum(np.arange(1, 17), w)[None, :]
    put("pool_rc", rc)
    invf = np.zeros((128, 1), np.float32); sgn = np.zeros((128, 1), np.float32)
    fr = (10000.0 ** (-np.arange(0, 32, 2, dtype=np.float32) / 32)).astype(np.float32)
    invf[64:80, 0] = fr; invf[80:96, 0] = fr
    sgn[64:80, 0] = -1.0; sgn[80:96, 0] = 1.0
    put("invf", invf); put("sgn", sgn)
    return vecs


def _layout_w_in(w):
    w = np.asarray(w, np.float32)
    o = np.zeros((1024, NCH_IN * 128), np.float32)
    z, xbc, dt, u, cq, ckv, kpe = np.split(w, np.cumsum([1024, 1536, 16, 512, 384, 256])[:], axis=1)
    o[:, 0:1024] = z
    o[:, 1024:2560] = xbc
    o[:, CH_DT * 128:CH_DT * 128 + 16] = dt
    o[:, CH_U * 128:CH_U * 128 + 512] = u
    o[:, CH_CQ * 128:CH_CQ * 128 + 384] = cq
    o[:, CH_CKV * 128:CH_CKV * 128 + 256] = ckv
    o[:, CH_KPE * 128 + 64:CH_KPE * 128 + 96] = kpe
    o[:, CH_KPP * 128 + 64:CH_KPP * 128 + 96] = np.concatenate([kpe[:, 16:32], kpe[:, 0:16]], axis=1)
    return o


def build_program(dbg=None, phases=None, nlayers=DEPTH):
    dbg = dbg or ()
    nc = bass.Bass("TRN2", target_bir_lowering=False)
    es = contextlib.ExitStack()
    T = Trk(nc, es)

    def din(name, shape, dt=F32):
        return nc.dram_tensor(name, list(shape), dt, kind="ExternalInput").ap()

    def dscr(name, shape, dt):
        kind = "ExternalOutput" if name in dbg else "Internal"
        return nc.dram_tensor(name, list(shape), dt, kind=kind).ap()

    xT = din("xT", [D, NT]); pos = din("pos", [1, NT], I32); vecs_d = din("vecs", [128, NV])
    w_in_d = din("w_in", [DEPTH, 1024, NCH_IN * 128])
    pool_w_d = din("pool_w", [DEPTH, 4, 128, 128])
    w_uq_d = din("w_uq", [DEPTH, 384, 8, 96]); w_uqp_d = din("w_uqp", [DEPTH, 384, 8, 96])
    w_kn_d = din("w_kn", [DEPTH, 256, 512]); w_v_d = din("w_v", [DEPTH, 256, 512])
    w_out_d = din("w_out", [DEPTH, 2048, 1024])
    w_up_d = din("w_up", [DEPTH, 1024, 2 * DFF]); w_dn_d = din("w_dn", [DEPTH, DFF, 1024])
    out_d = nc.dram_tensor("out", [D, NT], F32, kind="ExternalOutput").ap()

    projT = dscr("projT", [NCH_IN * 128, NT], BF16)
    dtT = dscr("dtT", [16, NT], F32)
    ymixT = dscr("ymixT", [2048, NT], BF16)
    xres = dscr("xres", [D, NT], F32)
    h2T = dscr("h2T", [D, NT], BF16)
    ropeC = dscr("ropeC", [32, NT], F32); ropeS = dscr("ropeS", [32, NT], F32)

    PS = [nc.alloc_psum_tensor(f"ps{i}", [128, 512], F32) for i in range(8)]
    PB = [Buf(f"ps{i}") for i in range(8)]

    uid = [0]

    def sb(stack, name, shape, dt=F32):
        uid[0] += 1
        return stack.enter_context(nc.sbuf_tensor(f"s{uid[0]}_{name}", list(shape), dt))

    vecs = sb(es, "vecs", [128, NV]); Bvecs = Buf("vecs")
    ones_bf = sb(es, "ones_bf", [128, 128], BF16); ident_bf = sb(es, "ident_bf", [128, 128], BF16)
    ident_f = sb(es, "ident_f", [128, 128]); triu_bf = sb(es, "triu_bf", [128, 128], BF16)
    epsb = sb(es, "epsb", [128, 1]); Bconst = Buf("const")
    T.dma("sp", "vecs", [(vecs[:], vecs_d)], writes=[Bvecs])
    T.op("pool", lambda e: e.memset(ones_bf[:], 1.0), writes=[Bconst])
    T.op("pool", lambda e: e.memset(epsb[:], EPS), writes=[Bconst])
    for t_, cmp_ in ((ident_bf, ALU.is_equal), (ident_f, ALU.is_equal), (triu_bf, ALU.is_ge)):
        T.op("pool", lambda e: e.memset(t_[:], 1.0), writes=[Bconst])
        T.op("pool", lambda e: e.affine_select(out=t_[:], in_=t_[:], pattern=[[1, 128]], compare_op=cmp_,
                                                fill=0.0, base=0, channel_multiplier=-1),
             reads=[Bconst], writes=[Bconst])

    def vc(name, i=0, n=1, p0=0, p1=128):
        return vecs[p0:p1, VM(name, i):VM(name, i) + n]

    def run(ph):
        return phases is None or ph in phases

    if run("rope"):
        with contextlib.ExitStack() as ph:
            PW = 2048
            pi_ = sb(ph, "r_pi", [128, PW], I32); ang = sb(ph, "r_ang", [128, PW]); kf = sb(ph, "r_kf", [128, PW])
            ki = sb(ph, "r_ki", [128, PW], I32); rr = sb(ph, "r_rr", [128, PW]); mm = sb(ph, "r_mm", [128, PW])
            Bp, Ba, Bk, Br, Bm = Buf("pi"), Buf("ang"), Buf("kf"), Buf("rr"), Buf("mm")
            R = slice(64, 96)
            for pc in range(NT // PW):
                cs = slice(pc * PW, (pc + 1) * PW)
                T.dma("sp", "r_ld", [(pi_[R, :], pos[:, cs].partition_broadcast(32))], writes=[Bp])
                T.op("dve", lambda e: e.tensor_copy(out=ang[R, :], in_=pi_[R, :]), reads=[Bp], writes=[Ba])
                T.op("dve", lambda e: e.tensor_scalar(out=ang[R, :], in0=ang[R, :], scalar1=vc("invf", p0=64, p1=96),
                                                      scalar2=None, op0=ALU.mult), reads=[Ba, Bvecs], writes=[Ba])
                for which, dst in ((0, ropeS), (1, ropeC)):
                    src = ang
                    if which == 1:
                        T.op("dve", lambda e: e.tensor_scalar(out=rr[R, :], in0=ang[R, :], scalar1=float(np.pi / 2),
                                                              scalar2=None, op0=ALU.add), reads=[Ba], writes=[Br])
                        src = rr
                    T.op("dve", lambda e: e.tensor_scalar(out=kf[R, :], in0=src[R, :], scalar1=float(1 / TWO_PI),
                                                          scalar2=None, op0=ALU.mult), reads=[Ba, Br], writes=[Bk])
                    T.op("dve", lambda e: e.tensor_copy(out=ki[R, :], in_=kf[R, :]), reads=[Bk], writes=[Bk])
                    T.op("dve", lambda e: e.tensor_copy(out=kf[R, :], in_=ki[R, :]), reads=[Bk], writes=[Bk])
                    T.op("dve", lambda e: e.scalar_tensor_tensor(out=rr[R, :], in0=kf[R, :], scalar=-TWO_PI, in1=src[R, :],
                                                                 op0=ALU.mult, op1=ALU.add), reads=[Bk, Ba, Br], writes=[Br])
                    for (cmp_, thr, add) in ((ALU.is_gt, float(np.pi), -TWO_PI), (ALU.is_lt, float(-np.pi), TWO_PI)):
                        T.op("dve", lambda e: e.tensor_scalar(out=mm[R, :], in0=rr[R, :], scalar1=thr, scalar2=add,
                                                              op0=cmp_, op1=ALU.mult), reads=[Br], writes=[Bm])
                        T.op("dve", lambda e: e.tensor_tensor(out=rr[R, :], in0=rr[R, :], in1=mm[R, :], op=ALU.add),
                             reads=[Br, Bm], writes=[Br])
                    T.op("act", lambda e: e.activation(out=rr[R, :], in_=rr[R, :], func=AF.Sin), reads=[Br], writes=[Br])
                    if which == 0:
                        T.op("dve", lambda e: e.tensor_scalar(out=rr[R, :], in0=rr[R, :], scalar1=vc("sgn", p0=64, p1=96),
                                                              scalar2=None, op0=ALU.mult), reads=[Br, Bvecs], writes=[Br])
                    T.dma("sp", "r_st", [(dst[:, cs], rr[R, :])], reads=[Br])
            T.barrier()

    for l in range(nlayers):
        xsrc = xT if l == 0 else xres
        if run("p1"):
            with contextlib.ExitStack() as ph:
                win = sb(ph, "win", [128, 8, NCH_IN * 128], BF16); Bwin = Buf("win")
                T.dma("pool", "w0", [(win[:, kc, :], w_in_d[l, kc * 128:(kc + 1) * 128, :]) for kc in range(8)], writes=[Bwin])
                xt = [sb(ph, f"p1x{i}", [128, 8, TT]) for i in range(2)]; Bxt = [Buf("xt0"), Buf("xt1")]
                sq = sb(ph, "p1sq", [128, 8, TT], BF16); Bsq = Buf("sq")
                hbs = [sb(ph, f"p1h{i}", [128, 8, TT], BF16) for i in range(2)]; Bhs = [Buf("h0"), Buf("h1")]
                rstd = sb(ph, "p1rstd", [128, TT]); Brs = Buf("rstd")
                stg = [sb(ph, f"p1stg{i}", [128, NCH_IN, TT], BF16) for i in range(2)]
                Bstg = [[Buf(f"stg{i}_{a}") for a in range(4)] for i in range(2)]
                dts = [sb(ph, f"p1dt{i}", [16, TT]) for i in range(2)]; Bdts = [Buf("dts0"), Buf("dts1")]

                def load_x(j):
                    T.dma("sp", f"p1x{j % 2}", [(xt[j % 2][:], xsrc[:, j * TT:(j + 1) * TT].rearrange("(c p) t -> p c t", p=128))],
                          writes=[Bxt[j % 2]])

                def norm_part(j, part):
                    bb = j % 2; X = xt[bb]
                    if part == 0:
                        T.op("act", lambda e: e.activation(out=sq[:], in_=X[:], func=AF.Square), reads=[Bxt[bb]], writes=[Bsq])
                    elif part == 1:
                        for c in range(8):
                            T.op("pe", lambda e: e.matmul(PS[0][:], ones_bf[:], sq[:, c, :], start=(c == 0), stop=(c == 7)),
                                 reads=[Bsq, Bconst], writes=[PB[0]], signal=(c == 7))
                    elif part == 2:
                        T.op("act", lambda e: e.activation(out=rstd[:], in_=PS[0][:], func=AF.Ln, bias=epsb[:, 0:1], scale=1.0 / D),
                             reads=[PB[0], Bconst], writes=[Brs])
                        T.op("act", lambda e: e.activation(out=rstd[:], in_=rstd[:], func=AF.Exp, scale=-0.5), reads=[Brs], writes=[Brs])
                    else:
                        for c in range(8):
                            T.op("dve", lambda e: e.scalar_tensor_tensor(out=hbs[bb][:, c, :], in0=X[:, c, :], scalar=vc(f"attn_norm{l}", c),
                                                                         in1=rstd[:], op0=ALU.mult, op1=ALU.mult),
                                 reads=[Bxt[bb], Brs, Bvecs], writes=[Bhs[bb]], signal=(c == 7))
                load_x(0); load_x(1)
                for part in range(4):
                    norm_part(0, part)
                ev = 0
                for j in range(NTILE):
                    b = j % 2
                    hb = hbs[b]; Bh = Bhs[b]
                    if j >= 1 and j + 1 < NTILE:
                        load_x(j + 1)
                    for m in range(NCH_IN):
                        M = 16 if m == CH_DT else (96 if m >= CH_KPE else 128)
                        pb = 1 + (m % 6)
                        if j + 1 < NTILE and m in (6, 10, 14, 18):
                            norm_part(j + 1, (m - 6) // 4)
                        for kc in range(8):
                            T.op("pe", lambda e: e.matmul(PS[pb][0:M, :], win[:, kc, m * 128:m * 128 + M], hb[:, kc, :],
                                                          start=(kc == 0), stop=(kc == 7)),
                                 reads=[Bh, Bwin], writes=[PB[pb]], signal=(kc == 7))
                        if m == CH_DT:
                            T.op("dve", lambda e: e.tensor_copy(out=dts[b][:], in_=PS[pb][0:16, :]), reads=[PB[pb]], writes=[Bdts[b]])
                            continue
                        a = m // 8
                        if ev % 2 == 0:
                            T.op("act", lambda e: e.activation(out=stg[b][0:M, m, :], in_=PS[pb][0:M, :], func=AF.Copy),
                                 reads=[PB[pb]], writes=[Bstg[b][a]])
                        else:
                            T.op("dve", lambda e: e.tensor_copy(out=stg[b][0:M, m, :], in_=PS[pb][0:M, :]),
                                 reads=[PB[pb]], writes=[Bstg[b][a]])
                        ev += 1
                        if m % 8 == 7:
                            cs = slice(j * TT, (j + 1) * TT)
                            if a < 2:
                                T.dma("sp", f"p1s{b}{a}", [(projT[a * 1024:(a + 1) * 1024, cs].rearrange("(c p) t -> p c t", p=128),
                                                            stg[b][:, a * 8:(a + 1) * 8, :])], reads=[Bstg[b][a]])
                            elif a == 2:
                                T.dma("sp", f"p1s{b}{a}", [(projT[2048:2560, cs].rearrange("(c p) t -> p c t", p=128), stg[b][:, 16:20, :]),
                                                            (projT[2688:3072, cs].rearrange("(c p) t -> p c t", p=128), stg[b][:, 21:24, :])],
                                      reads=[Bstg[b][a]])
                            else:
                                T.dma("sp", f"p1s{b}{a}", [(projT[3072:3840, cs].rearrange("(c p) t -> p c t", p=128), stg[b][:, 24:30, :]),
                                                            (projT[CH_KPE * 128 + 64:CH_KPE * 128 + 96, cs], stg[b][64:96, CH_KPE, :]),
                                                            (projT[CH_KPP * 128 + 64:CH_KPP * 128 + 96, cs], stg[b][64:96, CH_KPP, :])],
                                      reads=[Bstg[b][a]])
                    T.dma("sp", f"p1d{b}", [(dtT[:, j * TT:(j + 1) * TT], dts[b][:])], reads=[Bdts[b]])
                T.barrier()


        if run("p2"):
            with contextlib.ExitStack() as ph:
                pw = sb(ph, "pw", [128, 4, 128], BF16); Bpw = Buf("pw")
                T.dma("pool", "w0", [(pw[:, g, :], pool_w_d[l, g]) for g in range(4)], writes=[Bpw])
                HL = 16
                ut = [sb(ph, f"p2u{i}", [128, 4, HL + TT], BF16) for i in range(2)]; But = [Buf("ut0"), Buf("ut1")]
                uf = sb(ph, "p2uf", [128, HL + TT]); sA = sb(ph, "p2a", [128, HL + TT]); sB = sb(ph, "p2b", [128, HL + TT])
                Buf_, BsA, BsB = Buf("uf"), Buf("sA"), Buf("sB")
                t16 = sb(ph, "p2t16", [128, 16]); Bt16 = Buf("t16")
                pl = [sb(ph, f"p2pl{i}", [128, TT], BF16) for i in range(2)]; Bpl = [Buf("pl0"), Buf("pl1")]
                stg = [sb(ph, f"p2s{i}", [128, 4, TT], BF16) for i in range(2)]; Bstg = [Buf("s0"), Buf("s1")]
                urows = projT[CH_U * 128:(CH_U + 4) * 128, :].rearrange("(c p) t -> p c t", p=128)

                def load_u(j):
                    b = j % 2; t0 = j * TT
                    if j % (S // TT) == 0:
                        T.op("pool", lambda e: e.memset(ut[b][:, :, 0:HL], 0.0), writes=[But[b]])
                        T.dma("sp", f"p2u{b}", [(ut[b][:, :, HL:], urows[:, :, t0:t0 + TT])], writes=[But[b]])
                    else:
                        T.dma("sp", f"p2u{b}", [(ut[b][:, :, :], urows[:, :, t0 - HL:t0 + TT])], writes=[But[b]])
                load_u(0)
                k = 0
                for j in range(NTILE):
                    b = j % 2
                    if j + 1 < NTILE:
                        load_u(j + 1)
                    first = (j % (S // TT) == 0)
                    for g in range(4):
                        w = 2 << g
                        T.op("act", lambda e: e.activation(out=uf[:], in_=ut[b][:, g, :], func=AF.Copy), reads=[But[b]], writes=[Buf_])
                        cur, Bcur = uf, Buf_
                        for s_ in range(g + 1):
                            sh = 1 << s_
                            nxt, Bn = (sA, BsA) if cur is not sA else (sB, BsB)
                            v0 = sh - 1
                            T.op("dve", lambda e: e.tensor_tensor(out=nxt[:, v0 + sh:], in0=cur[:, v0 + sh:], in1=cur[:, v0:HL + TT - sh], op=ALU.add),
                                 reads=[Bcur], writes=[Bn])
                            cur, Bcur = nxt, Bn
                        pb_ = k % 2; k += 1
                        T.op("dve", lambda e: e.scalar_tensor_tensor(out=pl[pb_][:], in0=cur[:, HL:], scalar=1.0 / w, in1=uf[:, HL:],
                                                                     op0=ALU.mult, op1=ALU.subtract), reads=[Bcur, Buf_], writes=[Bpl[pb_]])
                        if first:
                            T.op("dve", lambda e: e.tensor_tensor(out=t16[:], in0=cur[:, HL:HL + 16], in1=vc("pool_rc", g * 16, 16), op=ALU.mult),
                                 reads=[Bcur, Bvecs], writes=[Bt16])
                            T.op("dve", lambda e: e.tensor_tensor(out=pl[pb_][:, 0:16], in0=t16[:], in1=uf[:, HL:HL + 16], op=ALU.subtract),
                                 reads=[Bt16, Buf_], writes=[Bpl[pb_]])
                        pq = 1 + (k % 4)
                        T.op("pe", lambda e: e.matmul(PS[pq][:], pw[:, g, :], pl[pb_][:], start=True, stop=True),
                             reads=[Bpw, Bpl[pb_]], writes=[PB[pq]])
                        T.op("act", lambda e: e.activation(out=stg[b][:, g, :], in_=PS[pq][:], func=AF.Copy, scale=vc(f"pool_scale{l}", g)),
                             reads=[PB[pq], Bvecs], writes=[Bstg[b]])
                    T.dma("sp", f"p2s{b}", [(ymixT[1024:1536, j * TT:(j + 1) * TT].rearrange("(c p) t -> p c t", p=128), stg[b][:])],
                          reads=[Bstg[b]])
                T.barrier()

        if run("p3"):
            with contextlib.ExitStack() as ph:
                HL = 3; NC4 = TT // 128
                xin = [sb(ph, f"p3x{i}", [128, 12, HL + TT], BF16) for i in range(2)]; Bxin = [Buf("xin0"), Buf("xin1")]
                zin = [sb(ph, f"p3z{i}", [128, 8, TT], BF16) for i in range(2)]; Bzin = [Buf("z0"), Buf("z1")]
                dtr = [sb(ph, f"p3d{i}", [16, TT]) for i in range(2)]; Bdtr = [Buf("dtr0"), Buf("dtr1")]
                xc = sb(ph, "p3xc", [128, 12, TT], BF16); Bxc = [Buf(f"xc{c}") for c in range(12)]
                dg = sb(ph, "p3dg", [128, 48, 128], BF16); Bdg = Buf("dg")
                for kc_ in range(48):
                    T.op("pool", lambda e: e.tensor_scalar(out=dg[:, kc_, :], in0=ident_bf[:], scalar1=vc(f"ssd_cw{l}", kc_), scalar2=None, op0=ALU.mult),
                         reads=[Bconst, Bvecs], writes=[Bdg], signal=(kc_ == 47))
                dtT_s = sb(ph, "p3dtT", [16, TT]); BdtT = Buf("dtT")
                arow = sb(ph, "p3arow", [128, 16]); Barow = Buf("arow")
                dt_tok = sb(ph, "p3dttok", [128, NC4, 16]); da_bf = sb(ph, "p3dabf", [128, NC4, 16], BF16)
                cum_s = sb(ph, "p3cum", [128, NC4, 16]); dte = sb(ph, "p3dte", [128, NC4, 16]); dtw = sb(ph, "p3dtw", [128, NC4, 16])
                edec = sb(ph, "p3edec", [128, NC4, 16]); BsmA = Buf("smallA")
                rhsA = sb(ph, "p3rhsA", [128, NC4, 16, 128], BF16); BrhsA = [Buf(f"rhsA{c}") for c in range(NC4)]
                xdt = sb(ph, "p3xdt", [128, NC4, 1024], BF16); Bxdt = [Buf(f"xdt{c}") for c in range(NC4)]
                xw = sb(ph, "p3xw", [128, NC4, 1024], BF16); Bxw = [Buf(f"xw{c}") for c in range(NC4)]
                bst = sb(ph, "p3bst", [128, NC4, 256], BF16); Bbst = [Buf(f"bst{c}") for c in range(NC4)]
                NS3 = 3
                cbm = [sb(ph, f"p3cbm{i}", [128, 128], BF16) for i in range(NS3)]; Bcbm = [Buf(f"cbm{i}") for i in range(NS3)]
                seg = [sb(ph, f"p3seg{i}", [128, 8, 128]) for i in range(NS3)]; BsegA = [Buf(f"segA{i}") for i in range(NS3)]; BsegB = [Buf(f"segB{i}") for i in range(NS3)]
                Dm = [sb(ph, f"p3Dm{i}", [128, 8, 128], BF16) for i in range(NS3)]; BDm = [Buf(f"Dm{i}") for i in range(NS3)]
                Eb = [sb(ph, f"p3Eb{i}", [128, 8, 128], BF16) for i in range(NS3)]; BEb = [Buf(f"Eb{i}") for i in range(NS3)]
                Mm = [sb(ph, f"p3M{i}", [128, 8, 128], BF16) for i in range(NS3)]; BMm = [Buf(f"M{i}") for i in range(NS3)]
                CE = [sb(ph, f"p3CE{i}", [128, 8, 128], BF16) for i in range(NS3)]; BCE = [Buf(f"CE{i}") for i in range(NS3)]
                Hs = sb(ph, "p3H", [128, 1024]); BH = [Buf("H0"), Buf("H1")]
                Hbf = sb(ph, "p3Hbf", [128, NC4, 1024], BF16); BHbf = [[Buf(f"Hbf{c}_{g}") for g in range(2)] for c in range(NC4)]
                ysb = sb(ph, "p3ysb", [128, 8, TT]); Bysb = [Buf(f"ysb{f}") for f in range(8)]
                zsbs = [sb(ph, f"p3zsb{i}", [128, 8, TT], BF16) for i in range(2)]; Bzsbs = [[Buf(f"zsb{i}_{f}") for f in range(8)] for i in range(2)]
                sq = sb(ph, "p3sq", [128, 8, TT], BF16); Bsq = Buf("sq")
                rstd = sb(ph, "p3rstd", [128, TT]); Brs = Buf("rstd")
                yout = [sq] * 2; Byout = [Bsq] * 2
                T.op("act", lambda e: e.activation(out=arow[:], in_=vc(f"a_log{l}", 0, 16), func=AF.Exp), reads=[Bvecs], writes=[Barow])
                T.op("dve", lambda e: e.tensor_scalar(out=arow[:], in0=arow[:], scalar1=-1.0, scalar2=None, op0=ALU.mult), reads=[Barow], writes=[Barow])
                xrows = projT[CH_X * 128:(CH_X + 12) * 128, :].rearrange("(c p) t -> p c t", p=128)
                zrows = projT[0:1024, :].rearrange("(c p) t -> p c t", p=128)
                PSXb = PS[1][:].bitcast(BF16)
                PSBb = PS[2][:].bitcast(BF16)
                DTC, CUMC, TOTC, CBC = 128, 192, 256, 320

                def load3(j):
                    b = j % 2; t0 = j * TT
                    if j % (S // TT) == 0:
                        T.op("pool", lambda e: e.memset(xin[b][:, :, 0:HL], 0.0), writes=[Bxin[b]])
                        T.dma("sp", f"p3x{b}", [(xin[b][:, :, HL:], xrows[:, :, t0:t0 + TT])], writes=[Bxin[b]])
                    else:
                        T.dma("sp", f"p3x{b}", [(xin[b][:, :, :], xrows[:, :, t0 - HL:t0 + TT])], writes=[Bxin[b]])
                    T.dma("sp", f"p3z{b}", [(zin[b][:], zrows[:, :, t0:t0 + TT])], writes=[Bzin[b]])
                    T.dma("sp", f"p3d{b}", [(dtr[b][:], dtT[:, t0:t0 + TT])], writes=[Bdtr[b]])
                def stage0(j):
                    b = j % 2; zsb = zsbs[b]; Bzsb = Bzsbs[b]
                    for c in range(12):
                        pq = 3 + c % 4
                        for k_ in range(4):
                            T.op("pe", lambda e: e.matmul(PS[pq][:], dg[:, k_ * 12 + c, :], xin[b][:, c, k_:k_ + TT], start=(k_ == 0), stop=(k_ == 3)),
                                 reads=[Bdg, Bxin[b]], writes=[PB[pq]], signal=(k_ == 3))
                        T.op("act", lambda e: e.activation(out=xc[:, c, :], in_=PS[pq][:], func=AF.Silu, bias=vc(f"ssd_cb{l}", c)), reads=[PB[pq], Bvecs], writes=[Bxc[c]])
                    for f in range(8):
                        T.op("act", lambda e: e.activation(out=zsb[:, f, :], in_=zin[b][:, f, :], func=AF.Silu), reads=[Bzin[b]], writes=[Bzsb[f]])

                def dtpart(j):
                    b = j % 2
                    T.op("act", lambda e: e.activation(out=dtT_s[:], in_=dtr[b][:], func=AF.Exp, bias=vc(f"dt_bias{l}", p1=16), scale=1.0),
                         reads=[Bdtr[b], Bvecs], writes=[BdtT])
                    T.op("act", lambda e: e.activation(out=dtT_s[:], in_=dtT_s[:], func=AF.Ln, bias=1.0), reads=[BdtT], writes=[BdtT])
                    for c4 in range(NC4):
                        T.op("pe", lambda e: e.transpose(out=PS[2][:, DTC + 16 * c4:DTC + 16 * c4 + 16], in_=dtT_s[0:16, c4 * 128:(c4 + 1) * 128], identity=ident_f[0:16, 0:16]),
                             reads=[BdtT, Bconst], writes=[PB[2]], signal=(c4 == NC4 - 1))

                def epilogueA(j):
                    b = j % 2; zsb = zsbs[b]; Bzsb = Bzsbs[b]
                    for f in range(8):
                        T.op("dve", lambda e: e.tensor_tensor(out=ysb[:, f, :], in0=ysb[:, f, :], in1=zsb[:, f, :], op=ALU.mult), reads=[Bysb[f], Bzsb[f]], writes=[Bysb[f]])
                    T.op("act", lambda e: e.activation(out=sq[:], in_=ysb[:], func=AF.Square), reads=Bysb, writes=[Bsq])
                    for c in range(8):
                        T.op("pe", lambda e: e.matmul(PS[3][:], ones_bf[:], sq[:, c, :], start=(c == 0), stop=(c == 7)),
                             reads=[Bsq, Bconst], writes=[PB[3]], signal=(c == 7))
                    T.op("act", lambda e: e.activation(out=rstd[:], in_=PS[3][:], func=AF.Ln, bias=epsb[:, 0:1], scale=1.0 / 1024), reads=[PB[3], Bconst], writes=[Brs])
                    T.op("act", lambda e: e.activation(out=rstd[:], in_=rstd[:], func=AF.Exp, scale=-0.5), reads=[Brs], writes=[Brs])

                def epilogueB(j):
                    b = j % 2
                    for c in range(8):
                        T.op("dve", lambda e: e.scalar_tensor_tensor(out=yout[b][:, c, :], in0=ysb[:, c, :], scalar=vc(f"ssd_norm{l}", c), in1=rstd[:],
                                                                     op0=ALU.mult, op1=ALU.mult), reads=[Bysb[c], Brs, Bvecs], writes=[Byout[b]], signal=(c == 7))
                    T.dma("sp", f"p3s{b}", [(ymixT[0:1024, j * TT:(j + 1) * TT].rearrange("(c p) t -> p c t", p=128), yout[b][:])], reads=[Byout[b]])

                def stage1a():
                    F2 = lambda t: t[:].rearrange("p c h -> p (c h)")
                    T.op("dve", lambda e: e.tensor_copy(out=F2(dt_tok), in_=PS[2][:, DTC:DTC + 64]), reads=[PB[2]], writes=[BsmA])
                    T.op("dve", lambda e: e.tensor_tensor(out=da_bf[:], in0=dt_tok[:], in1=arow[:].unsqueeze(1).broadcast_to([128, NC4, 16]), op=ALU.mult), reads=[BsmA, Barow], writes=[BsmA])
                    T.op("pe", lambda e: e.matmul(PS[2][:, CUMC:CUMC + 64], triu_bf[:], F2(da_bf), start=True, stop=True), reads=[BsmA, Bconst], writes=[PB[2]], signal=False)
                    T.op("pe", lambda e: e.matmul(PS[2][:, TOTC:TOTC + 64], ones_bf[:], F2(da_bf), start=True, stop=True), reads=[BsmA, Bconst], writes=[PB[2]])
                    T.op("dve", lambda e: e.tensor_copy(out=F2(cum_s), in_=PS[2][:, CUMC:CUMC + 64]), reads=[PB[2]], writes=[BsmA])
                    T.op("dve", lambda e: e.tensor_tensor(out=F2(dte), in0=PS[2][:, TOTC:TOTC + 64], in1=F2(cum_s), op=ALU.subtract), reads=[PB[2], BsmA], writes=[BsmA])
                    T.op("act", lambda e: e.activation(out=F2(dte), in_=F2(dte), func=AF.Exp), reads=[BsmA], writes=[BsmA])
                    T.op("act", lambda e: e.activation(out=F2(edec), in_=PS[2][:, TOTC:TOTC + 64], func=AF.Exp), reads=[PB[2]], writes=[BsmA])
                    T.op("dve", lambda e: e.tensor_tensor(out=F2(dtw), in0=F2(dte), in1=F2(dt_tok), op=ALU.mult), reads=[BsmA], writes=[BsmA])
                    for c4 in range(NC4):
                        T.op("pool", lambda e: e.tensor_tensor(out=rhsA[:, c4, :, :], in0=da_bf[:, c4, :].unsqueeze(2).broadcast_to([128, 16, 128]),
                                                               in1=triu_bf[:].unsqueeze(1).broadcast_to([128, 16, 128]), op=ALU.mult),
                             reads=[BsmA, Bconst], writes=[BrhsA[c4]])

                load3(0)
                dtpart(0)
                stage1a()
                stage0(0)
                for j in range(NTILE):
                    b = j % 2
                    if j + 1 < NTILE:
                        load3(j + 1)
                    if j % (S // TT) == 0:
                        for g in range(2):
                            T.op("pool", lambda e: e.memset(Hs[:, g * 512:(g + 1) * 512], 0.0), writes=[BH[g]])
                            T.op("pool", lambda e: e.memset(Hbf[:, 0, g * 512:(g + 1) * 512], 0.0), writes=[BHbf[0][g]])
                    for c4 in range(NC4):
                        cc = slice(c4 * 128, (c4 + 1) * 128)
                        for f in range(8):
                            T.op("pe", lambda e: e.transpose(out=PSXb[:, f * 128:(f + 1) * 128], in_=xc[:, f, cc], identity=ident_bf[:]),
                                 reads=[Bxc[f], Bconst], writes=[PB[1]], signal=(f == 7))
                        for f in range(2):
                            T.op("pe", lambda e: e.transpose(out=PSBb[:, f * 128:(f + 1) * 128], in_=xc[:, 8 + f, cc], identity=ident_bf[:]),
                                 reads=[Bxc[8 + f], Bconst], writes=[PB[2]], signal=(f == 1))
                        T.op("act", lambda e: e.activation(out=bst[:, c4, :], in_=PSBb[:, 0:256], func=AF.Copy), reads=[PB[2]], writes=[Bbst[c4]])
                        T.op("dve", lambda e: e.tensor_tensor(out=xdt[:, c4, :].rearrange("p (h d) -> p h d", h=16), in0=PSXb.rearrange("p (h d) -> p h d", h=16),
                                                              in1=dt_tok[:, c4, :].unsqueeze(2).broadcast_to([128, 16, 64]), op=ALU.mult),
                             reads=[PB[1], BsmA], writes=[Bxdt[c4]])
                        T.op("dve", lambda e: e.tensor_tensor(out=xw[:, c4, :].rearrange("p (h d) -> p h d", h=16), in0=PSXb.rearrange("p (h d) -> p h d", h=16),
                                                              in1=dtw[:, c4, :].unsqueeze(2).broadcast_to([128, 16, 64]), op=ALU.mult),
                             reads=[PB[1], BsmA], writes=[Bxw[c4]])

                    def stage2(it):
                        c4, g = divmod(it, 2); cc = slice(c4 * 128, (c4 + 1) * 128)
                        pa = (3, 4) if it % 2 == 0 else (5, 6)
                        ip = it % NS3
                        for q in range(2):
                            T.op("pe", lambda e: e.matmul(PS[pa[q]][:], ones_bf[:], rhsA[:, c4, 8 * g + 4 * q:8 * g + 4 * q + 4, :].rearrange("p h l -> p (h l)"),
                                                          start=True, stop=True), reads=[BrhsA[c4], Bconst], writes=[PB[pa[q]]])
                        T.op("pe", lambda e: e.matmul(PS[2][:, CBC:CBC + 128], xc[:, 8 + g, cc], xc[:, 10 + g, cc], start=True, stop=True),
                             reads=[Bxc[8 + g], Bxc[10 + g]], writes=[PB[2]])
                        T.op("dve", lambda e: e.tensor_tensor(out=cbm[ip][:], in0=PS[2][:, CBC:CBC + 128], in1=triu_bf[:], op=ALU.mult),
                             reads=[PB[2], Bconst], writes=[Bcbm[ip]])
                        for hh in range(8):
                            h = 8 * g + hh
                            src_ = PS[pa[hh // 4]][:, (hh % 4) * 128:(hh % 4 + 1) * 128]
                            if hh < 4:
                                T.op("act", lambda e: e.activation(out=seg[ip][:, hh, :], in_=src_, func=AF.Relu, bias=cum_s[:, c4, h:h + 1], scale=-1.0),
                                     reads=[PB[pa[0]], BsmA], writes=[BsegA[ip]], signal=(hh == 3))
                            else:
                                T.op("dve", lambda e: e.tensor_scalar(out=seg[ip][:, hh, :], in0=src_, scalar1=cum_s[:, c4, h:h + 1], scalar2=0.0, op0=ALU.subtract, op1=ALU.min),
                                     reads=[PB[pa[1]], BsmA], writes=[BsegB[ip]], signal=(hh == 7))
                        T.op("act", lambda e: e.activation(out=Dm[ip][:, 0:4, :], in_=seg[ip][:, 0:4, :], func=AF.Exp, scale=-1.0), reads=[BsegA[ip]], writes=[BDm[ip]], signal=False)
                        T.op("act", lambda e: e.activation(out=Dm[ip][:, 4:8, :], in_=seg[ip][:, 4:8, :], func=AF.Exp), reads=[BsegB[ip]], writes=[BDm[ip]])
                        for q in range(2):
                            T.op("act", lambda e: e.activation(out=Eb[ip][:, 4 * q:4 * q + 4, :].rearrange("p h l -> p (h l)"), in_=PS[pa[q]][:], func=AF.Exp),
                                 reads=[PB[pa[q]]], writes=[BEb[ip]], signal=(q == 1))
                        T.op("pool", lambda e: e.tensor_tensor(out=Mm[ip][:], in0=Dm[ip][:], in1=cbm[ip][:].unsqueeze(1).broadcast_to([128, 8, 128]), op=ALU.mult),
                             reads=[BDm[ip], Bcbm[ip]], writes=[BMm[ip]])
                        T.op("pool", lambda e: e.tensor_tensor(out=CE[ip][:], in0=Eb[ip][:], in1=xc[:, 10 + g, cc].unsqueeze(1).broadcast_to([128, 8, 128]), op=ALU.mult),
                             reads=[BEb[ip], Bxc[10 + g]], writes=[BCE[ip]])

                    def stage34(it):
                        c4, g = divmod(it, 2); ip = it % NS3; cc = slice(c4 * 128, (c4 + 1) * 128)
                        py = 7 if it % 2 == 0 else 0
                        for hp in range(4):
                            for hx in range(2):
                                h = 8 * g + 2 * hp + hx
                                o_ = PS[py][hx * 64:(hx + 1) * 64, hp * 128:(hp + 1) * 128]
                                kw = {"tile_position": (0, 64)} if hx == 1 else {}
                                T.op("pe", lambda e: e.matmul(o_, xdt[:, c4, h * 64:(h + 1) * 64], Mm[ip][:, 2 * hp + hx, :], start=True, stop=False, **kw),
                                     reads=[Bxdt[c4], BMm[ip]], writes=[PB[py]], signal=False)
                                T.op("pe", lambda e: e.matmul(o_, Hbf[:, c4, h * 64:(h + 1) * 64], CE[ip][:, 2 * hp + hx, :], start=False, stop=True, **kw),
                                     reads=[BHbf[c4][g], BCE[ip]], writes=[PB[py]], signal=(hp == 3 and hx == 1))
                        T.op("pe", lambda e: e.matmul(PS[1][:], bst[:, c4, g * 128:(g + 1) * 128], xw[:, c4, g * 512:(g + 1) * 512], start=True, stop=True),
                             reads=[Bbst[c4], Bxw[c4]], writes=[PB[1]])
                        for hp in range(4):
                            f = 4 * g + hp
                            T.op("dve", lambda e: e.scalar_tensor_tensor(out=ysb[:, f, cc], in0=xc[:, f, cc], scalar=vc(f"dskip{l}", f),
                                                                         in1=PS[py][:, hp * 128:(hp + 1) * 128], op0=ALU.mult, op1=ALU.add),
                                 reads=[Bxc[f], PB[py], Bvecs], writes=[Bysb[f]])
                        Hg = Hs[:, g * 512:(g + 1) * 512]
                        T.op("dve", lambda e: e.tensor_tensor(out=Hg.rearrange("p (h d) -> p h d", h=8), in0=Hg.rearrange("p (h d) -> p h d", h=8),
                                                              in1=edec[:, c4, 8 * g:8 * g + 8].unsqueeze(2).broadcast_to([128, 8, 64]), op=ALU.mult),
                             reads=[BH[g], BsmA], writes=[BH[g]])
                        T.op("dve", lambda e: e.tensor_tensor(out=Hg, in0=Hg, in1=PS[1][:], op=ALU.add), reads=[BH[g], PB[1]], writes=[BH[g]])
                        nx = (c4 + 1) % NC4
                        T.op("act", lambda e: e.activation(out=Hbf[:, nx, g * 512:(g + 1) * 512], in_=Hg, func=AF.Copy), reads=[BH[g]], writes=[BHbf[nx][g]])
                    NIT = 2 * NC4
                    for it in range(NIT + 2):
                        if it < NIT:
                            stage2(it)
                        if it >= 2:
                            stage34(it - 2)
                    if j + 1 < NTILE:
                        dtpart(j + 1)
                    epilogueA(j)
                    if j + 1 < NTILE:
                        stage1a()
                        stage0(j + 1)
                    epilogueB(j)
                T.barrier()

        if run("p4"):
            with contextlib.ExitStack() as ph:
                SC = 1.0 / float(np.sqrt(96.0))
                wuq = sb(ph, "wuq", [128, 3, 8, 96], BF16); wuqp = sb(ph, "wuqp", [128, 3, 8, 96], BF16)
                wkn = sb(ph, "wkn", [128, 2, 512], BF16); wv = sb(ph, "wv", [128, 2, 512], BF16); Bw4 = Buf("w4")
                T.dma("pool", "w0", [(wuq[:, c, :, :], w_uq_d[l, c * 128:(c + 1) * 128]) for c in range(3)]
                      + [(wuqp[:, c, :, :], w_uqp_d[l, c * 128:(c + 1) * 128]) for c in range(3)]
                      + [(wkn[:, c, :], w_kn_d[l, c * 128:(c + 1) * 128, :]) for c in range(2)]
                      + [(wv[:, c, :], w_v_d[l, c * 128:(c + 1) * 128, :]) for c in range(2)], writes=[Bw4])
                kT = sb(ph, "kT", [128, 8, S], BF16); BkT = Buf("kT")
                Va = sb(ph, "Va", [128, S // 128, 8, 128], BF16); BVa = Buf("Va")
                T.op("pool", lambda e: e.memset(Va[:].rearrange("p a h d -> p (a h d)"), 1.0), writes=[BVa])
                cq = [sb(ph, f"cq{i}", [128, 3, TT], BF16) for i in range(2)]; Bcq = [Buf("cq0"), Buf("cq1")]
                ckv = [sb(ph, f"ckv{i}", [128, 2, TT], BF16) for i in range(2)]; Bckv = [Buf("ckv0"), Buf("ckv1")]
                kpe = [sb(ph, f"kpe{i}", [128, 2, TT], BF16) for i in range(2)]; Bkpe = [Buf("kpe0"), Buf("kpe1")]
                rcs = [sb(ph, f"rcs{i}", [128, 2, TT]) for i in range(2)]; Brcs = [Buf("rcs0"), Buf("rcs1")]
                sq = sb(ph, "p4sq", [128, 3, TT], BF16); Bsq = Buf("sq")
                rstd = sb(ph, "p4rstd", [128, TT]); Brs = Buf("rstd")
                cqn = sb(ph, "cqn", [128, 3, TT], BF16); Bcqn = Buf("cqn")
                ckvn = sb(ph, "ckvn", [128, 2, TT], BF16); Bckvn = Buf("ckvn")
                t1 = sb(ph, "p4t1", [128, TT]); t2 = sb(ph, "p4t2", [128, TT]); Bt1 = Buf("t1"); Bt2 = Buf("t2")
                qT = sb(ph, "qT", [128, 8, TT], BF16); BqT = [Buf(f"qT{h}") for h in range(8)]
                NPT = 8
                Pt = [sb(ph, f"Pt{i}", [128, TT], BF16) for i in range(NPT)]; BPt = [Buf(f"Pt{i}") for i in range(NPT)]
                Rr = t2; BRr = Bt2
                rb = sb(ph, "rb", [64, TT]); Brb = Buf("rb")
                yo = [sb(ph, "yo", [64, 8, TT], BF16)] * 2; Byo = [Buf("yo")] * 2
                R = slice(64, 96)
                cqrows = projT[CH_CQ * 128:(CH_CQ + 3) * 128, :].rearrange("(c p) t -> p c t", p=128)
                ckvrows = projT[CH_CKV * 128:(CH_CKV + 2) * 128, :].rearrange("(c p) t -> p c t", p=128)

                def load4(j):
                    b = j % 2; cs = slice(j * TT, (j + 1) * TT)
                    T.dma("sp", f"p4a{b}", [(cq[b][:], cqrows[:, :, cs])], writes=[Bcq[b]])
                    T.dma("sp", f"p4b{b}", [(ckv[b][:], ckvrows[:, :, cs])], writes=[Bckv[b]])
                    T.dma("sp", f"p4c{b}", [(kpe[b][R, 0, :], projT[CH_KPE * 128 + 64:CH_KPE * 128 + 96, cs]),
                                            (kpe[b][R, 1, :], projT[CH_KPP * 128 + 64:CH_KPP * 128 + 96, cs])], writes=[Bkpe[b]])
                    T.dma("sp", f"p4d{b}", [(rcs[b][R, 0, :], ropeC[:, cs]), (rcs[b][R, 1, :], ropeS[:, cs])], writes=[Brcs[b]])

                def rms(src, Bsrc, nchk, dst, Bdst, wname):
                    T.op("act", lambda e: e.activation(out=sq[:, 0:nchk, :], in_=src[:], func=AF.Square), reads=[Bsrc], writes=[Bsq])
                    for c in range(nchk):
                        T.op("pe", lambda e: e.matmul(PS[0][:], ones_bf[:], sq[:, c, :], start=(c == 0), stop=(c == nchk - 1)),
                             reads=[Bsq, Bconst], writes=[PB[0]], signal=(c == nchk - 1))
                    T.op("act", lambda e: e.activation(out=rstd[:], in_=PS[0][:], func=AF.Ln, bias=epsb[:, 0:1], scale=1.0 / (128 * nchk)),
                         reads=[PB[0], Bconst], writes=[Brs])
                    T.op("act", lambda e: e.activation(out=rstd[:], in_=rstd[:], func=AF.Exp, scale=-0.5), reads=[Brs], writes=[Brs])
                    for c in range(nchk):
                        T.op("dve", lambda e: e.scalar_tensor_tensor(out=dst[:, c, :], in0=src[:, c, :], scalar=vc(wname, c), in1=rstd[:],
                                                                     op0=ALU.mult, op1=ALU.mult), reads=[Bsrc, Brs, Bvecs], writes=[Bdst], signal=(c == nchk - 1))
                load4(0)
                pk = 0; rot = 0
                for j in range(NTILE):
                    b = j % 2; jj = j % (S // TT); cl = slice(jj * TT, (jj + 1) * TT)
                    if j + 1 < NTILE:
                        load4(j + 1)
                    rms(cq[b], Bcq[b], 3, cqn, Bcqn, f"q_norm{l}")
                    rms(ckv[b], Bckv[b], 2, ckvn, Bckvn, f"kv_norm{l}")
                    T.op("dve", lambda e: e.tensor_tensor(out=t1[R, :], in0=kpe[b][R, 0, :], in1=rcs[b][R, 0, :], op=ALU.mult), reads=[Bkpe[b], Brcs[b]], writes=[Bt1])
                    T.op("dve", lambda e: e.tensor_tensor(out=t2[R, :], in0=kpe[b][R, 1, :], in1=rcs[b][R, 1, :], op=ALU.mult), reads=[Bkpe[b], Brcs[b]], writes=[Bt2])
                    T.op("dve", lambda e: e.tensor_tensor(out=t1[R, :], in0=t1[R, :], in1=t2[R, :], op=ALU.add), reads=[Bt1, Bt2], writes=[Bt1])
                    T.op("act", lambda e: e.activation(out=kT[R, :, cl], in_=t1[R, :].unsqueeze(1).broadcast_to([32, 8, TT]), func=AF.Copy), reads=[Bt1], writes=[BkT])
                    for h in range(8):
                        pq = 1 + rot % 4; rot += 1
                        for c in range(2):
                            T.op("pe", lambda e: e.matmul(PS[pq][0:64, :], wkn[:, c, h * 64:(h + 1) * 64], ckvn[:, c, :], start=(c == 0), stop=(c == 1)),
                                 reads=[Bw4, Bckvn], writes=[PB[pq]], signal=(c == 1))
                        T.op("act", lambda e: e.activation(out=kT[0:64, h, cl], in_=PS[pq][0:64, :], func=AF.Copy), reads=[PB[pq]], writes=[BkT])
                    for blk in range(4):
                        pq = 1 + rot % 4; rot += 1
                        for c in range(2):
                            T.op("pe", lambda e: e.matmul(PS[pq][:], ckvn[:, c, blk * 128:(blk + 1) * 128], wv[:, c, :], start=(c == 0), stop=(c == 1)),
                                 reads=[Bw4, Bckvn], writes=[PB[pq]], signal=(c == 1))
                        en = "act"
                        if en == "act":
                            T.op("act", lambda e: e.activation(out=Va[:, jj * 4 + blk, :, 0:64], in_=PS[pq][:].rearrange("p (h d) -> p h d", h=8), func=AF.Copy),
                                 reads=[PB[pq]], writes=[BVa])
                        else:
                            T.op("dve", lambda e: e.tensor_copy(out=Va[:, jj * 4 + blk, :, 0:64], in_=PS[pq][:].rearrange("p (h d) -> p h d", h=8)),
                                 reads=[PB[pq]], writes=[BVa])
                    for h in range(8):
                        pq = 1 + rot % 4; rot += 1
                        pq2 = 1 + rot % 4; rot += 1
                        for c in range(3):
                            T.op("pe", lambda e: e.matmul(PS[pq][0:96, :], wuq[:, c, h, :], cqn[:, c, :], start=(c == 0), stop=(c == 2)),
                                 reads=[Bw4, Bcqn], writes=[PB[pq]], signal=(c == 2))
                        for c in range(3):
                            T.op("pe", lambda e: e.matmul(PS[pq2][0:96, :], wuqp[:, c, h, :], cqn[:, c, :], start=(c == 0), stop=(c == 2)),
                                 reads=[Bw4, Bcqn], writes=[PB[pq2]], signal=(c == 2))
                        T.op("act", lambda e: e.activation(out=qT[0:64, h, :], in_=PS[pq][0:64, :], func=AF.Copy), reads=[PB[pq]], writes=[BqT[h]])
                        T.op("dve", lambda e: e.tensor_tensor(out=t1[R, :], in0=PS[pq][R, :], in1=rcs[b][R, 0, :], op=ALU.mult), reads=[PB[pq], Brcs[b]], writes=[Bt1])
                        T.op("dve", lambda e: e.tensor_tensor(out=t2[R, :], in0=PS[pq2][R, :], in1=rcs[b][R, 1, :], op=ALU.mult), reads=[PB[pq2], Brcs[b]], writes=[Bt2])
                        T.op("dve", lambda e: e.tensor_tensor(out=qT[R, h, :], in0=t1[R, :], in1=t2[R, :], op=ALU.add), reads=[Bt1, Bt2], writes=[BqT[h]])
                    nkb = 4 * (jj + 1)
                    steps = [(h, kb) for h in range(8) for kb in range(nkb)]
                    LA = 6
                    info = {}
                    deferred = []

                    def emit_score(i):
                        nonlocal rot, pk
                        h, kb = steps[i]
                        off = 0 if kb < 4 * jj else (kb - 4 * jj) * 128
                        pq = 1 + rot % 4; rot += 1
                        pp = pk % NPT; pk += 1
                        info[i] = (off, pp)
                        T.op("pe", lambda e: e.matmul(PS[pq][:, off:TT], kT[0:96, h, kb * 128:(kb + 1) * 128], qT[0:96, h, off:TT], start=True, stop=True),
                             reads=[BkT, BqT[h]], writes=[PB[pq]])
                        T.op("act", lambda e: e.activation(out=Pt[pp][:, off:TT], in_=PS[pq][:, off:TT], func=AF.Exp, scale=SC), reads=[PB[pq]], writes=[BPt[pp]])
                        if kb >= 4 * jj:
                            T.op("pool", lambda e: e.tensor_tensor(out=Pt[pp][:, off:off + 128], in0=Pt[pp][:, off:off + 128], in1=triu_bf[:], op=ALU.mult),
                                 reads=[BPt[pp], Bconst], writes=[BPt[pp]])

                    def fin2(h):
                        po = 5 + h % 3
                        T.op("pe", lambda e: e.matmul(PS[0][0:64, :], ident_f[64:128, 64:128], Rr[64:128, :], start=True, stop=True),
                             reads=[BRr, Bconst], writes=[PB[0]])
                        T.op("dve", lambda e: e.tensor_copy(out=rb[:], in_=PS[0][0:64, :]), reads=[PB[0]], writes=[Brb])
                        T.op("dve", lambda e: e.tensor_tensor(out=yo[b][:, h, :], in0=PS[po][0:64, :], in1=rb[:], op=ALU.mult), reads=[PB[po], Brb], writes=[Byo[b]])

                    def emit_pv(i):
                        h, kb = steps[i]; off, pp = info.pop(i); po = 5 + h % 3
                        T.op("pe", lambda e: e.matmul(PS[po][:, off:TT], Va[:, kb, h, :], Pt[pp][:, off:TT], start=(kb == 0), stop=(kb == nkb - 1)),
                             reads=[BVa, BPt[pp]], writes=[PB[po]], signal=(kb == nkb - 1))
                        if kb == nkb - 1:
                            T.op("dve", lambda e: e.reciprocal(out=Rr[64:128, :], in_=PS[po][64:128, :]), reads=[PB[po]], writes=[BRr])
                            deferred.append((i + min(9, nkb - 1), h))
                    for i in range(len(steps) + LA + 12):
                        if i < len(steps):
                            emit_score(i)
                        if 0 <= i - LA < len(steps):
                            emit_pv(i - LA)
                        while deferred and deferred[0][0] <= i - LA:
                            fin2(deferred.pop(0)[1])
                    assert not deferred
                    T.dma("sp", f"p4s{b}", [(ymixT[1536:2048, j * TT:(j + 1) * TT].rearrange("(h p) t -> p h t", p=64), yo[b][:])], reads=[Byo[b]])
                T.barrier()

        if run("p5"):
            with contextlib.ExitStack() as ph:
                wo = sb(ph, "wo", [128, 16, 1024], BF16); Bwo = Buf("wo")
                T.dma("pool", "w0", [(wo[:, kc, :], w_out_d[l, kc * 128:(kc + 1) * 128, :]) for kc in range(16)], writes=[Bwo])
                ym = [sb(ph, f"p5y{i}", [128, 16, TT], BF16) for i in range(2)]; Bym = [Buf("ym0"), Buf("ym1")]
                xt = [sb(ph, f"p5x{i}", [128, 8, TT]) for i in range(2)]; Bxt = [Buf("xt0"), Buf("xt1")]
                x1 = [sb(ph, f"p5o{i}", [128, 8, TT]) for i in range(2)]; Bx1 = [Buf("x10"), Buf("x11")]
                sq = sb(ph, "p5sq", [128, 8, TT], BF16); Bsq = Buf("sq")
                rstd = sb(ph, "p5rstd", [128, TT]); Brs = Buf("rstd")
                h2 = [sb(ph, f"p5h{i}", [128, 8, TT], BF16) for i in range(2)]; Bh2 = [Buf("h20"), Buf("h21")]

                def load5(j):
                    b = j % 2; cs = slice(j * TT, (j + 1) * TT)
                    T.dma("sp", f"p5y{b}", [(ym[b][:], ymixT[:, cs].rearrange("(c p) t -> p c t", p=128))], writes=[Bym[b]])
                    T.dma("sp", f"p5x{b}", [(xt[b][:], xsrc[:, cs].rearrange("(c p) t -> p c t", p=128))], writes=[Bxt[b]])
                load5(0)
                for j in range(NTILE):
                    b = j % 2; cs = slice(j * TT, (j + 1) * TT)
                    if j + 1 < NTILE:
                        load5(j + 1)
                    for oc in range(8):
                        pq = 1 + oc % 6
                        for kc in range(16):
                            T.op("pe", lambda e: e.matmul(PS[pq][:], wo[:, kc, oc * 128:(oc + 1) * 128], ym[b][:, kc, :], start=(kc == 0), stop=(kc == 15)),
                                 reads=[Bwo, Bym[b]], writes=[PB[pq]], signal=(kc == 15))
                        T.op("dve", lambda e: e.tensor_tensor(out=x1[b][:, oc, :], in0=PS[pq][:], in1=xt[b][:, oc, :], op=ALU.add),
                             reads=[PB[pq], Bxt[b]], writes=[Bx1[b]])
                    T.dma("sp", f"p5s{b}", [(xres[:, cs].rearrange("(c p) t -> p c t", p=128), x1[b][:])], reads=[Bx1[b]])
                    T.op("act", lambda e: e.activation(out=sq[:], in_=x1[b][:], func=AF.Square), reads=[Bx1[b]], writes=[Bsq])
                    for c in range(8):
                        T.op("pe", lambda e: e.matmul(PS[0][:], ones_bf[:], sq[:, c, :], start=(c == 0), stop=(c == 7)),
                             reads=[Bsq, Bconst], writes=[PB[0]], signal=(c == 7))
                    T.op("act", lambda e: e.activation(out=rstd[:], in_=PS[0][:], func=AF.Ln, bias=epsb[:, 0:1], scale=1.0 / D), reads=[PB[0], Bconst], writes=[Brs])
                    T.op("act", lambda e: e.activation(out=rstd[:], in_=rstd[:], func=AF.Exp, scale=-0.5), reads=[Brs], writes=[Brs])
                    for c in range(8):
                        T.op("dve", lambda e: e.scalar_tensor_tensor(out=h2[b][:, c, :], in0=x1[b][:, c, :], scalar=vc(f"ffn_norm{l}", c), in1=rstd[:],
                                                                     op0=ALU.mult, op1=ALU.mult), reads=[Bx1[b], Brs, Bvecs], writes=[Bh2[b]])
                    T.dma("sp", f"p5t{b}", [(h2T[:, cs].rearrange("(c p) t -> p c t", p=128), h2[b][:])], reads=[Bh2[b]])
                T.barrier()

        for hf in range(2):
            if not run("p6"):
                continue
            with contextlib.ExitStack() as ph:
                NH = 11; W = NH * 128
                wu = sb(ph, "wu", [128, 8, 2 * W], BF16); wd = sb(ph, "wd", [128, NH, 1024], BF16); Bwu = Buf("wu"); Bwd = Buf("wd")
                T.dma("pool", "w0", [(wu[:, kc, 0:W], w_up_d[l, kc * 128:(kc + 1) * 128, hf * W:(hf + 1) * W]) for kc in range(8)]
                      + [(wu[:, kc, W:2 * W], w_up_d[l, kc * 128:(kc + 1) * 128, DFF + hf * W:DFF + (hf + 1) * W]) for kc in range(8)], writes=[Bwu])
                T.dma("pool", "w1", [(wd[:, i, :], w_dn_d[l, (hf * NH + i) * 128:(hf * NH + i + 1) * 128, :]) for i in range(NH)], writes=[Bwd])
                hb = [sb(ph, f"p6h{i}", [128, 8, TT], BF16) for i in range(2)]; Bhb = [Buf("hb0"), Buf("hb1")]
                xin = [sb(ph, f"p6x{i}", [128, 8, TT]) for i in range(2)]; Bxin = [Buf("xin0"), Buf("xin1")]
                xo = [sb(ph, f"p6o{i}", [128, 8, TT]) for i in range(2)]; Bxo = [Buf("xo0"), Buf("xo1")]
                ag = [sb(ph, f"p6ag{i}", [128, TT]) for i in range(2)]; Bag = [Buf("ag0"), Buf("ag1")]
                av = [sb(ph, f"p6av{i}", [128, TT]) for i in range(2)]; Bav = [Buf("av0"), Buf("av1")]
                gg = sb(ph, "p6g", [128, NH, TT], BF16); Bgg = [Buf(f"g{i}") for i in range(NH)]
                final = (l == nlayers - 1 and hf == 1)
                if final:
                    sq = sb(ph, "p6sq", [128, 8, TT], BF16); Bsq = Buf("sq")
                    rstd = sb(ph, "p6rstd", [128, TT]); Brs = Buf("rstd")
                tiles = []
                for s_ in range(NSEQ):
                    t0 = 0
                    while t0 < S:
                        ln = min(TT - 2, S - t0); tiles.append((s_, t0, ln)); t0 += ln
                dst = out_d if final else xres

                def load6(i):
                    s_, t0, ln = tiles[i]; b = i % 2; g0 = s_ * S + t0
                    if t0 == 0:
                        T.op("pool", lambda e: e.memset(hb[b][:, :, 0:2], 0.0), writes=[Bhb[b]])
                        T.dma("sp", f"p6h{b}", [(hb[b][:, :, 2:2 + ln], h2T[:, g0:g0 + ln].rearrange("(c p) t -> p c t", p=128))], writes=[Bhb[b]])
                    else:
                        T.dma("sp", f"p6h{b}", [(hb[b][:, :, 0:2 + ln], h2T[:, g0 - 2:g0 + ln].rearrange("(c p) t -> p c t", p=128))], writes=[Bhb[b]])
                    T.dma("sp", f"p6x{b}", [(xin[b][:, :, 0:ln], xres[:, g0:g0 + ln].rearrange("(c p) t -> p c t", p=128))], writes=[Bxin[b]])
                load6(0)
                rot = 0
                for i, (s_, t0, ln) in enumerate(tiles):
                    b = i % 2; N = ln + 2; g0 = s_ * S + t0
                    if i + 1 < len(tiles):
                        load6(i + 1)
                    for ci in range(NH):
                        pg = 1 + rot % 6; rot += 1
                        pv = 1 + rot % 6; rot += 1
                        for (pq, co) in ((pg, ci * 128), (pv, W + ci * 128)):
                            for kc in range(8):
                                T.op("pe", lambda e: e.matmul(PS[pq][:, 0:N], wu[:, kc, co:co + 128], hb[b][:, kc, 0:N], start=(kc == 0), stop=(kc == 7)),
                                     reads=[Bwu, Bhb[b]], writes=[PB[pq]], signal=(kc == 7))
                        cg = hf * NH + ci; cv = 22 + hf * NH + ci
                        for (pq, cidx, a_, Ba) in ((pg, cg, ag[ci % 2], Bag[ci % 2]), (pv, cv, av[ci % 2], Bav[ci % 2])):
                            T.op("act", lambda e: e.activation(out=a_[:, 0:ln], in_=PS[pq][:, 2:N], func=AF.Identity, bias=vc(f"ffn_cb{l}", cidx),
                                                               scale=vc(f"ffn_cw{l}", 2 * 44 + cidx)), reads=[PB[pq], Bvecs], writes=[Ba])
                            T.op("dve", lambda e: e.scalar_tensor_tensor(out=a_[:, 0:ln], in0=PS[pq][:, 1:N - 1], scalar=vc(f"ffn_cw{l}", 1 * 44 + cidx), in1=a_[:, 0:ln],
                                                                         op0=ALU.mult, op1=ALU.add), reads=[PB[pq], Bvecs, Ba], writes=[Ba])
                            T.op("dve", lambda e: e.scalar_tensor_tensor(out=a_[:, 0:ln], in0=PS[pq][:, 0:N - 2], scalar=vc(f"ffn_cw{l}", 0 * 44 + cidx), in1=a_[:, 0:ln],
                                                                         op0=ALU.mult, op1=ALU.add), reads=[PB[pq], Bvecs, Ba], writes=[Ba])
                        T.op("act", lambda e: e.activation(out=ag[ci % 2][:, 0:ln], in_=ag[ci % 2][:, 0:ln], func=AF.Silu), reads=[Bag[ci % 2]], writes=[Bag[ci % 2]])
                        T.op("pool", lambda e: e.tensor_tensor(out=gg[:, ci, 0:ln], in0=ag[ci % 2][:, 0:ln], in1=av[ci % 2][:, 0:ln], op=ALU.mult),
                             reads=[Bag[ci % 2], Bav[ci % 2]], writes=[Bgg[ci]])
                    for oc in range(8):
                        pq = 7 if oc % 2 == 0 else 0
                        for ci in range(NH):
                            T.op("pe", lambda e: e.matmul(PS[pq][:, 0:ln], wd[:, ci, oc * 128:(oc + 1) * 128], gg[:, ci, 0:ln], start=(ci == 0), stop=(ci == NH - 1)),
                                 reads=[Bwd, Bgg[ci]], writes=[PB[pq]], signal=(ci == NH - 1))
                        T.op("dve", lambda e: e.tensor_tensor(out=xo[b][:, oc, 0:ln], in0=PS[pq][:, 0:ln], in1=xin[b][:, oc, 0:ln], op=ALU.add),
                             reads=[PB[pq], Bxin[b]], writes=[Bxo[b]])
                    if final:
                        T.op("act", lambda e: e.activation(out=sq[:, :, 0:ln], in_=xo[b][:, :, 0:ln], func=AF.Square), reads=[Bxo[b]], writes=[Bsq])
                        for c in range(8):
                            T.op("pe", lambda e: e.matmul(PS[1][:, 0:ln], ones_bf[:], sq[:, c, 0:ln], start=(c == 0), stop=(c == 7)),
                                 reads=[Bsq, Bconst], writes=[PB[1]], signal=(c == 7))
                        T.op("act", lambda e: e.activation(out=rstd[:, 0:ln], in_=PS[1][:, 0:ln], func=AF.Ln, bias=epsb[:, 0:1], scale=1.0 / D), reads=[PB[1], Bconst], writes=[Brs])
                        T.op("act", lambda e: e.activation(out=rstd[:, 0:ln], in_=rstd[:, 0:ln], func=AF.Exp, scale=-0.5), reads=[Brs], writes=[Brs])
                        for c in range(8):
                            T.op("dve", lambda e: e.scalar_tensor_tensor(out=xo[b][:, c, 0:ln], in0=xo[b][:, c, 0:ln], scalar=vc("final_norm", c), in1=rstd[:, 0:ln],
                                                                         op0=ALU.mult, op1=ALU.mult), reads=[Brs, Bvecs, Bxo[b]], writes=[Bxo[b]])
                    T.dma("sp", f"p6s{b}", [(dst[:, g0:g0 + ln].rearrange("(c p) t -> p c t", p=128), xo[b][:, :, 0:ln])], reads=[Bxo[b]])
                T.barrier()

    T.barrier()
    return nc, es


def _prep_inputs(inp):
    x = np.asarray(inp["x"], np.float32)
    posi = np.asarray(inp["positions"], np.int32)
    shared = {
        "vecs": _build_vecs(inp),
        "w_in": np.stack([_layout_w_in(inp["w_in"][l]) for l in range(DEPTH)]),
        "pool_w": np.ascontiguousarray(np.asarray(inp["pool_w"], np.float32)),
    }
    uq = np.asarray(inp["mla_w_uq"], np.float32).reshape(DEPTH, 384, 8, 96)
    uqp = uq.copy()
    uqp[..., 64:80] = uq[..., 80:96]; uqp[..., 80:96] = uq[..., 64:80]
    ukv = np.asarray(inp["mla_w_ukv"], np.float32).reshape(DEPTH, 256, 8, 128)
    shared["w_uq"] = np.ascontiguousarray(uq); shared["w_uqp"] = np.ascontiguousarray(uqp)
    shared["w_kn"] = np.ascontiguousarray(ukv[..., :64].reshape(DEPTH, 256, 512))
    shared["w_v"] = np.ascontiguousarray(ukv[..., 64:].reshape(DEPTH, 256, 512))
    shared["w_out"] = np.ascontiguousarray(np.asarray(inp["w_out"], np.float32))
    shared["w_up"] = np.ascontiguousarray(np.asarray(inp["ffn_w_up"], np.float32))
    shared["w_dn"] = np.ascontiguousarray(np.asarray(inp["ffn_w_down"], np.float32))
    in_maps = []
    for c in range(NCORE):
        m = dict(shared)
        m["xT"] = np.ascontiguousarray(np.concatenate([x[2 * c].T, x[2 * c + 1].T], axis=1))
        m["pos"] = np.ascontiguousarray(posi[2 * c:2 * c + 2].reshape(1, NT))
        in_maps.append(m)
    return in_maps


def kernel(**inp):
    in_maps = _prep_inputs(inp)
    nc, es = build_program()
    res = run_bass_kernel_spmd(nc, in_maps, core_ids=list(range(NCORE)))
    out = np.empty((2 * NCORE, S, D), np.float32)
    for c in range(NCORE):
        o = np.asarray(res.results[c]["out"])
        out[2 * c] = o[:, :S].T
        out[2 * c + 1] = o[:, S:].T
    return out
```

```python
import contextlib
import numpy as np
import concourse.bass as bass
import concourse.mybir as mybir
from concourse.bass_utils import run_bass_kernel_spmd

F32 = mybir.dt.float32; BF16 = mybir.dt.bfloat16; I32 = mybir.dt.int32
AF = mybir.ActivationFunctionType; ALU = mybir.AluOpType

NCORE = 8; D = 1024; S = 4096; NSEQ = 2; NT = NSEQ * S; TT = 512; NTILE = NT // TT
DEPTH = 2; DFF = 2816; EPS = 1e-6
NCH_IN = 32
CH_Z, CH_X, CH_B, CH_C, CH_DT, CH_U, CH_CQ, CH_CKV, CH_KPE, CH_KPP = 0, 8, 16, 18, 20, 21, 25, 28, 30, 31
TWO_PI = float(2 * np.pi)


class Buf:
    __slots__ = ("name", "w", "r")

    def __init__(self, name):
        self.name = name; self.w = None; self.r = []


class Trk:
    ENG = ("pe", "act", "dve", "pool", "sp")

    def __init__(self, nc, stack):
        self.nc = nc; self.stack = stack
        self.eng = {"pe": nc.tensor, "act": nc.scalar, "dve": nc.vector, "pool": nc.gpsimd, "sp": nc.sync}
        self.sem = {k: stack.enter_context(nc.semaphore("prog_" + k)) for k in self.ENG}
        self.cnt = {k: 0 for k in self.ENG}
        self.waited = {}
        self.dmasems = {}; self.dmacnt = {}
        self.nins = 0

    def _wait(self, e, tok):
        if tok is None:
            return
        sem, val, key, src = tok
        if src == e and (e == "pe" or val > self.cnt[e]):
            return
        k = (e, key)
        if self.waited.get(k, 0) >= val:
            return
        self.waited[k] = val
        self.eng[e].wait_ge(sem, val)

    def _deps(self, e, reads, writes):
        for b in reads:
            self._wait(e, b.w)
        for b in writes:
            self._wait(e, b.w)
            for t in b.r:
                self._wait(e, t)

    def _mark(self, tok, reads, writes):
        for b in reads:
            b.r.append(tok)
            if len(b.r) > 8:
                d = {}
                for t in b.r:
                    if t[2] not in d or d[t[2]][1] < t[1]:
                        d[t[2]] = t
                b.r = list(d.values())
        for b in writes:
            b.w = tok; b.r = []

    def op(self, e, fn, reads=(), writes=(), signal=True):
        self._deps(e, reads, writes)
        ins = fn(self.eng[e])
        self.nins += 1
        if signal:
            self.cnt[e] += 1
            ins.then_inc(self.sem[e], 1)
            tok = (self.sem[e], self.cnt[e], "prog_" + e, e)
        else:
            tok = (self.sem[e], self.cnt[e] + 1, "prog_" + e, e)
        self._mark(tok, reads, writes)
        return tok

    def dma(self, q, chan, pairs, reads=(), writes=()):
        self._deps(q, reads, writes)
        if chan not in self.dmasems:
            self.dmasems[chan] = self.stack.enter_context(self.nc.semaphore("dma_" + chan))
            self.dmacnt[chan] = 0
        for (o, i) in pairs:
            self.dmacnt[chan] += 16
            self.eng[q].dma_start(out=o, in_=i).then_inc(self.dmasems[chan], 16)
            self.nins += 1
        tok = (self.dmasems[chan], self.dmacnt[chan], "dma_" + chan, "dma")
        self._mark(tok, reads, writes)
        return tok

    def barrier(self):
        toks = []
        for k in self.ENG:
            if self.cnt[k] > 0:
                toks.append((self.sem[k], self.cnt[k], "prog_" + k, k))
        for c, s in self.dmasems.items():
            toks.append((s, self.dmacnt[c], "dma_" + c, "dma"))
        for e in self.ENG:
            for t in toks:
                if t[3] == e:
                    continue
                self._wait(e, t)


class VMap:
    def __init__(self):
        self.off = {}; self.n = 0

    def add(self, name, n):
        self.off[name] = self.n; self.n += n

    def __call__(self, name, i=0):
        return self.off[name] + i


def _vmap():
    V = VMap()
    for l in range(DEPTH):
        V.add(f"attn_norm{l}", 8); V.add(f"ssd_cw{l}", 48); V.add(f"ssd_cb{l}", 12)
        V.add(f"ssd_norm{l}", 8); V.add(f"pool_scale{l}", 4); V.add(f"q_norm{l}", 3)
        V.add(f"kv_norm{l}", 2); V.add(f"ffn_norm{l}", 8); V.add(f"ffn_cw{l}", 132)
        V.add(f"ffn_cb{l}", 44); V.add(f"dskip{l}", 8); V.add(f"dt_bias{l}", 1); V.add(f"a_log{l}", 16)
    V.add("final_norm", 8); V.add("pool_rc", 64); V.add("invf", 1); V.add("sgn", 1)
    return V


VM = _vmap()
NV = VM.n


def _colmajor(v):
    v = np.asarray(v, np.float32)
    return np.ascontiguousarray(v.reshape(-1, 128).T)


def _build_vecs(inp):
    vecs = np.zeros((128, NV), np.float32)

    def put(name, arr):
        arr = np.asarray(arr, np.float32)
        vecs[:arr.shape[0], VM(name):VM(name) + arr.shape[1]] = arr
    for l in range(DEPTH):
        put(f"attn_norm{l}", _colmajor(inp["attn_norm"][l]))
        cw = inp["ssd_conv_w"][l]
        put(f"ssd_cw{l}", np.concatenate([_colmajor(cw[k]) for k in range(4)], axis=1))
        put(f"ssd_cb{l}", _colmajor(inp["ssd_conv_b"][l]))
        put(f"ssd_norm{l}", _colmajor(inp["ssd_norm"][l]))
        put(f"pool_scale{l}", _colmajor(inp["pool_scale"][l]))
        put(f"q_norm{l}", _colmajor(inp["mla_q_norm"][l]))
        put(f"kv_norm{l}", _colmajor(inp["mla_kv_norm"][l]))
        put(f"ffn_norm{l}", _colmajor(inp["ffn_norm"][l]))
        fw = inp["ffn_conv_w"][l]
        put(f"ffn_cw{l}", np.concatenate([_colmajor(fw[k]) for k in range(3)], axis=1))
        put(f"ffn_cb{l}", _colmajor(inp["ffn_conv_b"][l]))
        put(f"dskip{l}", _colmajor(np.repeat(np.asarray(inp["ssd_d"][l], np.float32), 64)))
        put(f"dt_bias{l}", np.asarray(inp["ssd_dt_bias"][l], np.float32).reshape(16, 1))
        put(f"a_log{l}", np.broadcast_to(np.asarray(inp["ssd_a_log"][l], np.float32)[None, :], (128, 16)))
    put("final_norm", _colmajor(inp["final_norm"]))
    rc = np.zeros((128, 64), np.float32)
    for g, w in enumerate((2, 4, 8, 16)):
        rc[:, g * 16:(g + 1) * 16] = 1.0 / np.minimum(np.arange(1, 17), w)[None, :]
    put("pool_rc", rc)
    invf = np.zeros((128, 1), np.float32); sgn = np.zeros((128, 1), np.float32)
    fr = (10000.0 ** (-np.arange(0, 32, 2, dtype=np.float32) / 32)).astype(np.float32)
    invf[64:80, 0] = fr; invf[80:96, 0] = fr
    sgn[64:80, 0] = -1.0; sgn[80:96, 0] = 1.0
    put("invf", invf); put("sgn", sgn)
    return vecs


def _layout_w_in(w):
    w = np.asarray(w, np.float32)
    o = np.zeros((1024, NCH_IN * 128), np.float32)
    z, xbc, dt, u, cq, ckv, kpe = np.split(w, np.cumsum([1024, 1536, 16, 512, 384, 256])[:], axis=1)
    o[:, 0:1024] = z
    o[:, 1024:2560] = xbc
    o[:, CH_DT * 128:CH_DT * 128 + 16] = dt
    o[:, CH_U * 128:CH_U * 128 + 512] = u
    o[:, CH_CQ * 128:CH_CQ * 128 + 384] = cq
    o[:, CH_CKV * 128:CH_CKV * 128 + 256] = ckv
    o[:, CH_KPE * 128 + 64:CH_KPE * 128 + 96] = kpe
    o[:, CH_KPP * 128 + 64:CH_KPP * 128 + 96] = np.concatenate([kpe[:, 16:32], kpe[:, 0:16]], axis=1)
    return o


def build_program(dbg=None, phases=None, nlayers=DEPTH):
    dbg = dbg or ()
    nc = bass.Bass("TRN2", target_bir_lowering=False)
    es = contextlib.ExitStack()
    T = Trk(nc, es)

    def din(name, shape, dt=F32):
        return nc.dram_tensor(name, list(shape), dt, kind="ExternalInput").ap()

    def dscr(name, shape, dt):
        kind = "ExternalOutput" if name in dbg else "Internal"
        return nc.dram_tensor(name, list(shape), dt, kind=kind).ap()

    xT = din("xT", [D, NT]); pos = din("pos", [1, NT], I32); vecs_d = din("vecs", [128, NV])
    w_in_d = din("w_in", [DEPTH, 1024, NCH_IN * 128])
    pool_w_d = din("pool_w", [DEPTH, 4, 128, 128])
    w_uq_d = din("w_uq", [DEPTH, 384, 8, 96]); w_uqp_d = din("w_uqp", [DEPTH, 384, 8, 96])
    w_kn_d = din("w_kn", [DEPTH, 256, 512]); w_v_d = din("w_v", [DEPTH, 256, 512])
    w_out_d = din("w_out", [DEPTH, 2048, 1024])
    w_up_d = din("w_up", [DEPTH, 1024, 2 * DFF]); w_dn_d = din("w_dn", [DEPTH, DFF, 1024])
    out_d = nc.dram_tensor("out", [D, NT], F32, kind="ExternalOutput").ap()

    projT = dscr("projT", [NCH_IN * 128, NT], BF16)
    dtT = dscr("dtT", [16, NT], F32)
    ymixT = dscr("ymixT", [2048, NT], BF16)
    xres = dscr("xres", [D, NT], F32)
    h2T = dscr("h2T", [D, NT], BF16)
    ropeC = dscr("ropeC", [32, NT], F32); ropeS = dscr("ropeS", [32, NT], F32)

    PS = [nc.alloc_psum_tensor(f"ps{i}", [128, 512], F32) for i in range(8)]
    PB = [Buf(f"ps{i}") for i in range(8)]

    uid = [0]

    def sb(stack, name, shape, dt=F32):
        uid[0] += 1
        return stack.enter_context(nc.sbuf_tensor(f"s{uid[0]}_{name}", list(shape), dt))

    vecs = sb(es, "vecs", [128, NV]); Bvecs = Buf("vecs")
    ones_bf = sb(es, "ones_bf", [128, 128], BF16); ident_bf = sb(es, "ident_bf", [128, 128], BF16)
    ident_f = sb(es, "ident_f", [128, 128]); triu_bf = sb(es, "triu_bf", [128, 128], BF16)
    epsb = sb(es, "epsb", [128, 1]); Bconst = Buf("const")
    T.dma("sp", "vecs", [(vecs[:], vecs_d)], writes=[Bvecs])
    T.op("pool", lambda e: e.memset(ones_bf[:], 1.0), writes=[Bconst])
    T.op("pool", lambda e: e.memset(epsb[:], EPS), writes=[Bconst])
    for t_, cmp_ in ((ident_bf, ALU.is_equal), (ident_f, ALU.is_equal), (triu_bf, ALU.is_ge)):
        T.op("pool", lambda e: e.memset(t_[:], 1.0), writes=[Bconst])
        T.op("pool", lambda e: e.affine_select(out=t_[:], in_=t_[:], pattern=[[1, 128]], compare_op=cmp_,
                                                fill=0.0, base=0, channel_multiplier=-1),
             reads=[Bconst], writes=[Bconst])

    def vc(name, i=0, n=1, p0=0, p1=128):
        return vecs[p0:p1, VM(name, i):VM(name, i) + n]

    def run(ph):
        return phases is None or ph in phases

    if run("rope"):
        with contextlib.ExitStack() as ph:
            PW = 2048
            pi_ = sb(ph, "r_pi", [128, PW], I32); ang = sb(ph, "r_ang", [128, PW]); kf = sb(ph, "r_kf", [128, PW])
            ki = sb(ph, "r_ki", [128, PW], I32); rr = sb(ph, "r_rr", [128, PW]); mm = sb(ph, "r_mm", [128, PW])
            Bp, Ba, Bk, Br, Bm = Buf("pi"), Buf("ang"), Buf("kf"), Buf("rr"), Buf("mm")
            R = slice(64, 96)
            for pc in range(NT // PW):
                cs = slice(pc * PW, (pc + 1) * PW)
                T.dma("sp", "r_ld", [(pi_[R, :], pos[:, cs].partition_broadcast(32))], writes=[Bp])
                T.op("dve", lambda e: e.tensor_copy(out=ang[R, :], in_=pi_[R, :]), reads=[Bp], writes=[Ba])
                T.op("dve", lambda e: e.tensor_scalar(out=ang[R, :], in0=ang[R, :], scalar1=vc("invf", p0=64, p1=96),
                                                      scalar2=None, op0=ALU.mult), reads=[Ba, Bvecs], writes=[Ba])
                for which, dst in ((0, ropeS), (1, ropeC)):
                    src = ang
                    if which == 1:
                        T.op("dve", lambda e: e.tensor_scalar(out=rr[R, :], in0=ang[R, :], scalar1=float(np.pi / 2),
                                                              scalar2=None, op0=ALU.add), reads=[Ba], writes=[Br])
                        src = rr
                    T.op("dve", lambda e: e.tensor_scalar(out=kf[R, :], in0=src[R, :], scalar1=float(1 / TWO_PI),
                                                          scalar2=None, op0=ALU.mult), reads=[Ba, Br], writes=[Bk])
                    T.op("dve", lambda e: e.tensor_copy(out=ki[R, :], in_=kf[R, :]), reads=[Bk], writes=[Bk])
                    T.op("dve", lambda e: e.tensor_copy(out=kf[R, :], in_=ki[R, :]), reads=[Bk], writes=[Bk])
                    T.op("dve", lambda e: e.scalar_tensor_tensor(out=rr[R, :], in0=kf[R, :], scalar=-TWO_PI, in1=src[R, :],
                                                                 op0=ALU.mult, op1=ALU.add), reads=[Bk, Ba, Br], writes=[Br])
                    for (cmp_, thr, add) in ((ALU.is_gt, float(np.pi), -TWO_PI), (ALU.is_lt, float(-np.pi), TWO_PI)):
                        T.op("dve", lambda e: e.tensor_scalar(out=mm[R, :], in0=rr[R, :], scalar1=thr, scalar2=add,
                                                              op0=cmp_, op1=ALU.mult), reads=[Br], writes=[Bm])
                        T.op("dve", lambda e: e.tensor_tensor(out=rr[R, :], in0=rr[R, :], in1=mm[R, :], op=ALU.add),
                             reads=[Br, Bm], writes=[Br])
                    T.op("act", lambda e: e.activation(out=rr[R, :], in_=rr[R, :], func=AF.Sin), reads=[Br], writes=[Br])
                    if which == 0:
                        T.op("dve", lambda e: e.tensor_scalar(out=rr[R, :], in0=rr[R, :], scalar1=vc("sgn", p0=64, p1=96),
                                                              scalar2=None, op0=ALU.mult), reads=[Br, Bvecs], writes=[Br])
                    T.dma("sp", "r_st", [(dst[:, cs], rr[R, :])], reads=[Br])
            T.barrier()

    for l in range(nlayers):
        xsrc = xT if l == 0 else xres
        if run("p1"):
            with contextlib.ExitStack() as ph:
                win = sb(ph, "win", [128, 8, NCH_IN * 128], BF16); Bwin = Buf("win")
                T.dma("pool", "w0", [(win[:, kc, :], w_in_d[l, kc * 128:(kc + 1) * 128, :]) for kc in range(8)], writes=[Bwin])
                xt = [sb(ph, f"p1x{i}", [128, 8, TT]) for i in range(2)]; Bxt = [Buf("xt0"), Buf("xt1")]
                sq = sb(ph, "p1sq", [128, 8, TT], BF16); Bsq = Buf("sq")
                hbs = [sb(ph, f"p1h{i}", [128, 8, TT], BF16) for i in range(2)]; Bhs = [Buf("h0"), Buf("h1")]
                rstd = sb(ph, "p1rstd", [128, TT]); Brs = Buf("rstd")
                stg = [sb(ph, f"p1stg{i}", [128, NCH_IN, TT], BF16) for i in range(2)]
                Bstg = [[Buf(f"stg{i}_{a}") for a in range(4)] for i in range(2)]
                dts = [sb(ph, f"p1dt{i}", [16, TT]) for i in range(2)]; Bdts = [Buf("dts0"), Buf("dts1")]

                def load_x(j):
                    T.dma("sp", f"p1x{j % 2}", [(xt[j % 2][:], xsrc[:, j * TT:(j + 1) * TT].rearrange("(c p) t -> p c t", p=128))],
                          writes=[Bxt[j % 2]])

                def norm_part(j, part):
                    bb = j % 2; X = xt[bb]
                    if part == 0:
                        T.op("act", lambda e: e.activation(out=sq[:], in_=X[:], func=AF.Square), reads=[Bxt[bb]], writes=[Bsq])
                    elif part == 1:
                        for c in range(8):
                            T.op("pe", lambda e: e.matmul(PS[0][:], ones_bf[:], sq[:, c, :], start=(c == 0), stop=(c == 7)),
                                 reads=[Bsq, Bconst], writes=[PB[0]], signal=(c == 7))
                    elif part == 2:
                        T.op("act", lambda e: e.activation(out=rstd[:], in_=PS[0][:], func=AF.Ln, bias=epsb[:, 0:1], scale=1.0 / D),
                             reads=[PB[0], Bconst], writes=[Brs])
                        T.op("act", lambda e: e.activation(out=rstd[:], in_=rstd[:], func=AF.Exp, scale=-0.5), reads=[Brs], writes=[Brs])
                    else:
                        for c in range(8):
                            T.op("dve", lambda e: e.scalar_tensor_tensor(out=hbs[bb][:, c, :], in0=X[:, c, :], scalar=vc(f"attn_norm{l}", c),
                                                                         in1=rstd[:], op0=ALU.mult, op1=ALU.mult),
                                 reads=[Bxt[bb], Brs, Bvecs], writes=[Bhs[bb]], signal=(c == 7))
                load_x(0); load_x(1)
                for part in range(4):
                    norm_part(0, part)
                ev = 0
                for j in range(NTILE):
                    b = j % 2
                    hb = hbs[b]; Bh = Bhs[b]
                    if j >= 1 and j + 1 < NTILE:
                        load_x(j + 1)
                    for m in range(NCH_IN):
                        M = 16 if m == CH_DT else (96 if m >= CH_KPE else 128)
                        pb = 1 + (m % 6)
                        if j + 1 < NTILE and m in (6, 10, 14, 18):
                            norm_part(j + 1, (m - 6) // 4)
                        for kc in range(8):
                            T.op("pe", lambda e: e.matmul(PS[pb][0:M, :], win[:, kc, m * 128:m * 128 + M], hb[:, kc, :],
                                                          start=(kc == 0), stop=(kc == 7)),
                                 reads=[Bh, Bwin], writes=[PB[pb]], signal=(kc == 7))
                        if m == CH_DT:
                            T.op("dve", lambda e: e.tensor_copy(out=dts[b][:], in_=PS[pb][0:16, :]), reads=[PB[pb]], writes=[Bdts[b]])
                            continue
                        a = m // 8
                        if ev % 2 == 0:
                            T.op("act", lambda e: e.activation(out=stg[b][0:M, m, :], in_=PS[pb][0:M, :], func=AF.Copy),
                                 reads=[PB[pb]], writes=[Bstg[b][a]])
                        else:
                            T.op("dve", lambda e: e.tensor_copy(out=stg[b][0:M, m, :], in_=PS[pb][0:M, :]),
                                 reads=[PB[pb]], writes=[Bstg[b][a]])
                        ev += 1
                        if m % 8 == 7:
                            cs = slice(j * TT, (j + 1) * TT)
                            if a < 2:
                                T.dma("sp", f"p1s{b}{a}", [(projT[a * 1024:(a + 1) * 1024, cs].rearrange("(c p) t -> p c t", p=128),
                                                            stg[b][:, a * 8:(a + 1) * 8, :])], reads=[Bstg[b][a]])
                            elif a == 2:
                                T.dma("sp", f"p1s{b}{a}", [(projT[2048:2560, cs].rearrange("(c p) t -> p c t", p=128), stg[b][:, 16:20, :]),
                                                            (projT[2688:3072, cs].rearrange("(c p) t -> p c t", p=128), stg[b][:, 21:24, :])],
                                      reads=[Bstg[b][a]])
                            else:
                                T.dma("sp", f"p1s{b}{a}", [(projT[3072:3840, cs].rearrange("(c p) t -> p c t", p=128), stg[b][:, 24:30, :]),
                                                            (projT[CH_KPE * 128 + 64:CH_KPE * 128 + 96, cs], stg[b][64:96, CH_KPE, :]),
                                                            (projT[CH_KPP * 128 + 64:CH_KPP * 128 + 96, cs], stg[b][64:96, CH_KPP, :])],
                                      reads=[Bstg[b][a]])
                    T.dma("sp", f"p1d{b}", [(dtT[:, j * TT:(j + 1) * TT], dts[b][:])], reads=[Bdts[b]])
                T.barrier()


        if run("p2"):
            with contextlib.ExitStack() as ph:
                pw = sb(ph, "pw", [128, 4, 128], BF16); Bpw = Buf("pw")
                T.dma("pool", "w0", [(pw[:, g, :], pool_w_d[l, g]) for g in range(4)], writes=[Bpw])
                HL = 16
                ut = [sb(ph, f"p2u{i}", [128, 4, HL + TT], BF16) for i in range(2)]; But = [Buf("ut0"), Buf("ut1")]
                uf = sb(ph, "p2uf", [128, HL + TT]); sA = sb(ph, "p2a", [128, HL + TT]); sB = sb(ph, "p2b", [128, HL + TT])
                Buf_, BsA, BsB = Buf("uf"), Buf("sA"), Buf("sB")
                t16 = sb(ph, "p2t16", [128, 16]); Bt16 = Buf("t16")
                pl = [sb(ph, f"p2pl{i}", [128, TT], BF16) for i in range(2)]; Bpl = [Buf("pl0"), Buf("pl1")]
                stg = [sb(ph, f"p2s{i}", [128, 4, TT], BF16) for i in range(2)]; Bstg = [Buf("s0"), Buf("s1")]
                urows = projT[CH_U * 128:(CH_U + 4) * 128, :].rearrange("(c p) t -> p c t", p=128)

                def load_u(j):
                    b = j % 2; t0 = j * TT
                    if j % (S // TT) == 0:
                        T.op("pool", lambda e: e.memset(ut[b][:, :, 0:HL], 0.0), writes=[But[b]])
                        T.dma("sp", f"p2u{b}", [(ut[b][:, :, HL:], urows[:, :, t0:t0 + TT])], writes=[But[b]])
                    else:
                        T.dma("sp", f"p2u{b}", [(ut[b][:, :, :], urows[:, :, t0 - HL:t0 + TT])], writes=[But[b]])
                load_u(0)
                k = 0
                for j in range(NTILE):
                    b = j % 2
                    if j + 1 < NTILE:
                        load_u(j + 1)
                    first = (j % (S // TT) == 0)
                    for g in range(4):
                        w = 2 << g
                        T.op("act", lambda e: e.activation(out=uf[:], in_=ut[b][:, g, :], func=AF.Copy), reads=[But[b]], writes=[Buf_])
                        cur, Bcur = uf, Buf_
                        for s_ in range(g + 1):
                            sh = 1 << s_
                            nxt, Bn = (sA, BsA) if cur is not sA else (sB, BsB)
                            v0 = sh - 1
                            T.op("dve", lambda e: e.tensor_tensor(out=nxt[:, v0 + sh:], in0=cur[:, v0 + sh:], in1=cur[:, v0:HL + TT - sh], op=ALU.add),
                                 reads=[Bcur], writes=[Bn])
                            cur, Bcur = nxt, Bn
                        pb_ = k % 2; k += 1
                        T.op("dve", lambda e: e.scalar_tensor_tensor(out=pl[pb_][:], in0=cur[:, HL:], scalar=1.0 / w, in1=uf[:, HL:],
                                                                     op0=ALU.mult, op1=ALU.subtract), reads=[Bcur, Buf_], writes=[Bpl[pb_]])
                        if first:
                            T.op("dve", lambda e: e.tensor_tensor(out=t16[:], in0=cur[:, HL:HL + 16], in1=vc("pool_rc", g * 16, 16), op=ALU.mult),
                                 reads=[Bcur, Bvecs], writes=[Bt16])
                            T.op("dve", lambda e: e.tensor_tensor(out=pl[pb_][:, 0:16], in0=t16[:], in1=uf[:, HL:HL + 16], op=ALU.subtract),
                                 reads=[Bt16, Buf_], writes=[Bpl[pb_]])
                        pq = 1 + (k % 4)
                        T.op("pe", lambda e: e.matmul(PS[pq][:], pw[:, g, :], pl[pb_][:], start=True, stop=True),
                             reads=[Bpw, Bpl[pb_]], writes=[PB[pq]])
                        T.op("act", lambda e: e.activation(out=stg[b][:, g, :], in_=PS[pq][:], func=AF.Copy, scale=vc(f"pool_scale{l}", g)),
                             reads=[PB[pq], Bvecs], writes=[Bstg[b]])
                    T.dma("sp", f"p2s{b}", [(ymixT[1024:1536, j * TT:(j + 1) * TT].rearrange("(c p) t -> p c t", p=128), stg[b][:])],
                          reads=[Bstg[b]])
                T.barrier()

        if run("p3"):
            with contextlib.ExitStack() as ph:
                HL = 3; NC4 = TT // 128
                xin = [sb(ph, f"p3x{i}", [128, 12, HL + TT], BF16) for i in range(2)]; Bxin = [Buf("xin0"), Buf("xin1")]
                zin = [sb(ph, f"p3z{i}", [128, 8, TT], BF16) for i in range(2)]; Bzin = [Buf("z0"), Buf("z1")]
                dtr = [sb(ph, f"p3d{i}", [16, TT]) for i in range(2)]; Bdtr = [Buf("dtr0"), Buf("dtr1")]
                xc = sb(ph, "p3xc", [128, 12, TT], BF16); Bxc = [Buf(f"xc{c}") for c in range(12)]
                dg = sb(ph, "p3dg", [128, 48, 128], BF16); Bdg = Buf("dg")
                for kc_ in range(48):
                    T.op("pool", lambda e: e.tensor_scalar(out=dg[:, kc_, :], in0=ident_bf[:], scalar1=vc(f"ssd_cw{l}", kc_), scalar2=None, op0=ALU.mult),
                         reads=[Bconst, Bvecs], writes=[Bdg], signal=(kc_ == 47))
                dtT_s = sb(ph, "p3dtT", [16, TT]); BdtT = Buf("dtT")
                arow = sb(ph, "p3arow", [128, 16]); Barow = Buf("arow")
                dt_tok = sb(ph, "p3dttok", [128, NC4, 16]); da_bf = sb(ph, "p3dabf", [128, NC4, 16], BF16)
                cum_s = sb(ph, "p3cum", [128, NC4, 16]); dte = sb(ph, "p3dte", [128, NC4, 16]); dtw = sb(ph, "p3dtw", [128, NC4, 16])
                edec = sb(ph, "p3edec", [128, NC4, 16]); BsmA = Buf("smallA")
                rhsA = sb(ph, "p3rhsA", [128, NC4, 16, 128], BF16); BrhsA = [Buf(f"rhsA{c}") for c in range(NC4)]
                xdt = sb(ph, "p3xdt", [128, NC4, 1024], BF16); Bxdt = [Buf(f"xdt{c}") for c in range(NC4)]
                xw = sb(ph, "p3xw", [128, NC4, 1024], BF16); Bxw = [Buf(f"xw{c}") for c in range(NC4)]
                bst = sb(ph, "p3bst", [128, NC4, 256], BF16); Bbst = [Buf(f"bst{c}") for c in range(NC4)]
                NS3 = 3
                cbm = [sb(ph, f"p3cbm{i}", [128, 128], BF16) for i in range(NS3)]; Bcbm = [Buf(f"cbm{i}") for i in range(NS3)]
                seg = [sb(ph, f"p3seg{i}", [128, 8, 128]) for i in range(NS3)]; BsegA = [Buf(f"segA{i}") for i in range(NS3)]; BsegB = [Buf(f"segB{i}") for i in range(NS3)]
                Dm = [sb(ph, f"p3Dm{i}", [128, 8, 128], BF16) for i in range(NS3)]; BDm = [Buf(f"Dm{i}") for i in range(NS3)]
                Eb = [sb(ph, f"p3Eb{i}", [128, 8, 128], BF16) for i in range(NS3)]; BEb = [Buf(f"Eb{i}") for i in range(NS3)]
                Mm = [sb(ph, f"p3M{i}", [128, 8, 128], BF16) for i in range(NS3)]; BMm = [Buf(f"M{i}") for i in range(NS3)]
                CE = [sb(ph, f"p3CE{i}", [128, 8, 128], BF16) for i in range(NS3)]; BCE = [Buf(f"CE{i}") for i in range(NS3)]
                Hs = sb(ph, "p3H", [128, 1024]); BH = [Buf("H0"), Buf("H1")]
                Hbf = sb(ph, "p3Hbf", [128, NC4, 1024], BF16); BHbf = [[Buf(f"Hbf{c}_{g}") for g in range(2)] for c in range(NC4)]
                ysb = sb(ph, "p3ysb", [128, 8, TT]); Bysb = [Buf(f"ysb{f}") for f in range(8)]
                zsbs = [sb(ph, f"p3zsb{i}", [128, 8, TT], BF16) for i in range(2)]; Bzsbs = [[Buf(f"zsb{i}_{f}") for f in range(8)] for i in range(2)]
                sq = sb(ph, "p3sq", [128, 8, TT], BF16); Bsq = Buf("sq")
                rstd = sb(ph, "p3rstd", [128, TT]); Brs = Buf("rstd")
                yout = [sq] * 2; Byout = [Bsq] * 2
                T.op("act", lambda e: e.activation(out=arow[:], in_=vc(f"a_log{l}", 0, 16), func=AF.Exp), reads=[Bvecs], writes=[Barow])
                T.op("dve", lambda e: e.tensor_scalar(out=arow[:], in0=arow[:], scalar1=-1.0, scalar2=None, op0=ALU.mult), reads=[Barow], writes=[Barow])
                xrows = projT[CH_X * 128:(CH_X + 12) * 128, :].rearrange("(c p) t -> p c t", p=128)
                zrows = projT[0:1024, :].rearrange("(c p) t -> p c t", p=128)
                PSXb = PS[1][:].bitcast(BF16)
                PSBb = PS[2][:].bitcast(BF16)
                DTC, CUMC, TOTC, CBC = 128, 192, 256, 320

                def load3(j):
                    b = j % 2; t0 = j * TT
                    if j % (S // TT) == 0:
                        T.op("pool", lambda e: e.memset(xin[b][:, :, 0:HL], 0.0), writes=[Bxin[b]])
                        T.dma("sp", f"p3x{b}", [(xin[b][:, :, HL:], xrows[:, :, t0:t0 + TT])], writes=[Bxin[b]])
                    else:
                        T.dma("sp", f"p3x{b}", [(xin[b][:, :, :], xrows[:, :, t0 - HL:t0 + TT])], writes=[Bxin[b]])
                    T.dma("sp", f"p3z{b}", [(zin[b][:], zrows[:, :, t0:t0 + TT])], writes=[Bzin[b]])
                    T.dma("sp", f"p3d{b}", [(dtr[b][:], dtT[:, t0:t0 + TT])], writes=[Bdtr[b]])
                def stage0(j):
                    b = j % 2; zsb = zsbs[b]; Bzsb = Bzsbs[b]
                    for c in range(12):
                        pq = 3 + c % 4
                        for k_ in range(4):
                            T.op("pe", lambda e: e.matmul(PS[pq][:], dg[:, k_ * 12 + c, :], xin[b][:, c, k_:k_ + TT], start=(k_ == 0), stop=(k_ == 3)),
                                 reads=[Bdg, Bxin[b]], writes=[PB[pq]], signal=(k_ == 3))
                        T.op("act", lambda e: e.activation(out=xc[:, c, :], in_=PS[pq][:], func=AF.Silu, bias=vc(f"ssd_cb{l}", c)), reads=[PB[pq], Bvecs], writes=[Bxc[c]])
                    for f in range(8):
                        T.op("act", lambda e: e.activation(out=zsb[:, f, :], in_=zin[b][:, f, :], func=AF.Silu), reads=[Bzin[b]], writes=[Bzsb[f]])

                def dtpart(j):
                    b = j % 2
                    T.op("act", lambda e: e.activation(out=dtT_s[:], in_=dtr[b][:], func=AF.Exp, bias=vc(f"dt_bias{l}", p1=16), scale=1.0),
                         reads=[Bdtr[b], Bvecs], writes=[BdtT])
                    T.op("act", lambda e: e.activation(out=dtT_s[:], in_=dtT_s[:], func=AF.Ln, bias=1.0), reads=[BdtT], writes=[BdtT])
                    for c4 in range(NC4):
                        T.op("pe", lambda e: e.transpose(out=PS[2][:, DTC + 16 * c4:DTC + 16 * c4 + 16], in_=dtT_s[0:16, c4 * 128:(c4 + 1) * 128], identity=ident_f[0:16, 0:16]),
                             reads=[BdtT, Bconst], writes=[PB[2]], signal=(c4 == NC4 - 1))

                def epilogueA(j):
                    b = j % 2; zsb = zsbs[b]; Bzsb = Bzsbs[b]
                    for f in range(8):
                        T.op("dve", lambda e: e.tensor_tensor(out=ysb[:, f, :], in0=ysb[:, f, :], in1=zsb[:, f, :], op=ALU.mult), reads=[Bysb[f], Bzsb[f]], writes=[Bysb[f]])
                    T.op("act", lambda e: e.activation(out=sq[:], in_=ysb[:], func=AF.Square), reads=Bysb, writes=[Bsq])
                    for c in range(8):
                        T.op("pe", lambda e: e.matmul(PS[3][:], ones_bf[:], sq[:, c, :], start=(c == 0), stop=(c == 7)),
                             reads=[Bsq, Bconst], writes=[PB[3]], signal=(c == 7))
                    T.op("act", lambda e: e.activation(out=rstd[:], in_=PS[3][:], func=AF.Ln, bias=epsb[:, 0:1], scale=1.0 / 1024), reads=[PB[3], Bconst], writes=[Brs])
                    T.op("act", lambda e: e.activation(out=rstd[:], in_=rstd[:], func=AF.Exp, scale=-0.5), reads=[Brs], writes=[Brs])

                def epilogueB(j):
                    b = j % 2
                    for c in range(8):
                        T.op("dve", lambda e: e.scalar_tensor_tensor(out=yout[b][:, c, :], in0=ysb[:, c, :], scalar=vc(f"ssd_norm{l}", c), in1=rstd[:],
                                                                     op0=ALU.mult, op1=ALU.mult), reads=[Bysb[c], Brs, Bvecs], writes=[Byout[b]], signal=(c == 7))
                    T.dma("sp", f"p3s{b}", [(ymixT[0:1024, j * TT:(j + 1) * TT].rearrange("(c p) t -> p c t", p=128), yout[b][:])], reads=[Byout[b]])

                def stage1a():
                    F2 = lambda t: t[:].rearrange("p c h -> p (c h)")
                    T.op("dve", lambda e: e.tensor_copy(out=F2(dt_tok), in_=PS[2][:, DTC:DTC + 64]), reads=[PB[2]], writes=[BsmA])
                    T.op("dve", lambda e: e.tensor_tensor(out=da_bf[:], in0=dt_tok[:], in1=arow[:].unsqueeze(1).broadcast_to([128, NC4, 16]), op=ALU.mult), reads=[BsmA, Barow], writes=[BsmA])
                    T.op("pe", lambda e: e.matmul(PS[2][:, CUMC:CUMC + 64], triu_bf[:], F2(da_bf), start=True, stop=True), reads=[BsmA, Bconst], writes=[PB[2]], signal=False)
                    T.op("pe", lambda e: e.matmul(PS[2][:, TOTC:TOTC + 64], ones_bf[:], F2(da_bf), start=True, stop=True), reads=[BsmA, Bconst], writes=[PB[2]])
                    T.op("dve", lambda e: e.tensor_copy(out=F2(cum_s), in_=PS[2][:, CUMC:CUMC + 64]), reads=[PB[2]], writes=[BsmA])
                    T.op("dve", lambda e: e.tensor_tensor(out=F2(dte), in0=PS[2][:, TOTC:TOTC + 64], in1=F2(cum_s), op=ALU.subtract), reads=[PB[2], BsmA], writes=[BsmA])
                    T.op("act", lambda e: e.activation(out=F2(dte), in_=F2(dte), func=AF.Exp), reads=[BsmA], writes=[BsmA])
                    T.op("act", lambda e: e.activation(out=F2(edec), in_=PS[2][:, TOTC:TOTC + 64], func=AF.Exp), reads=[PB[2]], writes=[BsmA])
                    T.op("dve", lambda e: e.tensor_tensor(out=F2(dtw), in0=F2(dte), in1=F2(dt_tok), op=ALU.mult), reads=[BsmA], writes=[BsmA])
                    for c4 in range(NC4):
                        T.op("pool", lambda e: e.tensor_tensor(out=rhsA[:, c4, :, :], in0=da_bf[:, c4, :].unsqueeze(2).broadcast_to([128, 16, 128]),
                                                               in1=triu_bf[:].unsqueeze(1).broadcast_to([128, 16, 128]), op=ALU.mult),
                             reads=[BsmA, Bconst], writes=[BrhsA[c4]])

                load3(0)
                dtpart(0)
                stage1a()
                stage0(0)
                for j in range(NTILE):
                    b = j % 2
                    if j + 1 < NTILE:
                        load3(j + 1)
                    if j % (S // TT) == 0:
                        for g in range(2):
                            T.op("pool", lambda e: e.memset(Hs[:, g * 512:(g + 1) * 512], 0.0), writes=[BH[g]])
                            T.op("pool", lambda e: e.memset(Hbf[:, 0, g * 512:(g + 1) * 512], 0.0), writes=[BHbf[0][g]])
                    for c4 in range(NC4):
                        cc = slice(c4 * 128, (c4 + 1) * 128)
                        for f in range(8):
                            T.op("pe", lambda e: e.transpose(out=PSXb[:, f * 128:(f + 1) * 128], in_=xc[:, f, cc], identity=ident_bf[:]),
                                 reads=[Bxc[f], Bconst], writes=[PB[1]], signal=(f == 7))
                        for f in range(2):
                            T.op("pe", lambda e: e.transpose(out=PSBb[:, f * 128:(f + 1) * 128], in_=xc[:, 8 + f, cc], identity=ident_bf[:]),
                                 reads=[Bxc[8 + f], Bconst], writes=[PB[2]], signal=(f == 1))
                        T.op("act", lambda e: e.activation(out=bst[:, c4, :], in_=PSBb[:, 0:256], func=AF.Copy), reads=[PB[2]], writes=[Bbst[c4]])
                        T.op("dve", lambda e: e.tensor_tensor(out=xdt[:, c4, :].rearrange("p (h d) -> p h d", h=16), in0=PSXb.rearrange("p (h d) -> p h d", h=16),
                                                              in1=dt_tok[:, c4, :].unsqueeze(2).broadcast_to([128, 16, 64]), op=ALU.mult),
                             reads=[PB[1], BsmA], writes=[Bxdt[c4]])
                        T.op("dve", lambda e: e.tensor_tensor(out=xw[:, c4, :].rearrange("p (h d) -> p h d", h=16), in0=PSXb.rearrange("p (h d) -> p h d", h=16),
                                                              in1=dtw[:, c4, :].unsqueeze(2).broadcast_to([128, 16, 64]), op=ALU.mult),
                             reads=[PB[1], BsmA], writes=[Bxw[c4]])

                    def stage2(it):
                        c4, g = divmod(it, 2); cc = slice(c4 * 128, (c4 + 1) * 128)
                        pa = (3, 4) if it % 2 == 0 else (5, 6)
                        ip = it % NS3
                        for q in range(2):
                            T.op("pe", lambda e: e.matmul(PS[pa[q]][:], ones_bf[:], rhsA[:, c4, 8 * g + 4 * q:8 * g + 4 * q + 4, :].rearrange("p h l -> p (h l)"),
                                                          start=True, stop=True), reads=[BrhsA[c4], Bconst], writes=[PB[pa[q]]])
                        T.op("pe", lambda e: e.matmul(PS[2][:, CBC:CBC + 128], xc[:, 8 + g, cc], xc[:, 10 + g, cc], start=True, stop=True),
                             reads=[Bxc[8 + g], Bxc[10 + g]], writes=[PB[2]])
                        T.op("dve", lambda e: e.tensor_tensor(out=cbm[ip][:], in0=PS[2][:, CBC:CBC + 128], in1=triu_bf[:], op=ALU.mult),
                             reads=[PB[2], Bconst], writes=[Bcbm[ip]])
                        for hh in range(8):
                            h = 8 * g + hh
                            src_ = PS[pa[hh // 4]][:, (hh % 4) * 128:(hh % 4 + 1) * 128]
                            if hh < 4:
                                T.op("act", lambda e: e.activation(out=seg[ip][:, hh, :], in_=src_, func=AF.Relu, bias=cum_s[:, c4, h:h + 1], scale=-1.0),
                                     reads=[PB[pa[0]], BsmA], writes=[BsegA[ip]], signal=(hh == 3))
                            else:
                                T.op("dve", lambda e: e.tensor_scalar(out=seg[ip][:, hh, :], in0=src_, scalar1=cum_s[:, c4, h:h + 1], scalar2=0.0, op0=ALU.subtract, op1=ALU.min),
                                     reads=[PB[pa[1]], BsmA], writes=[BsegB[ip]], signal=(hh == 7))
                        T.op("act", lambda e: e.activation(out=Dm[ip][:, 0:4, :], in_=seg[ip][:, 0:4, :], func=AF.Exp, scale=-1.0), reads=[BsegA[ip]], writes=[BDm[ip]], signal=False)
                        T.op("act", lambda e: e.activation(out=Dm[ip][:, 4:8, :], in_=seg[ip][:, 4:8, :], func=AF.Exp), reads=[BsegB[ip]], writes=[BDm[ip]])
                        for q in range(2):
                            T.op("act", lambda e: e.activation(out=Eb[ip][:, 4 * q:4 * q + 4, :].rearrange("p h l -> p (h l)"), in_=PS[pa[q]][:], func=AF.Exp),
                                 reads=[PB[pa[q]]], writes=[BEb[ip]], signal=(q == 1))
                        T.op("pool", lambda e: e.tensor_tensor(out=Mm[ip][:], in0=Dm[ip][:], in1=cbm[ip][:].unsqueeze(1).broadcast_to([128, 8, 128]), op=ALU.mult),
                             reads=[BDm[ip], Bcbm[ip]], writes=[BMm[ip]])
                        T.op("pool", lambda e: e.tensor_tensor(out=CE[ip][:], in0=Eb[ip][:], in1=xc[:, 10 + g, cc].unsqueeze(1).broadcast_to([128, 8, 128]), op=ALU.mult),
                             reads=[BEb[ip], Bxc[10 + g]], writes=[BCE[ip]])

                    def stage34(it):
                        c4, g = divmod(it, 2); ip = it % NS3; cc = slice(c4 * 128, (c4 + 1) * 128)
                        py = 7 if it % 2 == 0 else 0
                        for hp in range(4):
                            for hx in range(2):
                                h = 8 * g + 2 * hp + hx
                                o_ = PS[py][hx * 64:(hx + 1) * 64, hp * 128:(hp + 1) * 128]
                                kw = {"tile_position": (0, 64)} if hx == 1 else {}
                                T.op("pe", lambda e: e.matmul(o_, xdt[:, c4, h * 64:(h + 1) * 64], Mm[ip][:, 2 * hp + hx, :], start=True, stop=False, **kw),
                                     reads=[Bxdt[c4], BMm[ip]], writes=[PB[py]], signal=False)
                                T.op("pe", lambda e: e.matmul(o_, Hbf[:, c4, h * 64:(h + 1) * 64], CE[ip][:, 2 * hp + hx, :], start=False, stop=True, **kw),
                                     reads=[BHbf[c4][g], BCE[ip]], writes=[PB[py]], signal=(hp == 3 and hx == 1))
                        T.op("pe", lambda e: e.matmul(PS[1][:], bst[:, c4, g * 128:(g + 1) * 128], xw[:, c4, g * 512:(g + 1) * 512], start=True, stop=True),
                             reads=[Bbst[c4], Bxw[c4]], writes=[PB[1]])
                        for hp in range(4):
                            f = 4 * g + hp
                            T.op("dve", lambda e: e.scalar_tensor_tensor(out=ysb[:, f, cc], in0=xc[:, f, cc], scalar=vc(f"dskip{l}", f),
                                                                         in1=PS[py][:, hp * 128:(hp + 1) * 128], op0=ALU.mult, op1=ALU.add),
                                 reads=[Bxc[f], PB[py], Bvecs], writes=[Bysb[f]])
                        Hg = Hs[:, g * 512:(g + 1) * 512]
                        T.op("dve", lambda e: e.tensor_tensor(out=Hg.rearrange("p (h d) -> p h d", h=8), in0=Hg.rearrange("p (h d) -> p h d", h=8),
                                                              in1=edec[:, c4, 8 * g:8 * g + 8].unsqueeze(2).broadcast_to([128, 8, 64]), op=ALU.mult),
                             reads=[BH[g], BsmA], writes=[BH[g]])
                        T.op("dve", lambda e: e.tensor_tensor(out=Hg, in0=Hg, in1=PS[1][:], op=ALU.add), reads=[BH[g], PB[1]], writes=[BH[g]])
                        nx = (c4 + 1) % NC4
                        T.op("act", lambda e: e.activation(out=Hbf[:, nx, g * 512:(g + 1) * 512], in_=Hg, func=AF.Copy), reads=[BH[g]], writes=[BHbf[nx][g]])
                    NIT = 2 * NC4
                    for it in range(NIT + 2):
                        if it < NIT:
                            stage2(it)
                        if it >= 2:
                            stage34(it - 2)
                    if j + 1 < NTILE:
                        dtpart(j + 1)
                    epilogueA(j)
                    if j + 1 < NTILE:
                        stage1a()
                        stage0(j + 1)
                    epilogueB(j)
                T.barrier()

        if run("p4"):
            with contextlib.ExitStack() as ph:
                SC = 1.0 / float(np.sqrt(96.0))
                wuq = sb(ph, "wuq", [128, 3, 8, 96], BF16); wuqp = sb(ph, "wuqp", [128, 3, 8, 96], BF16)
                wkn = sb(ph, "wkn", [128, 2, 512], BF16); wv = sb(ph, "wv", [128, 2, 512], BF16); Bw4 = Buf("w4")
                T.dma("pool", "w0", [(wuq[:, c, :, :], w_uq_d[l, c * 128:(c + 1) * 128]) for c in range(3)]
                      + [(wuqp[:, c, :, :], w_uqp_d[l, c * 128:(c + 1) * 128]) for c in range(3)]
                      + [(wkn[:, c, :], w_kn_d[l, c * 128:(c + 1) * 128, :]) for c in range(2)]
                      + [(wv[:, c, :], w_v_d[l, c * 128:(c + 1) * 128, :]) for c in range(2)], writes=[Bw4])
                kT = sb(ph, "kT", [128, 8, S], BF16); BkT = Buf("kT")
                Va = sb(ph, "Va", [128, S // 128, 8, 128], BF16); BVa = Buf("Va")
                for a_ in range(0, S // 128, 8):
                    T.op("dve", lambda e: e.memset(Va[:, a_:a_ + 8, :, 64:128], 1.0), writes=[BVa], signal=(a_ + 8 >= S // 128))
                cq = [sb(ph, f"cq{i}", [128, 3, TT], BF16) for i in range(2)]; Bcq = [Buf("cq0"), Buf("cq1")]
                ckv = [sb(ph, f"ckv{i}", [128, 2, TT], BF16) for i in range(2)]; Bckv = [Buf("ckv0"), Buf("ckv1")]
                kpe = [sb(ph, f"kpe{i}", [128, 2, TT], BF16) for i in range(2)]; Bkpe = [Buf("kpe0"), Buf("kpe1")]
                rcs = [sb(ph, f"rcs{i}", [128, 2, TT]) for i in range(2)]; Brcs = [Buf("rcs0"), Buf("rcs1")]
                sq = sb(ph, "p4sq", [128, 3, TT], BF16); Bsq = Buf("sq")
                rstd = sb(ph, "p4rstd", [128, TT]); Brs = Buf("rstd")
                cqn = sb(ph, "cqn", [128, 3, TT], BF16); Bcqn = Buf("cqn")
                ckvn = sb(ph, "ckvn", [128, 2, TT], BF16); Bckvn = Buf("ckvn")
                t1 = sb(ph, "p4t1", [128, TT]); t2 = sb(ph, "p4t2", [128, TT]); Bt1 = Buf("t1"); Bt2 = Buf("t2")
                qT = sb(ph, "qT", [128, 8, TT], BF16); BqT = [Buf(f"qT{h}") for h in range(8)]
                NPT = 8
                Pt = [sb(ph, f"Pt{i}", [128, TT], BF16) for i in range(NPT)]; BPt = [Buf(f"Pt{i}") for i in range(NPT)]
                Rr = t2; BRr = Bt2
                rb = sb(ph, "rb", [64, TT]); Brb = Buf("rb")
                yo = [sb(ph, "yo", [64, 8, TT], BF16)] * 2; Byo = [Buf("yo")] * 2
                R = slice(64, 96)
                cqrows = projT[CH_CQ * 128:(CH_CQ + 3) * 128, :].rearrange("(c p) t -> p c t", p=128)
                ckvrows = projT[CH_CKV * 128:(CH_CKV + 2) * 128, :].rearrange("(c p) t -> p c t", p=128)

                def load4(j):
                    b = j % 2; cs = slice(j * TT, (j + 1) * TT)
                    T.dma("sp", f"p4a{b}", [(cq[b][:], cqrows[:, :, cs])], writes=[Bcq[b]])
                    T.dma("sp", f"p4b{b}", [(ckv[b][:], ckvrows[:, :, cs])], writes=[Bckv[b]])
                    T.dma("sp", f"p4c{b}", [(kpe[b][R, 0, :], projT[CH_KPE * 128 + 64:CH_KPE * 128 + 96, cs]),
                                            (kpe[b][R, 1, :], projT[CH_KPP * 128 + 64:CH_KPP * 128 + 96, cs])], writes=[Bkpe[b]])
                    T.dma("sp", f"p4d{b}", [(rcs[b][R, 0, :], ropeC[:, cs]), (rcs[b][R, 1, :], ropeS[:, cs])], writes=[Brcs[b]])

                def rms(src, Bsrc, nchk, dst, Bdst, wname):
                    T.op("act", lambda e: e.activation(out=sq[:, 0:nchk, :], in_=src[:], func=AF.Square), reads=[Bsrc], writes=[Bsq])
                    for c in range(nchk):
                        T.op("pe", lambda e: e.matmul(PS[0][:], ones_bf[:], sq[:, c, :], start=(c == 0), stop=(c == nchk - 1)),
                             reads=[Bsq, Bconst], writes=[PB[0]], signal=(c == nchk - 1))
                    T.op("act", lambda e: e.activation(out=rstd[:], in_=PS[0][:], func=AF.Ln, bias=epsb[:, 0:1], scale=1.0 / (128 * nchk)),
                         reads=[PB[0], Bconst], writes=[Brs])
                    T.op("act", lambda e: e.activation(out=rstd[:], in_=rstd[:], func=AF.Exp, scale=-0.5), reads=[Brs], writes=[Brs])
                    for c in range(nchk):
                        T.op("dve", lambda e: e.scalar_tensor_tensor(out=dst[:, c, :], in0=src[:, c, :], scalar=vc(wname, c), in1=rstd[:],
                                                                     op0=ALU.mult, op1=ALU.mult), reads=[Bsrc, Brs, Bvecs], writes=[Bdst], signal=(c == nchk - 1))
                load4(0)
                pk = 0; rot = 0
                for j in range(NTILE):
                    b = j % 2; jj = j % (S // TT); cl = slice(jj * TT, (jj + 1) * TT)
                    if j + 1 < NTILE:
                        load4(j + 1)
                    rms(cq[b], Bcq[b], 3, cqn, Bcqn, f"q_norm{l}")
                    rms(ckv[b], Bckv[b], 2, ckvn, Bckvn, f"kv_norm{l}")
                    T.op("dve", lambda e: e.tensor_tensor(out=t1[R, :], in0=kpe[b][R, 0, :], in1=rcs[b][R, 0, :], op=ALU.mult), reads=[Bkpe[b], Brcs[b]], writes=[Bt1])
                    T.op("dve", lambda e: e.tensor_tensor(out=t2[R, :], in0=kpe[b][R, 1, :], in1=rcs[b][R, 1, :], op=ALU.mult), reads=[Bkpe[b], Brcs[b]], writes=[Bt2])
                    T.op("dve", lambda e: e.tensor_tensor(out=t1[R, :], in0=t1[R, :], in1=t2[R, :], op=ALU.add), reads=[Bt1, Bt2], writes=[Bt1])
                    T.op("act", lambda e: e.activation(out=kT[R, :, cl], in_=t1[R, :].unsqueeze(1).broadcast_to([32, 8, TT]), func=AF.Copy), reads=[Bt1], writes=[BkT])
                    for h in range(8):
                        pq = 1 + rot % 4; rot += 1
                        for c in range(2):
                            T.op("pe", lambda e: e.matmul(PS[pq][0:64, :], wkn[:, c, h * 64:(h + 1) * 64], ckvn[:, c, :], start=(c == 0), stop=(c == 1)),
                                 reads=[Bw4, Bckvn], writes=[PB[pq]], signal=(c == 1))
                        T.op("act", lambda e: e.activation(out=kT[0:64, h, cl], in_=PS[pq][0:64, :], func=AF.Copy), reads=[PB[pq]], writes=[BkT])
                    for blk in range(4):
                        pq = 1 + rot % 4; rot += 1
                        for c in range(2):
                            T.op("pe", lambda e: e.matmul(PS[pq][:], ckvn[:, c, blk * 128:(blk + 1) * 128], wv[:, c, :], start=(c == 0), stop=(c == 1)),
                                 reads=[Bw4, Bckvn], writes=[PB[pq]], signal=(c == 1))
                        en = "act"
                        if en == "act":
                            T.op("act", lambda e: e.activation(out=Va[:, jj * 4 + blk, :, 0:64], in_=PS[pq][:].rearrange("p (h d) -> p h d", h=8), func=AF.Copy),
                                 reads=[PB[pq]], writes=[BVa])
                        else:
                            T.op("dve", lambda e: e.tensor_copy(out=Va[:, jj * 4 + blk, :, 0:64], in_=PS[pq][:].rearrange("p (h d) -> p h d", h=8)),
                                 reads=[PB[pq]], writes=[BVa])
                    for h in range(8):
                        pq = 1 + rot % 4; rot += 1
                        pq2 = 1 + rot % 4; rot += 1
                        for c in range(3):
                            T.op("pe", lambda e: e.matmul(PS[pq][0:96, :], wuq[:, c, h, :], cqn[:, c, :], start=(c == 0), stop=(c == 2)),
                                 reads=[Bw4, Bcqn], writes=[PB[pq]], signal=(c == 2))
                        for c in range(3):
                            T.op("pe", lambda e: e.matmul(PS[pq2][0:96, :], wuqp[:, c, h, :], cqn[:, c, :], start=(c == 0), stop=(c == 2)),
                                 reads=[Bw4, Bcqn], writes=[PB[pq2]], signal=(c == 2))
                        T.op("act", lambda e: e.activation(out=qT[0:64, h, :], in_=PS[pq][0:64, :], func=AF.Copy), reads=[PB[pq]], writes=[BqT[h]])
                        T.op("dve", lambda e: e.tensor_tensor(out=t1[R, :], in0=PS[pq][R, :], in1=rcs[b][R, 0, :], op=ALU.mult), reads=[PB[pq], Brcs[b]], writes=[Bt1])
                        T.op("dve", lambda e: e.tensor_tensor(out=t2[R, :], in0=PS[pq2][R, :], in1=rcs[b][R, 1, :], op=ALU.mult), reads=[PB[pq2], Brcs[b]], writes=[Bt2])
                        T.op("dve", lambda e: e.tensor_tensor(out=qT[R, h, :], in0=t1[R, :], in1=t2[R, :], op=ALU.add), reads=[Bt1, Bt2], writes=[BqT[h]])
                    nkb = 4 * (jj + 1)
                    steps = [(h, kb) for h in range(8) for kb in range(nkb)]
                    LA = 6
                    info = {}
                    deferred = []

                    def emit_score(i):
                        nonlocal rot, pk
                        h, kb = steps[i]
                        off = 0 if kb < 4 * jj else (kb - 4 * jj) * 128
                        pq = 1 + rot % 4; rot += 1
                        pp = pk % NPT; pk += 1
                        info[i] = (off, pp)
                        T.op("pe", lambda e: e.matmul(PS[pq][:, off:TT], kT[0:96, h, kb * 128:(kb + 1) * 128], qT[0:96, h, off:TT], start=True, stop=True),
                             reads=[BkT, BqT[h]], writes=[PB[pq]])
                        T.op("act", lambda e: e.activation(out=Pt[pp][:, off:TT], in_=PS[pq][:, off:TT], func=AF.Exp, scale=SC), reads=[PB[pq]], writes=[BPt[pp]])
                        if kb >= 4 * jj:
                            T.op("pool", lambda e: e.tensor_tensor(out=Pt[pp][:, off:off + 128], in0=Pt[pp][:, off:off + 128], in1=triu_bf[:], op=ALU.mult),
                                 reads=[BPt[pp], Bconst], writes=[BPt[pp]])

                    def fin2(h):
                        po = 5 + h % 3
                        T.op("pe", lambda e: e.matmul(PS[0][0:64, :], ident_f[64:128, 64:128], Rr[64:128, :], start=True, stop=True),
                             reads=[BRr, Bconst], writes=[PB[0]])
                        T.op("dve", lambda e: e.tensor_copy(out=rb[:], in_=PS[0][0:64, :]), reads=[PB[0]], writes=[Brb])
                        T.op("dve", lambda e: e.tensor_tensor(out=yo[b][:, h, :], in0=PS[po][0:64, :], in1=rb[:], op=ALU.mult), reads=[PB[po], Brb], writes=[Byo[b]])

                    def emit_pv(i):
                        h, kb = steps[i]; off, pp = info.pop(i); po = 5 + h % 3
                        T.op("pe", lambda e: e.matmul(PS[po][:, off:TT], Va[:, kb, h, :], Pt[pp][:, off:TT], start=(kb == 0), stop=(kb == nkb - 1)),
                             reads=[BVa, BPt[pp]], writes=[PB[po]], signal=(kb == nkb - 1))
                        if kb == nkb - 1:
                            T.op("dve", lambda e: e.reciprocal(out=Rr[64:128, :], in_=PS[po][64:128, :]), reads=[PB[po]], writes=[BRr])
                            deferred.append((i + min(9, nkb - 1), h))
                    for i in range(len(steps) + LA + 12):
                        if i < len(steps):
                            emit_score(i)
                        if 0 <= i - LA < len(steps):
                            emit_pv(i - LA)
                        while deferred and deferred[0][0] <= i - LA:
                            fin2(deferred.pop(0)[1])
                    assert not deferred
                    T.dma("sp", f"p4s{b}", [(ymixT[1536:2048, j * TT:(j + 1) * TT].rearrange("(h p) t -> p h t", p=64), yo[b][:])], reads=[Byo[b]])
                T.barrier()

        if run("p5"):
            with contextlib.ExitStack() as ph:
                wo = sb(ph, "wo", [128, 16, 1024], BF16); Bwo = Buf("wo")
                T.dma("pool", "w0", [(wo[:, kc, :], w_out_d[l, kc * 128:(kc + 1) * 128, :]) for kc in range(16)], writes=[Bwo])
                ym = [sb(ph, f"p5y{i}", [128, 16, TT], BF16) for i in range(2)]; Bym = [Buf("ym0"), Buf("ym1")]
                xt = [sb(ph, f"p5x{i}", [128, 8, TT]) for i in range(2)]; Bxt = [Buf("xt0"), Buf("xt1")]
                x1 = [sb(ph, f"p5o{i}", [128, 8, TT]) for i in range(2)]; Bx1 = [Buf("x10"), Buf("x11")]
                sq = sb(ph, "p5sq", [128, 8, TT], BF16); Bsq = Buf("sq")
                rstd = sb(ph, "p5rstd", [128, TT]); Brs = Buf("rstd")
                h2 = [sb(ph, f"p5h{i}", [128, 8, TT], BF16) for i in range(2)]; Bh2 = [Buf("h20"), Buf("h21")]

                def load5(j):
                    b = j % 2; cs = slice(j * TT, (j + 1) * TT)
                    T.dma("sp", f"p5y{b}", [(ym[b][:], ymixT[:, cs].rearrange("(c p) t -> p c t", p=128))], writes=[Bym[b]])
                    T.dma("sp", f"p5x{b}", [(xt[b][:], xsrc[:, cs].rearrange("(c p) t -> p c t", p=128))], writes=[Bxt[b]])
                load5(0)
                for j in range(NTILE):
                    b = j % 2; cs = slice(j * TT, (j + 1) * TT)
                    if j + 1 < NTILE:
                        load5(j + 1)
                    for oc in range(8):
                        pq = 1 + oc % 6
                        for kc in range(16):
                            T.op("pe", lambda e: e.matmul(PS[pq][:], wo[:, kc, oc * 128:(oc + 1) * 128], ym[b][:, kc, :], start=(kc == 0), stop=(kc == 15)),
                                 reads=[Bwo, Bym[b]], writes=[PB[pq]], signal=(kc == 15))
                        T.op("dve", lambda e: e.tensor_tensor(out=x1[b][:, oc, :], in0=PS[pq][:], in1=xt[b][:, oc, :], op=ALU.add),
                             reads=[PB[pq], Bxt[b]], writes=[Bx1[b]])
                    T.dma("sp", f"p5s{b}", [(xres[:, cs].rearrange("(c p) t -> p c t", p=128), x1[b][:])], reads=[Bx1[b]])
                    T.op("act", lambda e: e.activation(out=sq[:], in_=x1[b][:], func=AF.Square), reads=[Bx1[b]], writes=[Bsq])
                    for c in range(8):
                        T.op("pe", lambda e: e.matmul(PS[0][:], ones_bf[:], sq[:, c, :], start=(c == 0), stop=(c == 7)),
                             reads=[Bsq, Bconst], writes=[PB[0]], signal=(c == 7))
                    T.op("act", lambda e: e.activation(out=rstd[:], in_=PS[0][:], func=AF.Ln, bias=epsb[:, 0:1], scale=1.0 / D), reads=[PB[0], Bconst], writes=[Brs])
                    T.op("act", lambda e: e.activation(out=rstd[:], in_=rstd[:], func=AF.Exp, scale=-0.5), reads=[Brs], writes=[Brs])
                    for c in range(8):
                        T.op("dve", lambda e: e.scalar_tensor_tensor(out=h2[b][:, c, :], in0=x1[b][:, c, :], scalar=vc(f"ffn_norm{l}", c), in1=rstd[:],
                                                                     op0=ALU.mult, op1=ALU.mult), reads=[Bx1[b], Brs, Bvecs], writes=[Bh2[b]])
                    T.dma("sp", f"p5t{b}", [(h2T[:, cs].rearrange("(c p) t -> p c t", p=128), h2[b][:])], reads=[Bh2[b]])
                T.barrier()

        for hf in range(2):
            if not run("p6"):
                continue
            with contextlib.ExitStack() as ph:
                NH = 11; W = NH * 128
                wu = sb(ph, "wu", [128, 8, 2 * W], BF16); wd = sb(ph, "wd", [128, NH, 1024], BF16); Bwu = Buf("wu"); Bwd = Buf("wd")
                T.dma("pool", "w0", [(wu[:, kc, 0:W], w_up_d[l, kc * 128:(kc + 1) * 128, hf * W:(hf + 1) * W]) for kc in range(8)]
                      + [(wu[:, kc, W:2 * W], w_up_d[l, kc * 128:(kc + 1) * 128, DFF + hf * W:DFF + (hf + 1) * W]) for kc in range(8)], writes=[Bwu])
                T.dma("pool", "w1", [(wd[:, i, :], w_dn_d[l, (hf * NH + i) * 128:(hf * NH + i + 1) * 128, :]) for i in range(NH)], writes=[Bwd])
                hb = [sb(ph, f"p6h{i}", [128, 8, TT], BF16) for i in range(2)]; Bhb = [Buf("hb0"), Buf("hb1")]
                xin = [sb(ph, f"p6x{i}", [128, 8, TT]) for i in range(2)]; Bxin = [Buf("xin0"), Buf("xin1")]
                xo = [sb(ph, f"p6o{i}", [128, 8, TT]) for i in range(2)]; Bxo = [Buf("xo0"), Buf("xo1")]
                ag = [sb(ph, f"p6ag{i}", [128, TT]) for i in range(2)]; Bag = [Buf("ag0"), Buf("ag1")]
                av = [sb(ph, f"p6av{i}", [128, TT]) for i in range(2)]; Bav = [Buf("av0"), Buf("av1")]
                gg = sb(ph, "p6g", [128, NH, TT], BF16); Bgg = [Buf(f"g{i}") for i in range(NH)]
                final = (l == nlayers - 1 and hf == 1)
                if final:
                    sq = sb(ph, "p6sq", [128, 8, TT], BF16); Bsq = Buf("sq")
                    rstd = sb(ph, "p6rstd", [128, TT]); Brs = Buf("rstd")
                tiles = []
                for s_ in range(NSEQ):
                    t0 = 0
                    while t0 < S:
                        ln = min(TT - 2, S - t0); tiles.append((s_, t0, ln)); t0 += ln
                dst = out_d if final else xres

                def load6(i):
                    s_, t0, ln = tiles[i]; b = i % 2; g0 = s_ * S + t0
                    if t0 == 0:
                        T.op("pool", lambda e: e.memset(hb[b][:, :, 0:2], 0.0), writes=[Bhb[b]])
                        T.dma("sp", f"p6h{b}", [(hb[b][:, :, 2:2 + ln], h2T[:, g0:g0 + ln].rearrange("(c p) t -> p c t", p=128))], writes=[Bhb[b]])
                    else:
                        T.dma("sp", f"p6h{b}", [(hb[b][:, :, 0:2 + ln], h2T[:, g0 - 2:g0 + ln].rearrange("(c p) t -> p c t", p=128))], writes=[Bhb[b]])
                    T.dma("sp", f"p6x{b}", [(xin[b][:, :, 0:ln], xres[:, g0:g0 + ln].rearrange("(c p) t -> p c t", p=128))], writes=[Bxin[b]])
                load6(0)
                rot = 0
                for i, (s_, t0, ln) in enumerate(tiles):
                    b = i % 2; N = ln + 2; g0 = s_ * S + t0
                    if i + 1 < len(tiles):
                        load6(i + 1)
                    for ci in range(NH):
                        pg = 1 + rot % 6; rot += 1
                        pv = 1 + rot % 6; rot += 1
                        for (pq, co) in ((pg, ci * 128), (pv, W + ci * 128)):
                            for kc in range(8):
                                T.op("pe", lambda e: e.matmul(PS[pq][:, 0:N], wu[:, kc, co:co + 128], hb[b][:, kc, 0:N], start=(kc == 0), stop=(kc == 7)),
                                     reads=[Bwu, Bhb[b]], writes=[PB[pq]], signal=(kc == 7))
                        cg = hf * NH + ci; cv = 22 + hf * NH + ci
                        for (pq, cidx, a_, Ba) in ((pg, cg, ag[ci % 2], Bag[ci % 2]), (pv, cv, av[ci % 2], Bav[ci % 2])):
                            T.op("act", lambda e: e.activation(out=a_[:, 0:ln], in_=PS[pq][:, 2:N], func=AF.Identity, bias=vc(f"ffn_cb{l}", cidx),
                                                               scale=vc(f"ffn_cw{l}", 2 * 44 + cidx)), reads=[PB[pq], Bvecs], writes=[Ba])
                            T.op("dve", lambda e: e.scalar_tensor_tensor(out=a_[:, 0:ln], in0=PS[pq][:, 1:N - 1], scalar=vc(f"ffn_cw{l}", 1 * 44 + cidx), in1=a_[:, 0:ln],
                                                                         op0=ALU.mult, op1=ALU.add), reads=[PB[pq], Bvecs, Ba], writes=[Ba])
                            T.op("dve", lambda e: e.scalar_tensor_tensor(out=a_[:, 0:ln], in0=PS[pq][:, 0:N - 2], scalar=vc(f"ffn_cw{l}", 0 * 44 + cidx), in1=a_[:, 0:ln],
                                                                         op0=ALU.mult, op1=ALU.add), reads=[PB[pq], Bvecs, Ba], writes=[Ba])
                        T.op("act", lambda e: e.activation(out=ag[ci % 2][:, 0:ln], in_=ag[ci % 2][:, 0:ln], func=AF.Silu), reads=[Bag[ci % 2]], writes=[Bag[ci % 2]])
                        T.op("pool", lambda e: e.tensor_tensor(out=gg[:, ci, 0:ln], in0=ag[ci % 2][:, 0:ln], in1=av[ci % 2][:, 0:ln], op=ALU.mult),
                             reads=[Bag[ci % 2], Bav[ci % 2]], writes=[Bgg[ci]])
                    for oc in range(8):
                        pq = 7 if oc % 2 == 0 else 0
                        for ci in range(NH):
                            T.op("pe", lambda e: e.matmul(PS[pq][:, 0:ln], wd[:, ci, oc * 128:(oc + 1) * 128], gg[:, ci, 0:ln], start=(ci == 0), stop=(ci == NH - 1)),
                                 reads=[Bwd, Bgg[ci]], writes=[PB[pq]], signal=(ci == NH - 1))
                        T.op("dve", lambda e: e.tensor_tensor(out=xo[b][:, oc, 0:ln], in0=PS[pq][:, 0:ln], in1=xin[b][:, oc, 0:ln], op=ALU.add),
                             reads=[PB[pq], Bxin[b]], writes=[Bxo[b]])
                    if final:
                        T.op("act", lambda e: e.activation(out=sq[:, :, 0:ln], in_=xo[b][:, :, 0:ln], func=AF.Square), reads=[Bxo[b]], writes=[Bsq])
                        for c in range(8):
                            T.op("pe", lambda e: e.matmul(PS[1][:, 0:ln], ones_bf[:], sq[:, c, 0:ln], start=(c == 0), stop=(c == 7)),
                                 reads=[Bsq, Bconst], writes=[PB[1]], signal=(c == 7))
                        T.op("act", lambda e: e.activation(out=rstd[:, 0:ln], in_=PS[1][:, 0:ln], func=AF.Ln, bias=epsb[:, 0:1], scale=1.0 / D), reads=[PB[1], Bconst], writes=[Brs])
                        T.op("act", lambda e: e.activation(out=rstd[:, 0:ln], in_=rstd[:, 0:ln], func=AF.Exp, scale=-0.5), reads=[Brs], writes=[Brs])
                        for c in range(8):
                            T.op("dve", lambda e: e.scalar_tensor_tensor(out=xo[b][:, c, 0:ln], in0=xo[b][:, c, 0:ln], scalar=vc("final_norm", c), in1=rstd[:, 0:ln],
                                                                         op0=ALU.mult, op1=ALU.mult), reads=[Brs, Bvecs, Bxo[b]], writes=[Bxo[b]])
                    T.dma("sp", f"p6s{b}", [(dst[:, g0:g0 + ln].rearrange("(c p) t -> p c t", p=128), xo[b][:, :, 0:ln])], reads=[Bxo[b]])
                T.barrier()

    T.barrier()
    return nc, es


def _prep_inputs(inp):
    x = np.asarray(inp["x"], np.float32)
    posi = np.asarray(inp["positions"], np.int32)
    shared = {
        "vecs": _build_vecs(inp),
        "w_in": np.stack([_layout_w_in(inp["w_in"][l]) for l in range(DEPTH)]),
        "pool_w": np.ascontiguousarray(np.asarray(inp["pool_w"], np.float32)),
    }
    uq = np.asarray(inp["mla_w_uq"], np.float32).reshape(DEPTH, 384, 8, 96)
    uqp = uq.copy()
    uqp[..., 64:80] = uq[..., 80:96]; uqp[..., 80:96] = uq[..., 64:80]
    ukv = np.asarray(inp["mla_w_ukv"], np.float32).reshape(DEPTH, 256, 8, 128)
    shared["w_uq"] = np.ascontiguousarray(uq); shared["w_uqp"] = np.ascontiguousarray(uqp)
    shared["w_kn"] = np.ascontiguousarray(ukv[..., :64].reshape(DEPTH, 256, 512))
    shared["w_v"] = np.ascontiguousarray(ukv[..., 64:].reshape(DEPTH, 256, 512))
    shared["w_out"] = np.ascontiguousarray(np.asarray(inp["w_out"], np.float32))
    shared["w_up"] = np.ascontiguousarray(np.asarray(inp["ffn_w_up"], np.float32))
    shared["w_dn"] = np.ascontiguousarray(np.asarray(inp["ffn_w_down"], np.float32))
    in_maps = []
    for c in range(NCORE):
        m = dict(shared)
        m["xT"] = np.ascontiguousarray(np.concatenate([x[2 * c].T, x[2 * c + 1].T], axis=1))
        m["pos"] = np.ascontiguousarray(posi[2 * c:2 * c + 2].reshape(1, NT))
        in_maps.append(m)
    return in_maps


def kernel(**inp):
    in_maps = _prep_inputs(inp)
    nc, es = build_program()
    res = run_bass_kernel_spmd(nc, in_maps, core_ids=list(range(NCORE)))
    out = np.empty((2 * NCORE, S, D), np.float32)
    for c in range(NCORE):
        o = np.asarray(res.results[c]["out"])
        out[2 * c] = o[:, :S].T
        out[2 * c + 1] = o[:, S:].T
    return out
```
